# Optimizing a Trainium2 kernel written in Bass

```python
import math
import jax, jax.numpy as jnp
from jax import lax
import numpy as np

D_MODEL = 1024
BATCH = 1
SEQ = 16384
DEPTH = 1
DEC_BATCH = 128
DEC_SEQ = 1
PAST_LEN = 8192
PAGE_SIZE = 128

HEAD_DIM = 128
HEADS_PER_GROUP = 4
GROUPS = ((128, 1), (512, 4), (2048, 16))
N_GROUPS = len(GROUPS)
N_HEADS = N_GROUPS * HEADS_PER_GROUP
ATTN_QKV = N_HEADS * HEAD_DIM
ATTN_OUT = HEADS_PER_GROUP * HEAD_DIM
CONV_CH = D_MODEL // 2
CONV_WIDTH = 31
ALPHA = (2.0 * DEPTH) ** 0.25
BETA = (8.0 * DEPTH) ** -0.25
LN_EPS = 1e-5
NEG = -1e30
IN_SIZES = (ATTN_QKV, ATTN_QKV, ATTN_QKV, ATTN_OUT, 2 * CONV_CH, CONV_CH, D_MODEL, D_MODEL)
IN_COLS = sum(IN_SIZES)
SPLIT_POINTS = tuple(int(s) for s in np.cumsum(IN_SIZES)[:-1])

kernel_name = "hybrid_dilated_swa_conformer_conv_decode_step"


def _alibi_slopes():
    h = jnp.arange(1, N_HEADS + 1, dtype=jnp.float32)
    return (2.0 ** (-8.0 * h / N_HEADS)).reshape(N_GROUPS, HEADS_PER_GROUP)


def _layer_norm(x, g, b):
    xf = x.astype(jnp.float32)
    mu = jnp.mean(xf, axis=-1, keepdims=True)
    var = jnp.mean(jnp.square(xf - mu), axis=-1, keepdims=True)
    return ((xf - mu) * lax.rsqrt(var + LN_EPS) * g.astype(jnp.float32) + b.astype(jnp.float32)).astype(x.dtype)


def _branches_in(x, c, w_c, b_c, w_in, b_in):
    mod = c @ w_c + b_c
    shift, scale, gate = jnp.split(mod, 3, axis=-1)
    h = x * (1.0 + scale[:, None]) + shift[:, None]
    z = h @ w_in + b_in
    return gate, jnp.split(z, SPLIT_POINTS, axis=-1)


def _heads(a):
    return a.reshape(a.shape[0], a.shape[1], N_GROUPS, HEADS_PER_GROUP, HEAD_DIM)


def _dilated_window_prompt(q, k, v, slopes, window, dilation):
    B, L, H, Dh = q.shape
    n_keys = window // dilation
    blk = n_keys
    span = dilation * blk
    Lp = -(-L // span) * span
    nb = Lp // span
    ls = Lp // dilation

    def split(a):
        a = jnp.pad(a, ((0, 0), (0, Lp - L), (0, 0), (0, 0)))
        a = a.reshape(B, ls, dilation, H, Dh).transpose(0, 2, 1, 3, 4)
        return a.reshape(B, dilation, nb, blk, H, Dh)

    def with_prev(a):
        prev = jnp.pad(a, ((0, 0), (0, 0), (1, 0), (0, 0), (0, 0), (0, 0)))[:, :, :-1]
        return jnp.concatenate([prev, a], axis=3)

    qb = split(q)
    kk = with_prev(split(k))
    vv = with_prev(split(v))
    s = jnp.einsum('bdnqhe,bdnkhe->bdnhqk', qb, kk).astype(jnp.float32)
    qi = jnp.arange(blk)
    ki = jnp.arange(2 * blk) - blk
    dist = qi[:, None] - ki[None, :]
    in_range = (dist >= 0) & (dist <= n_keys)
    after_start = (jnp.arange(nb)[:, None, None] * blk + ki[None, None, :]) >= 0
    mask = in_range[None] & after_start
    bias = -(slopes[:, None, None] * (dilation * dist).astype(jnp.float32)[None])
    s = jnp.where(mask[None, None, :, None], s + bias, NEG)
    lse = jax.nn.logsumexp(s, axis=-1)
    p = jnp.exp(s - lse[..., None])
    o = jnp.einsum('bdnhqk,bdnkhe->bdnqhe', p, vv.astype(jnp.float32))
    o = o.reshape(B, dilation, ls, H, Dh).transpose(0, 2, 1, 3, 4).reshape(B, Lp, H, Dh)[:, :L]
    lse = lse.transpose(0, 1, 2, 4, 3).reshape(B, dilation, ls, H).transpose(0, 2, 1, 3).reshape(B, Lp, H)[:, :L]
    return o, lse


def _dilated_window_sample(q, k_new, v_new, kv_cache, slopes, window, dilation):
    S = q.shape[1]
    Wb = kv_cache.shape[1]
    n_keys = window // dilation
    k_all = jnp.concatenate([kv_cache[:, :, 0], k_new.astype(kv_cache.dtype)], axis=1)
    v_all = jnp.concatenate([kv_cache[:, :, 1], v_new.astype(kv_cache.dtype)], axis=1)
    steps = jnp.arange(n_keys + 1)
    idx = Wb + jnp.arange(S)[:, None] - dilation * steps[None, :]
    valid = idx >= 0
    idx_c = jnp.maximum(idx, 0)
    kg = k_all[:, idx_c]
    vg = v_all[:, idx_c]
    s = jnp.einsum('bshe,bskhe->bhsk', q, kg.astype(q.dtype)).astype(jnp.float32)
    bias = -(slopes[:, None, None] * (dilation * steps).astype(jnp.float32)[None, None, :])
    s = jnp.where(valid[None, None], s + bias, NEG)
    lse = jax.nn.logsumexp(s, axis=-1)
    p = jnp.exp(s - lse[..., None])
    o = jnp.einsum('bhsk,bskhe->bshe', p, vg.astype(jnp.float32))
    return o, lse.transpose(0, 2, 1)


def _combine_groups(outs, lses):
    w = jax.nn.softmax(jnp.stack(lses, axis=0), axis=0)
    return jnp.sum(w[..., None] * jnp.stack(outs, axis=0), axis=0)


def _glu(u2):
    a, g = jnp.split(u2, 2, axis=-1)
    return a * jax.nn.sigmoid(g)


def _conv_tail(u_ext, conv_w, conv_b, cn_g, cn_b):
    y = lax.conv_general_dilated(u_ext, conv_w[:, None, :].astype(u_ext.dtype), window_strides=(1,),
                                 padding='VALID', dimension_numbers=('NWC', 'WIO', 'NWC'),
                                 feature_group_count=CONV_CH)
    y = y + conv_b
    return jax.nn.silu(_layer_norm(y, cn_g, cn_b))


def _branches_out(x, gate, o_attn, ga, conv_out, gb, ma, mb, w_pa, w_pb, w_o, ln_g, ln_b):
    B, T = x.shape[0], x.shape[1]
    a = (o_attn.reshape(B, T, ATTN_OUT).astype(x.dtype) * jax.nn.silu(ga)) @ w_pa
    b = (conv_out * jax.nn.silu(gb)) @ w_pb
    y = (jax.nn.sigmoid(ma) * a + jax.nn.sigmoid(mb) * b) @ w_o
    return _layer_norm(ALPHA * x + gate[:, None] * y, ln_g, ln_b)


def setup_inputs(seed: int = 0) -> dict:
    key = jax.random.key(seed)
    ks = jax.random.split(key, 24)
    f32 = jnp.float32

    def nrm(k, shape, s):
        return jax.random.normal(k, shape, f32) * s

    w_bufs = [min(w, PAST_LEN) for w, _ in GROUPS]
    w_in = nrm(ks[8], (D_MODEL, IN_COLS), D_MODEL ** -0.5)
    w_in = w_in.at[:, 2 * ATTN_QKV:3 * ATTN_QKV].multiply(BETA)
    return {
        "x_prompt": nrm(ks[0], (BATCH, SEQ, D_MODEL), 1.0),
        "x_sample": nrm(ks[1], (DEC_BATCH, DEC_SEQ, D_MODEL), 1.0),
        "c_prompt": nrm(ks[2], (BATCH, D_MODEL), 1.0),
        "c_sample": nrm(ks[3], (DEC_BATCH, D_MODEL), 1.0),
        "cache_kv_w128": nrm(ks[4], (DEC_BATCH, w_bufs[0], 2, HEADS_PER_GROUP, HEAD_DIM), 1.0),
        "cache_kv_w512": nrm(ks[5], (DEC_BATCH, w_bufs[1], 2, HEADS_PER_GROUP, HEAD_DIM), 1.0),
        "cache_kv_w2048": nrm(ks[6], (DEC_BATCH, w_bufs[2], 2, HEADS_PER_GROUP, HEAD_DIM), 1.0),
        "state_conv": nrm(ks[7], (DEC_BATCH, CONV_WIDTH - 1, CONV_CH), 0.5),
        "w_c": nrm(ks[9], (D_MODEL, 3 * D_MODEL), 0.5 * D_MODEL ** -0.5),
        "b_c": nrm(ks[10], (3 * D_MODEL,), 0.02),
        "w_in": w_in,
        "b_in": nrm(ks[11], (IN_COLS,), 0.02),
        "conv_w": nrm(ks[12], (CONV_WIDTH, CONV_CH), CONV_WIDTH ** -0.5),
        "conv_b": nrm(ks[13], (CONV_CH,), 0.02),
        "conv_norm_g": 1.0 + nrm(ks[14], (CONV_CH,), 0.02),
        "conv_norm_b": nrm(ks[15], (CONV_CH,), 0.02),
        "w_pa": nrm(ks[16], (ATTN_OUT, D_MODEL), BETA * ATTN_OUT ** -0.5),
        "w_pb": nrm(ks[17], (CONV_CH, D_MODEL), BETA * CONV_CH ** -0.5),
        "w_o": nrm(ks[18], (D_MODEL, D_MODEL), BETA * D_MODEL ** -0.5),
        "ln_g": 1.0 + nrm(ks[19], (D_MODEL,), 0.02),
        "ln_b": nrm(ks[20], (D_MODEL,), 0.02),
    }


def reference(x_prompt, x_sample, c_prompt, c_sample, cache_kv_w128, cache_kv_w512, cache_kv_w2048,
              state_conv, w_c, b_c, w_in, b_in, conv_w, conv_b, conv_norm_g, conv_norm_b,
              w_pa, w_pb, w_o, ln_g, ln_b):
    slopes = _alibi_slopes()
    caches = (cache_kv_w128, cache_kv_w512, cache_kv_w2048)
    seq = x_prompt.shape[1]

    x_p, x_s = x_prompt, x_sample
    for _layer in range(DEPTH):
        gate_p, (q, k, v, ga, glu, gb, ma, mb) = _branches_in(x_p, c_prompt, w_c, b_c, w_in, b_in)
        q, k, v = _heads(q) * (HEAD_DIM ** -0.5), _heads(k), _heads(v)
        outs, lses, kv_p = [], [], []
        for g, (window, dilation) in enumerate(GROUPS):
            o, l = _dilated_window_prompt(q[:, :, g], k[:, :, g], v[:, :, g], slopes[g], window, dilation)
            outs.append(o)
            lses.append(l)
            keep = min(window, seq)
            kv_p.append(jnp.stack([k[:, seq - keep:, g], v[:, seq - keep:, g]], axis=2))
        o_attn = _combine_groups(outs, lses)
        u = _glu(glu)
        u_ext = jnp.pad(u, ((0, 0), (CONV_WIDTH - 1, 0), (0, 0)))
        conv_out = _conv_tail(u_ext, conv_w, conv_b, conv_norm_g, conv_norm_b)
        conv_p = u[:, seq - (CONV_WIDTH - 1):]
        x_p = _branches_out(x_p, gate_p, o_attn, ga, conv_out, gb, ma, mb, w_pa, w_pb, w_o, ln_g, ln_b)

        gate_s, (q, k, v, ga, glu, gb, ma, mb) = _branches_in(x_s, c_sample, w_c, b_c, w_in, b_in)
        q, k, v = _heads(q) * (HEAD_DIM ** -0.5), _heads(k), _heads(v)
        outs, lses, kv_s = [], [], []
        for g, (window, dilation) in enumerate(GROUPS):
            o, l = _dilated_window_sample(q[:, :, g], k[:, :, g], v[:, :, g], caches[g], slopes[g], window, dilation)
            outs.append(o)
            lses.append(l)
            kv_s.append(jnp.stack([k[:, :, g], v[:, :, g]], axis=2))
        o_attn = _combine_groups(outs, lses)
        u = _glu(glu)
        u_ext = jnp.concatenate([state_conv.astype(u.dtype), u], axis=1)
        conv_out = _conv_tail(u_ext, conv_w, conv_b, conv_norm_g, conv_norm_b)
        conv_s = u_ext[:, u_ext.shape[1] - (CONV_WIDTH - 1):]
        x_s = _branches_out(x_s, gate_s, o_attn, ga, conv_out, gb, ma, mb, w_pa, w_pb, w_o, ln_g, ln_b)

    return (x_p, x_s, kv_p[0], kv_p[1], kv_p[2], conv_p, kv_s[0], kv_s[1], kv_s[2], conv_s)
```

```python
import math
import numpy as np
import concourse.bass as bass
import concourse.mybir as mybir
from concourse.bass_utils import run_bass_kernel_spmd

F32 = mybir.dt.float32
BF16 = mybir.dt.bfloat16
AF = mybir.ActivationFunctionType
ALU = mybir.AluOpType
AX = mybir.AxisListType

ENGS = ['pe', 'act', 'dve', 'pool', 'sp']
ND = 24
NCORES = 8
TOK = 2048
NS = 16
QS = 128 ** -0.5
ALPHA = 2.0 ** 0.25
EPS = 1e-5
NEGB = -30000.0
GROUPS = ((128, 1), (512, 4), (2048, 16))


def sl(start, n, step=1):
    return slice(start, start + step * (n - 1) + 1, step)


class Sched:
    def __init__(self):
        self.ops = []
        self.last_w = {}
        self.readers = {}
        self.eng_last = {e: None for e in ENGS}
        self.pending_dma = []

    def op(self, eng, fn, reads=(), writes=(), dma=False, extra_deps=()):
        idx = len(self.ops)
        deps = set(extra_deps)
        for r in reads:
            if r in self.last_w:
                deps.add(self.last_w[r])
        for r in writes:
            if r in self.last_w:
                deps.add(self.last_w[r])
            for x in self.readers.get(r, ()):
                deps.add(x)
        deps.discard(idx)
        self.ops.append(dict(eng=eng, fn=fn, deps=sorted(deps), dma=dma, sig=False))
        for r in reads:
            self.readers.setdefault(r, []).append(idx)
        for r in writes:
            self.last_w[r] = idx
            self.readers[r] = []
        self.eng_last[eng] = idx
        if dma:
            self.pending_dma.append(idx)
        return idx

    def barrier(self):
        deps = [v for v in self.eng_last.values() if v is not None] + list(self.pending_dma)
        b = self.op('sp', lambda e: e.nop(), extra_deps=deps)
        for e in ENGS:
            if e != 'sp':
                self.op(e, lambda eng: eng.nop(), extra_deps=[b])
        self.pending_dma = []
        self.last_w = {}
        self.readers = {}

    def _needs_wait(self, op, dop):
        if dop['dma']:
            return True
        if dop['eng'] == op['eng']:
            return op['eng'] != 'pe'
        return True

    def emit(self, nc, block, esems, dsems):
        ops = self.ops
        for op in ops:
            for d in op['deps']:
                if self._needs_wait(op, ops[d]):
                    ops[d]['sig'] = True
        cnt = {e: 0 for e in ENGS}
        ndma = 0
        for op in ops:
            if op['dma']:
                op['dk'] = ndma
                ndma += 1
            elif op['sig']:
                cnt[op['eng']] += 1
                op['count'] = cnt[op['eng']]
        per_eng = {e: [] for e in ENGS}
        known = {}
        knownd = {}
        for op in ops:
            E = op['eng']
            waits = []
            if op['dma'] and op['dk'] >= ND:
                s = op['dk'] % ND
                v = 16 * (op['dk'] // ND)
                if knownd.get((E, s), 0) < v:
                    waits.append((dsems[s], v))
                    knownd[(E, s)] = v
            for d in op['deps']:
                dop = ops[d]
                if not self._needs_wait(op, dop):
                    continue
                if dop['dma']:
                    s = dop['dk'] % ND
                    v = 16 * (dop['dk'] // ND + 1)
                    if knownd.get((E, s), 0) < v:
                        waits.append((dsems[s], v))
                        knownd[(E, s)] = v
                else:
                    Fe = dop['eng']
                    v = dop['count']
                    if known.get((E, Fe), 0) < v:
                        known[(E, Fe)] = v
                        waits.append((esems[Fe], v))
            mw = {}
            for s, v in waits:
                k = id(s)
                if k not in mw or mw[k][1] < v:
                    mw[k] = (s, v)
            op['waits'] = list(mw.values())
            per_eng[E].append(op)

        def mk(E):
            def run(engh):
                for op in per_eng[E]:
                    for s, v in op['waits']:
                        engh.wait_ge(s, v)
                    ins = op['fn'](engh)
                    if op['dma']:
                        ins.then_inc(dsems[op['dk'] % ND], 16)
                    elif op['sig']:
                        ins.then_inc(esems[E], 1)
            return run

        block.tensor(mk('pe'))
        block.scalar(mk('act'))
        block.vector(mk('dve'))
        block.gpsimd(mk('pool'))
        block.sync(mk('sp'))


class Alloc:
    def __init__(self, nc):
        self.nc = nc
        self.top = 16512
        self.lim = 16481 + 212863
        self.n = 0

    def __call__(self, name, shape, dt):
        size = 1
        for s in shape[1:]:
            size *= s
        size *= 4 if dt == F32 else 2
        size = (size + 63) // 64 * 64
        off = self.top
        self.top += size
        assert self.top <= self.lim, (name, self.top, self.lim)
        self.n += 1
        return self.nc.alloc_sbuf_tensor_at("%s_%d" % (name, self.n), list(shape), dt, offset=off)


class Ring:
    def __init__(self, name, bufs, base=0, names=None):
        self.name = name
        self.bufs = bufs
        self.i = 0
        self.base = base
        self.names = names

    def next(self):
        k = self.i % len(self.bufs)
        self.i += 1
        if self.names is not None:
            return self.bufs[k], self.names[k]
        return self.bufs[k], "%s%d" % (self.name, k + self.base)


def build_program():
    nc = bass.Bass("TRN2", target_bir_lowering=False)

    def din(name, shape):
        return nc.dram_tensor(name, list(shape), F32, kind="ExternalInput").ap()

    def dout(name, shape):
        return nc.dram_tensor(name, list(shape), F32, kind="ExternalOutput").ap()

    xT_d = din("xT", [128, 8, 4096])
    xtok_d = din("xtok", [TOK, 1024])
    xsT_d = din("xsT", [128, 8, NS])
    xstok_d = din("xstok", [NS, 1024])
    cT_d = din("cT", [128, 8, 17])
    cbc_d = din("cbc", [128, 8, 128])
    wc_d = din("wc", [128, 8, 3072])
    bcT_d = din("bcT", [128, 16])
    bg_d = din("bg", [128, 1024])
    win_d = din("win", [68, 128, 8, 128])
    bfm_d = din("bfm", [128, 68])
    bv_d = din("bvbc", [128, 1536])
    binbc_d = din("binbc", [NS, 8704])
    cwfm_d = din("cwfm", [128, 4, 31])
    cbfm_d = din("cbfm", [128, 4])
    cgfm_d = din("cgfm", [128, 4])
    cnbfm_d = din("cnbfm", [128, 4])
    cwbc_d = din("cwbc", [NS, 31, 512])
    cvec_d = din("cvec", [NS, 3, 512])
    wpa_d = din("wpa", [128, 4, 1024])
    wpb_d = din("wpb", [128, 4, 1024])
    wo_d = din("wo", [128, 8, 1024])
    lng_d = din("lng", [128, 1024])
    lnb_d = din("lnb", [128, 1024])
    tb_d = din("tb", [128, 12, 2, 256])
    tbs_d = din("tbs", [128, 12])
    valid_d = din("valid", [128, 1])
    ident_d = din("ident", [128, 128])
    sel_d = din("sel", [NS, NS, 128])
    ck_d = [din("ck0", [NS, 128, 1024]), din("ck1", [NS, 512, 1024]), din("ck2", [NS, 2048, 1024])]
    sconv_d = din("sconv", [NS, 30, 512])

    y_d = dout("y", [TOK, 1024])
    ys_d = dout("ys", [NS, 1024])
    kTo_d = [dout("kT0", [128, 4, 128]), dout("kT1", [128, 4, 512]), dout("kT2", [128, 4, 2048])]
    vo_d = [dout("v0", [128, 512]), dout("v1", [512, 512]), dout("v2", [2048, 512])]
    uT_d = dout("uT", [128, 4, 30])
    ksn_d = dout("ksn", [NS, 1536])
    vsn_d = dout("vsn", [NS, 1536])
    convs_d = dout("convs", [NS, 30, 512])

    S = Sched()
    A = Alloc(nc)
    PS = [nc.alloc_psum_tensor("psb%d" % i, [128, 512], F32) for i in range(8)]
    PSR = ["ps%d" % i for i in range(8)]

    zs_d = nc.dram_tensor("zs_scr", [NS, 8704], F32).ap()

    def dma(out, in_):
        return lambda e: e.dma_start(out=out, in_=in_)

    def load(eng, out, in_, res):
        S.op(eng, dma(out, in_), writes=[res], dma=True)

    modT = A("modT", [128, 16, 17], F32)
    gate_bc = A("gate_bc", [128, 1024], F32)
    gate_s = A("gate_s", [128, 1024], F32)
    bfm = A("bfm", [128, 68], F32)
    bqs = A("bqs", [128, 12], F32)
    hsT = A("hsT", [128, 8, NS], BF16)
    ones_bf = A("ones_bf", [128, 128], BF16)
    onesm = A("onesm", [128, 128], BF16)
    ident = A("ident", [128, 128], F32)
    ident_bf = A("ident_bf", [128, 128], BF16)
    validt = A("valid", [128, 1], F32)
    cwfm = A("cwfm", [128, 4, 31], F32)
    cbfm = A("cbfm", [128, 4], F32)
    cgfm = A("cgfm", [128, 4], F32)
    cnbfm = A("cnbfm", [128, 4], F32)
    uhalo = A("uhalo", [128, 4, 32], BF16)
    zstr = Ring("zst", [A("zst%d" % i, [NS, 128], F32) for i in range(2)])
    mark_persist = A.top

    def piggy(chunk, w_fn, res_list, ring):
        ps, pres = ring.next()
        S.op('pe', mm8(ps[0:NS, 0:128], lambda kc: hsT[:, kc, :], w_fn), reads=res_list, writes=[pres])
        zst, zres = zstr.next()
        S.op('act', lambda e, zst=zst, ps=ps: e.activation(out=zst[:], in_=ps[0:NS, 0:128], func=AF.Identity),
             reads=[pres], writes=[zres])
        S.op('sp', dma(zs_d[:, chunk * 128:(chunk + 1) * 128], zst[:]), reads=[zres], dma=True)
    hTm = A("hTm", [128, 8, TOK], BF16)
    ogT = A("ogT", [128, 4, TOK], BF16)
    mark_C = A.top

    load('sp', bfm[:], bfm_d, 'bfm')
    load('sp', ident[:], ident_d, 'ident')
    load('sp', validt[:], valid_d, 'valid')
    load('sp', cwfm[:], cwfm_d, 'cw')
    load('sp', cbfm[:], cbfm_d, 'cw2')
    load('sp', cgfm[:], cgfm_d, 'cw3')
    load('sp', cnbfm[:], cnbfm_d, 'cw4')
    S.op('dve', lambda e: e.tensor_scalar(out=bqs[:], in0=bfm[:, 0:12], scalar1=QS, scalar2=0.0, op0=ALU.mult, op1=ALU.add),
         reads=['bfm'], writes=['bqs'])
    S.op('pool', lambda e: e.memset(ones_bf[:], 1.0), writes=['ones'])
    S.op('pool', lambda e: e.memset(onesm[:], 1.0 / 512.0), writes=['onesm'])
    S.op('pool', lambda e: e.tensor_copy(out=ident_bf[:], in_=ident[:]), reads=['ident'], writes=['identbf'])

    hTh = A("hTh", [128, 8, TOK], BF16)
    mark_B = A.top
    cT_sb = A("cT", [128, 8, 17], F32)
    cbc_sb = A("cbc", [128, 8, 128], F32)
    bcT_sb = A("bcT", [128, 16], F32)
    bg_sb = A("bg", [128, 1024], F32)
    wcr = Ring("wc", [A("wc%d" % i, [128, 8, 512], BF16) for i in range(3)])
    cT_bf = A("cT_bf", [128, 8, 17], BF16)
    cbc_bf = A("cbc_bf", [128, 8, 128], BF16)
    load('sp', cT_sb[:], cT_d, 'cT')
    load('sp', cbc_sb[:], cbc_d, 'cbc')
    load('sp', bcT_sb[:], bcT_d, 'bcT')
    load('sp', bg_sb[:], bg_d, 'bg')
    S.op('dve', lambda e: e.tensor_copy(out=cT_bf[:], in_=cT_sb[:]), reads=['cT'], writes=['cTb'])
    S.op('dve', lambda e: e.tensor_copy(out=cbc_bf[:], in_=cbc_sb[:]), reads=['cbc'], writes=['cbcb'])
    zr = Ring("ps", PS[0:2])

    def mm8(out, lhs_fn, rhs_fn):
        def fn(pe):
            ins = None
            for kc in range(8):
                ins = pe.matmul(out, lhsT=lhs_fn(kc), rhs=rhs_fn(kc), start=(kc == 0), stop=(kc == 7))
            return ins
        return fn

    for blk in range(6):
        buf, res = wcr.next()
        load('pool', buf[:], wc_d[:, :, blk * 512:(blk + 1) * 512], res)
        if blk < 4:
            for sub in range(4):
                cc = blk * 4 + sub
                ps, pres = zr.next()
                S.op('pe', mm8(ps[:, 0:17], lambda kc, b=buf, s=sub: b[:, kc, s * 128:(s + 1) * 128],
                               lambda kc: cT_bf[:, kc, :]), reads=[res, 'cTb'], writes=[pres])
                S.op('dve', lambda e, ps=ps, cc=cc: e.tensor_scalar(
                    out=modT[:, cc, :], in0=ps[:, 0:17], scalar1=bcT_sb[:, cc:cc + 1],
                    scalar2=(1.0 if cc >= 8 else 0.0), op0=ALU.add, op1=ALU.add),
                    reads=[pres, 'bcT'], writes=['modT%d' % cc])
        else:
            half = blk - 4
            hs = slice(half * 512, (half + 1) * 512)
            ps, pres = zr.next()
            S.op('pe', mm8(ps[:, :], lambda kc: cbc_bf[:, kc, :], lambda kc, b=buf: b[:, kc, :]),
                 reads=[res, 'cbcb'], writes=[pres])
            S.op('dve', lambda e, ps=ps, hs=hs: e.tensor_tensor(out=gate_bc[:, hs], in0=ps[:, :], in1=bg_sb[:, hs], op=ALU.add),
                 reads=[pres, 'bg'], writes=['gate%d' % half])
            ps, pres = zr.next()
            S.op('pe', mm8(ps[0:NS, :], lambda kc: cT_bf[:, kc, 1:17], lambda kc, b=buf: b[:, kc, :]),
                 reads=[res, 'cTb'], writes=[pres])
            S.op('dve', lambda e, ps=ps, hs=hs: e.tensor_tensor(out=gate_s[0:NS, hs], in0=ps[0:NS, :], in1=bg_sb[0:NS, hs], op=ALU.add),
                 reads=[pres, 'bg'], writes=['gates%d' % half])

    xtr = Ring("xt", [A("xt%d" % i, [128, 8, 512], F32) for i in range(4)])
    xsT_sb = A("xsT", [128, 8, NS], F32)
    xs_tmp = A("xs_tmp", [128, 8, NS], F32)
    load('sp', xsT_sb[:], xsT_d, 'xsT')
    for tt in range(8):
        buf, res = xtr.next()
        load('sp', buf[:], xT_d[:, :, tt * 512:(tt + 1) * 512], res)
        for kc in range(8):
            dst = hTh[:, kc, tt * 512:(tt + 1) * 512] if tt < 4 else hTm[:, kc, (tt - 4) * 512:(tt - 3) * 512]
            if kc % 2 == 0:
                S.op('act', lambda e, dst=dst, buf=buf, kc=kc: e.activation(
                    out=dst, in_=buf[:, kc, :], func=AF.Identity, bias=modT[:, kc, 0:1], scale=modT[:, 8 + kc, 0:1]),
                    reads=[res, 'modT%d' % kc, 'modT%d' % (8 + kc)], writes=['hT%d_%d' % (tt, kc)])
            else:
                S.op('dve', lambda e, dst=dst, buf=buf, kc=kc: e.tensor_scalar(
                    out=dst, in0=buf[:, kc, :], scalar1=modT[:, 8 + kc, 0:1], scalar2=modT[:, kc, 0:1],
                    op0=ALU.mult, op1=ALU.add),
                    reads=[res, 'modT%d' % kc, 'modT%d' % (8 + kc)], writes=['hT%d_%d' % (tt, kc)])
    S.op('dve', lambda e: e.tensor_tensor(out=xs_tmp[:], in0=xsT_sb[:], in1=modT[:, 8:16, 1:17], op=ALU.mult),
         reads=['xsT'] + ['modT%d' % c for c in range(16)], writes=['xs_tmp'])
    S.op('dve', lambda e: e.tensor_tensor(out=hsT[:], in0=xs_tmp[:], in1=modT[:, 0:8, 1:17], op=ALU.add),
         reads=['xs_tmp'], writes=['hsT'])
    S.barrier()
    A.top = mark_B

    def hsl(kc, tok0, n, step=1):
        if tok0 < TOK:
            return hTh[:, kc, sl(tok0, n, step)]
        return hTm[:, kc, sl(tok0 - TOK, n, step)]

    hwr = Ring("hw", [A("hw%d" % i, [128, 8, 128], BF16) for i in range(4)])
    hsg = Ring("hsg", [A("hsg%d" % i, [128, 64], F32) for i in range(2)])
    zrh = Ring("ps", PS[0:4])
    for cc in range(4):
        ba, ra = hwr.next()
        load('pool', ba[:], win_d[40 + cc], ra)
        bgl, rg = hwr.next()
        load('pool', bgl[:], win_d[44 + cc], rg)
        psa, pra = zrh.next()
        S.op('pe', mm8(psa[:, 0:30], lambda kc, ba=ba: ba[:, kc, :], lambda kc: hTh[:, kc, TOK - 30:TOK]), reads=[ra], writes=[pra])
        psg, prg = zrh.next()
        S.op('pe', mm8(psg[:, 0:30], lambda kc, bgl=bgl: bgl[:, kc, :], lambda kc: hTh[:, kc, TOK - 30:TOK]), reads=[rg], writes=[prg])
        sg, sres = hsg.next()
        S.op('act', lambda e, sg=sg, psg=psg, cc=cc: e.activation(out=sg[:, 0:30], in_=psg[:, 0:30], func=AF.Sigmoid,
                                                                 bias=bfm[:, 44 + cc:45 + cc], scale=1.0),
             reads=[prg], writes=[sres])
        S.op('dve', lambda e, sg=sg, psa=psa, cc=cc: e.scalar_tensor_tensor(
            out=sg[:, 32:62], in0=psa[:, 0:30], scalar=bfm[:, 40 + cc:41 + cc], in1=sg[:, 0:30], op0=ALU.add, op1=ALU.mult),
            reads=[pra, sres], writes=[sres])
        S.op('dve', lambda e, sg=sg, cc=cc: e.tensor_scalar(
            out=uhalo[:, cc, 0:30], in0=sg[:, 32:62], scalar1=validt[:, 0:1], scalar2=0.0, op0=ALU.mult, op1=ALU.add),
            reads=[sres], writes=['uhalo%d' % cc])
    S.barrier()
    A.top = mark_B

    vtiles = [[], [], []]
    for a in range(17):
        vtiles[0].append((a, 1920 + 128 * a, 1, 0 if a == 16 else None, 1))
    for r in range(4):
        for n_ in range(5):
            vtiles[1].append((r * 5 + n_, 1536 + 512 * n_ + r, 4, r if n_ == 4 else None, 4))
    for r in range(16):
        for b in range(2):
            vtiles[2].append((r * 2 + b, 2048 * b + r, 16, r if b == 1 else None, 16))
    nvt = [17, 20, 32]
    Vt = [A("V%d" % g, [128, nvt[g], 256], BF16) for g in range(3)]
    mark_V = A.top

    for pp in range(2):
        A.top = mark_V
        wv = [A("wv%d" % g, [128, 2, 8, 128], BF16) for g in range(3)]
        bv_sb = A("bv", [128, 1536], F32)
        vstr = Ring("vst", [A("vst0", [128, 256], F32), A("vst1", [128, 256], F32)])
        load('sp', bv_sb[:], bv_d, 'bv')
        for g in range(3):
            c0 = 3072 + g * 512 + pp * 256
            for ci in range(2):
                S.op('pool', dma(wv[g][:, ci, :, :], win_d[c0 // 128 + ci]), writes=['wv%d_%d' % (g, ci)], dma=True)
        zr4 = Ring("ps", PS[0:4])
        for g in range(3):
            bsl = slice(g * 512 + pp * 256, g * 512 + pp * 256 + 256)
            for (vidx, tok0, step, row0, rstep) in vtiles[g]:
                ps, pres = zr4.next()
                S.op('pe', mm8(ps[:, 0:256], lambda kc, tok0=tok0, step=step: hsl(kc, tok0, 128, step),
                               lambda kc, g=g: wv[g][:, :, kc, :]), reads=['wv%d_0' % g, 'wv%d_1' % g], writes=[pres])
                if row0 is None:
                    S.op('dve', lambda e, ps=ps, g=g, vidx=vidx, bsl=bsl: e.tensor_tensor(
                        out=Vt[g][:, vidx, :], in0=ps[:, 0:256], in1=bv_sb[:, bsl], op=ALU.add),
                        reads=[pres, 'bv'], writes=['V%d_%d' % (g, vidx)])
                else:
                    vst, vres = vstr.next()
                    S.op('dve', lambda e, ps=ps, vst=vst, bsl=bsl: e.tensor_tensor(
                        out=vst[:], in0=ps[:, 0:256], in1=bv_sb[:, bsl], op=ALU.add),
                        reads=[pres, 'bv'], writes=[vres])
                    S.op('pool', lambda e, vst=vst, g=g, vidx=vidx: e.tensor_copy(out=Vt[g][:, vidx, :], in_=vst[:]),
                         reads=[vres], writes=['V%d_%d' % (g, vidx)])
                    S.op('sp', dma(vo_d[g][sl(row0, 128, rstep), pp * 256:(pp + 1) * 256], vst[:]),
                         reads=[vres], dma=True)
        for g in range(3):
            for ci in range(2):
                piggy((3072 + g * 512 + pp * 256) // 128 + ci, lambda kc, g=g, ci=ci: wv[g][:, ci, kc, :],
                      ['wv%d_%d' % (g, ci)], zr4)
        S.barrier()
        A.top = mark_V

        wr = [A("wr%d" % i, [128, 8, 128], BF16) for i in range(7)]
        qT = A("qT", [128, 3, TOK], BF16)
        kTl = [A("kT0", [128, 128 + TOK], BF16), A("kT1", [128, 512 + TOK], BF16), A("kT2", [128, 2 * TOK], BF16)]
        sga = A("sga", [128, TOK], BF16)
        kstr = Ring("kst", [A("kst0", [128, 512], F32), A("kst1", [128, 512], F32)])
        tbst = Ring("tbst", [A("tbst0", [128, 2, 256], F32), A("tbst1", [128, 2, 256], F32)])
        MTs = A("MTs", [128, 3, 2, 256], BF16)
        Er = Ring("E", [A("E%d" % i, [128, 512], BF16) for i in range(3)])
        Pr = Ring("P", [A("P%d" % i, [128, 512], BF16) for i in range(4)])
        P2 = A("P2", [128, 16, 256], BF16)
        rden = A("rden", [128, 512], F32)
        t1 = A("t1", [128, 512], F32)
        for jj in range(2):
            j = 2 * pp + jj
            cols = [g * 512 + j * 128 for g in range(3)] + [1536 + g * 512 + j * 128 for g in range(3)] + [4608 + j * 128]
            for i, c0 in enumerate(cols):
                load('pool', wr[i][:], win_d[c0 // 128], 'wr%d' % i)
            for g in range(3):
                h = 4 * g + j
                tbuf, tres = tbst.next()
                load('sp', tbuf[:], tb_d[:, h, :, :], tres)
                S.op('act', lambda e, tbuf=tbuf, g=g: e.activation(out=MTs[:, g, :, :], in_=tbuf[:], func=AF.Exp),
                     reads=[tres], writes=['MT%d' % g])
            zr2 = Ring("ps", PS[0:2])
            for g in range(3):
                ch = cols[g] // 128
                for tt in range(4):
                    ps, pres = zr2.next()
                    S.op('pe', mm8(ps[:, :], lambda kc, g=g: wr[g][:, kc, :],
                                   lambda kc, tt=tt: hTm[:, kc, tt * 512:(tt + 1) * 512]),
                         reads=['wr%d' % g], writes=[pres])
                    S.op('act', lambda e, ps=ps, g=g, tt=tt, ch=ch: e.activation(
                        out=qT[:, g, tt * 512:(tt + 1) * 512], in_=ps[:, :], func=AF.Identity,
                        bias=bqs[:, ch:ch + 1], scale=QS),
                        reads=[pres], writes=['qT%d_%d' % (g, tt)])
            kres = [[], [], []]
            for g in range(3):
                ch = cols[3 + g] // 128
                halo = [128, 512, 2048][g]
                tl = []
                if g == 0:
                    tl.append((1920, 128, 0, None))
                elif g == 1:
                    tl.append((1536, 512, 0, None))
                else:
                    for tt in range(4):
                        tl.append((tt * 512, 512, tt * 512, None))
                for tt in range(4):
                    tl.append((TOK + tt * 512, 512, halo + tt * 512, tt))
                for ti, (tok0, n, dst, mt) in enumerate(tl):
                    ps, pres = zr2.next()
                    S.op('pe', mm8(ps[:, 0:n], lambda kc, g=g: wr[3 + g][:, kc, :],
                                   lambda kc, tok0=tok0, n=n: hsl(kc, tok0, n)),
                         reads=['wr%d' % (3 + g)], writes=[pres])
                    kst, ksres = kstr.next()
                    S.op('dve', lambda e, ps=ps, kst=kst, n=n, ch=ch: e.tensor_scalar(
                        out=kst[:, 0:n], in0=ps[:, 0:n], scalar1=bfm[:, ch:ch + 1], scalar2=0.0,
                        op0=ALU.add, op1=ALU.add), reads=[pres], writes=[ksres])
                    kr = 'kT%d_%d' % (g, ti)
                    kres[g].append(kr)
                    S.op('act', lambda e, kst=kst, g=g, dst=dst, n=n: e.activation(
                        out=kTl[g][:, dst:dst + n], in_=kst[:, 0:n], func=AF.Identity),
                        reads=[ksres], writes=[kr])
                    if mt is not None:
                        if g == 2:
                            S.op('sp', dma(kTo_d[2][:, j, mt * 512:(mt + 1) * 512], kst[:, :]), reads=[ksres], dma=True)
                        elif g == 1 and mt == 3:
                            S.op('sp', dma(kTo_d[1][:, j, :], kst[:, :]), reads=[ksres], dma=True)
                        elif g == 0 and mt == 3:
                            S.op('sp', dma(kTo_d[0][:, j, :], kst[:, 384:512]), reads=[ksres], dma=True)
            ch = cols[6] // 128
            for tt in range(4):
                ps, pres = zr2.next()
                S.op('pe', mm8(ps[:, :], lambda kc: wr[6][:, kc, :],
                               lambda kc, tt=tt: hTm[:, kc, tt * 512:(tt + 1) * 512]),
                     reads=['wr6'], writes=[pres])
                S.op('act', lambda e, ps=ps, tt=tt, ch=ch: e.activation(
                    out=sga[:, tt * 512:(tt + 1) * 512], in_=ps[:, :], func=AF.Silu, bias=bfm[:, ch:ch + 1], scale=1.0),
                    reads=[pres], writes=['sga%d' % tt])
            for i in range(7):
                piggy(cols[i] // 128, lambda kc, i=i: wr[i][:, kc, :], ['wr%d' % i], zr2)
            qres = [['qT%d_%d' % (g, tt) for tt in range(4)] for g in range(3)]

            def tile_g0(t):
                return (0, kTl[0][:, 128 * t:128 * t + 128], kTl[0][:, 128 * (t + 1):128 * (t + 2)],
                        qT[:, 0, 128 * t:128 * (t + 1)], 1 if t == 0 else 0, t, t + 1,
                        slice(128 * (t % 4), 128 * (t % 4) + 128))

            def tile_g1(r, n_):
                return (1, kTl[1][:, sl(512 * n_ + r, 128, 4)], kTl[1][:, sl(512 + 512 * n_ + r, 128, 4)],
                        qT[:, 1, sl(512 * n_ + r, 128, 4)], 1 if n_ == 0 else 0, r * 5 + n_, r * 5 + n_ + 1,
                        sl(r, 128, 4))

            def tile_g2(r):
                return (2, kTl[2][:, sl(r, 128, 16)], kTl[2][:, sl(TOK + r, 128, 16)],
                        qT[:, 2, sl(r, 128, 16)], 1, 2 * r, 2 * r + 1, None)

            Sr = Ring("ps", PS[2:4], 2)

            def emit_S(pair, dests):
                ps, pres = Sr.next()

                def fn(pe, pair=pair, ps=ps):
                    ins = None
                    for ti, td in enumerate(pair):
                        off = ti * 256
                        pe.matmul(ps[:, off:off + 128], lhsT=td[1], rhs=td[3], start=True, stop=True)
                        ins = pe.matmul(ps[:, off + 128:off + 256], lhsT=td[2], rhs=td[3], start=True, stop=True)
                    return ins
                g = pair[0][0]
                S.op('pe', fn, reads=qres[g] + kres[g], writes=[pres])
                E, eres = Er.next()
                S.op('act', lambda e, E=E, ps=ps: e.activation(out=E[:, :], in_=ps[:, :], func=AF.Exp),
                     reads=[pres], writes=[eres])
                for ti, td in enumerate(pair):
                    pap, prs = dests[ti]
                    S.op('pool', lambda e, E=E, ti=ti, td=td, pap=pap: e.tensor_tensor(
                        out=pap, in0=E[:, ti * 256:(ti + 1) * 256], in1=MTs[:, td[0], td[4], :], op=ALU.mult),
                        reads=[eres, 'MT%d' % td[0]], writes=[prs])

            for u in range(8):
                pair = [tile_g2(2 * u), tile_g2(2 * u + 1)]
                emit_S(pair, [(P2[:, 2 * u, :], 'P2_%d' % (2 * u)), (P2[:, 2 * u + 1, :], 'P2_%d' % (2 * u + 1))])

            for w in range(4):
                num, nres = PS[4 + (w % 2)], PSR[4 + (w % 2)]
                den, dres = PS[6 + (w % 2)], PSR[6 + (w % 2)]
                pairs = [[tile_g0(4 * w), tile_g0(4 * w + 1)], [tile_g0(4 * w + 2), tile_g0(4 * w + 3)],
                         [tile_g1(0, w), tile_g1(1, w)], [tile_g1(2, w), tile_g1(3, w)]]
                pbufs = []
                state = {'first': True}

                def emit_PV(pair, pb, pres_, num=num, den=den, nres=nres, dres=dres, state=state):
                    first = state['first']
                    state['first'] = False

                    def fn(pe, pair=pair, pb=pb, first=first, jj=jj):
                        ins = None
                        st = first
                        for ti, td in enumerate(pair):
                            g = td[0]
                            off = ti * 256
                            oc = td[7]
                            pe.matmul(num[:, oc], lhsT=Vt[g][:, td[5], jj * 128:(jj + 1) * 128], rhs=pb[:, off:off + 128],
                                      start=st, stop=False, skip_group_check=True)
                            pe.matmul(num[:, oc], lhsT=Vt[g][:, td[6], jj * 128:(jj + 1) * 128], rhs=pb[:, off + 128:off + 256],
                                      start=False, stop=False, skip_group_check=True)
                            pe.matmul(den[:, oc], lhsT=ones_bf[:], rhs=pb[:, off:off + 128],
                                      start=st, stop=False, skip_group_check=True)
                            ins = pe.matmul(den[:, oc], lhsT=ones_bf[:], rhs=pb[:, off + 128:off + 256],
                                            start=False, stop=False, skip_group_check=True)
                            st = False
                        return ins
                    S.op('pe', fn, reads=[pres_ + 'a', pres_ + 'b'], writes=[nres, dres])

                prev = None
                for pi, pair in enumerate(pairs):
                    pb, pres_ = Pr.next()
                    emit_S(pair, [(pb[:, 0:256], pres_ + 'a'), (pb[:, 256:512], pres_ + 'b')])
                    if prev is not None:
                        emit_PV(*prev)
                    prev = (pair, pb, pres_)
                emit_PV(*prev)

                def fn2(pe, w=w, num=num, den=den, jj=jj):
                    ins = None
                    for r in range(16):
                        oc = sl(r, 32, 16)
                        for blk in range(2):
                            rhs = P2[:, r, blk * 128 + 32 * w:blk * 128 + 32 * w + 32]
                            pe.matmul(num[:, oc], lhsT=Vt[2][:, 2 * r + blk, jj * 128:(jj + 1) * 128], rhs=rhs,
                                      start=False, stop=False, skip_group_check=True)
                            ins = pe.matmul(den[:, oc], lhsT=ones_bf[:], rhs=rhs,
                                            start=False, stop=(r == 15 and blk == 1), skip_group_check=True)
                    return ins
                S.op('pe', fn2, reads=['P2_%d' % r for r in range(16)], writes=[nres, dres])
                ws = slice(w * 512, (w + 1) * 512)
                S.op('dve', lambda e, den=den: e.reciprocal(out=rden[:], in_=den[:, :]), reads=[dres], writes=['rden'])
                S.op('dve', lambda e, num=num: e.tensor_tensor(out=t1[:], in0=num[:, :], in1=rden[:], op=ALU.mult),
                     reads=[nres, 'rden'], writes=['t1'])
                S.op('dve', lambda e, ws=ws, j=j: e.tensor_tensor(out=ogT[:, j, ws], in0=t1[:], in1=sga[:, ws], op=ALU.mult),
                     reads=['t1', 'sga%d' % w], writes=['og%d_%d' % (j, w)])
        S.barrier()
    A.top = mark_C

    wpa = A("wpa", [128, 4, 1024], BF16)
    wpb = A("wpb", [128, 4, 1024], BF16)
    wo = A("wo", [128, 8, 1024], BF16)
    lng = A("lng", [128, 1024], F32)
    lnb = A("lnb", [128, 1024], F32)
    mark_P4w = A.top
    load('pool', wpa[:], wpa_d, 'wpa')
    load('pool', wpb[:], wpb_d, 'wpb')
    load('pool', wo[:], wo_d, 'wo')
    load('sp', lng[:], lng_d, 'lng')
    load('sp', lnb[:], lnb_d, 'lnb')
    wrr = Ring("wr", [A("wr%d" % i, [128, 8, 128], BF16) for i in range(7)])
    diagr = Ring("dg", [A("dg%d" % i, [128, 31, 128], BF16) for i in range(2)])
    uTr = [A("uT%d" % i, [128, 4, 542], BF16) for i in range(2)]
    sgr = Ring("sg", [A("sg%d" % i, [128, 512], F32) for i in range(2)])
    ybf = A("ybf", [128, 4, 512], BF16)
    ysq = A("ysq", [128, 4, 512], BF16)
    mean = A("mean", [128, 512], F32)
    var = A("var", [128, 512], F32)
    rstd = A("rstd", [128, 512], F32)
    tnr = Ring("tn", [A("tn%d" % i, [128, 512], F32) for i in range(2)])
    cnr = Ring("cn", [A("cn%d" % i, [128, 512], BF16) for i in range(2)])
    sgbt = A("sgbt", [128, 4, 512], BF16)
    cg = A("cg", [128, 4, 512], BF16)
    smar = Ring("sma", [A("sma%d" % i, [128, 512], F32) for i in range(2)])
    tbr = Ring("tbb", [A("tbb%d" % i, [128, 512], F32) for i in range(2)])
    mT = A("mT", [128, 8, 512], BF16)
    rr = Ring("r", [A("r%d" % i, [128, 1024], F32) for i in range(2)])
    xkr = Ring("xk", [A("xk%d" % i, [128, 1024], F32) for i in range(2)])
    sqb = A("sqb", [128, 1024], F32)
    st = A("st", [128, 8], F32)
    ulast = A("ulast", [128, 4, 30], F32)
    zr4 = Ring("ps", [PS[i] for i in (0, 1, 2, 3, 6, 7)], names=["ps%d" % i for i in (0, 1, 2, 3, 6, 7)])
    cvr = Ring("ps", PS[4:6], 4)

    def wload(c0):
        buf, res = wrr.next()
        load('pool', buf[:], win_d[c0 // 128], res)
        wl_chunk[id(buf)] = c0 // 128
        return buf, res

    wl_chunk = {}

    def zmm(buf, res, w):
        ps, pres = zr4.next()
        t0 = TOK + w * 512
        S.op('pe', mm8(ps[:, :], lambda kc, buf=buf: buf[:, kc, :],
                       lambda kc, t0=t0: hsl(kc, t0, 512)), reads=[res], writes=[pres])
        if w == 0:
            piggy(wl_chunk[id(buf)], lambda kc, buf=buf: buf[:, kc, :], [res], zr4)
        return ps, pres

    def final_stage(w):
        xks = {}

        def xload(ts):
            xk, xres = xkr.next()
            r0 = w * 512 + ts * 128
            load('sp', xk[:], xtok_d[r0:r0 + 128, :], xres)
            xks[ts] = (xk, xres)
        xload(0)
        xload(1)
        for ts in range(4):
            r, rres = rr.next()
            xk, xres = xks[ts]
            row0 = w * 512 + ts * 128
            for half in range(2):
                hs = slice(half * 512, (half + 1) * 512)
                psy, pry = zr4.next()

                def fny(pe, psy=psy, ts=ts, hs=hs):
                    ins = None
                    for oc in range(8):
                        ins = pe.matmul(psy[:, :], lhsT=mT[:, oc, ts * 128:(ts + 1) * 128], rhs=wo[:, oc, hs],
                                        start=(oc == 0), stop=(oc == 7))
                    return ins
                S.op('pe', fny, reads=['wo'] + ['mT%d' % c for c in range(8)], writes=[pry])
                S.op('dve', lambda e, psy=psy, r=r, hs=hs, xk=xk: e.scalar_tensor_tensor(
                    out=r[:, hs], in0=xk[:, hs], scalar=ALPHA, in1=psy[:, :], op0=ALU.mult, op1=ALU.add),
                    reads=[pry, xres], writes=[rres])
            emit_ln(S, r, rres, 128, sqb, st, lng[:, :], lnb[:, :], y_d[row0:row0 + 128, :], ['lng', 'lnb'])
            if ts + 2 < 4:
                xload(ts + 2)

    for w in range(4):
        uT = uTr[w % 2]
        uTn = uTr[(w + 1) % 2]
        up = w % 2
        if w == 0:
            for cc in range(4):
                S.op('pool', lambda e, cc=cc: e.tensor_copy(out=uTr[0][:, cc, 0:30], in_=uhalo[:, cc, 0:30]),
                     writes=['uh0_%d' % cc])
        for cc in range(4):
            ba, ra = wload(5120 + cc * 128)
            bgl, rg = wload(5632 + cc * 128)
            psa, pra = zmm(ba, ra, w)
            psg, prg = zmm(bgl, rg, w)
            sg, sres = sgr.next()
            S.op('act', lambda e, sg=sg, psg=psg, cc=cc: e.activation(out=sg[:, :], in_=psg[:, :], func=AF.Sigmoid,
                                                                     bias=bfm[:, 44 + cc:45 + cc], scale=1.0),
                 reads=[prg], writes=[sres])
            S.op('dve', lambda e, sg=sg, psa=psa, cc=cc, uT=uT: e.scalar_tensor_tensor(
                out=uT[:, cc, 30:542], in0=psa[:, :], scalar=bfm[:, 40 + cc:41 + cc], in1=sg[:, :], op0=ALU.add, op1=ALU.mult),
                reads=[pra, sres, 'uh%d_%d' % (up, cc)], writes=['u%d_%d' % (up, cc)])
            if w == 3:
                S.op('dve', lambda e, sg=sg, psa=psa, cc=cc: e.scalar_tensor_tensor(
                    out=ulast[:, cc, :], in0=psa[:, 482:512], scalar=bfm[:, 40 + cc:41 + cc], in1=sg[:, 482:512],
                    op0=ALU.add, op1=ALU.mult), reads=[pra, sres], writes=['ulast%d' % cc])
            else:
                S.op('dve', lambda e, uT=uT, uTn=uTn, cc=cc: e.tensor_copy(out=uTn[:, cc, 0:30], in_=uT[:, cc, 512:542]),
                     reads=['u%d_%d' % (up, cc)], writes=['uh%d_%d' % (1 - up, cc)])
        for cc in range(4):
            dg, dgres = diagr.next()
            S.op('pool', lambda e, dg=dg, cc=cc: e.tensor_tensor(
                out=dg[:, :, :], in0=ident_bf[:, :].unsqueeze(1).broadcast_to([128, 31, 128]),
                in1=cwfm[:, cc, :].unsqueeze(2).broadcast_to([128, 31, 128]), op=ALU.mult), writes=[dgres])
            psc, prc = cvr.next()

            def fnc(pe, psc=psc, dg=dg, uT=uT, cc=cc):
                ins = None
                for jt in range(31):
                    ins = pe.matmul(psc[:, :], lhsT=dg[:, jt, :], rhs=uT[:, cc, jt:jt + 512], start=(jt == 0), stop=(jt == 30))
                return ins
            S.op('pe', fnc, reads=[dgres, 'u%d_%d' % (up, cc), 'uh%d_%d' % (up, cc)],
                 writes=[prc])
            S.op('act', lambda e, psc=psc, cc=cc: e.activation(out=ybf[:, cc, :], in_=psc[:, :], func=AF.Identity,
                                                               bias=cbfm[:, cc:cc + 1], scale=1.0),
                 reads=[prc], writes=['ybf%d' % cc])
            S.op('act', lambda e, psc=psc, cc=cc: e.activation(out=ysq[:, cc, :], in_=psc[:, :], func=AF.Square,
                                                               bias=cbfm[:, cc:cc + 1], scale=1.0),
                 reads=[prc], writes=['ysq%d' % cc])
        if w == 0:
            S.op('dve', lambda e: e.tensor_tensor(out=wo[:, :, :], in0=wo[:, :, :],
                                                  in1=gate_bc[:, :].unsqueeze(1).broadcast_to([128, 8, 1024]), op=ALU.mult),
                 reads=['wo'], writes=['wo'])
        if w > 0:
            final_stage(w - 1)
        psm, prm = zr4.next()

        def fnm(pe, psm=psm):
            ins = None
            for cc in range(4):
                ins = pe.matmul(psm[:, :], lhsT=onesm[:], rhs=ybf[:, cc, :], start=(cc == 0), stop=(cc == 3))
            return ins
        S.op('pe', fnm, reads=['ybf%d' % c for c in range(4)], writes=[prm])
        psq, prq = zr4.next()

        def fnq(pe, psq=psq):
            ins = None
            for cc in range(4):
                ins = pe.matmul(psq[:, :], lhsT=onesm[:], rhs=ysq[:, cc, :], start=(cc == 0), stop=(cc == 3))
            return ins
        S.op('pe', fnq, reads=['ysq%d' % c for c in range(4)], writes=[prq])
        S.op('dve', lambda e, psm=psm: e.tensor_copy(out=mean[:], in_=psm[:, :]), reads=[prm], writes=['mean'])
        S.op('dve', lambda e: e.tensor_tensor(out=var[:], in0=mean[:], in1=mean[:], op=ALU.mult), reads=['mean'], writes=['var'])
        S.op('dve', lambda e, psq=psq: e.tensor_tensor(out=var[:], in0=psq[:, :], in1=var[:], op=ALU.subtract),
             reads=[prq, 'var'], writes=['var'])
        S.op('act', lambda e: e.activation(out=rstd[:], in_=var[:], func=AF.Sqrt, bias=EPS, scale=1.0),
             reads=['var'], writes=['rstd'])
        S.op('dve', lambda e: e.reciprocal(out=rstd[:], in_=rstd[:]), reads=['rstd'], writes=['rstd'])
        for cc in range(4):
            bgb, rgb = wload(6144 + cc * 128)
            psb_, prb = zmm(bgb, rgb, w)
            S.op('act', lambda e, psb_=psb_, cc=cc: e.activation(out=sgbt[:, cc, :], in_=psb_[:, :], func=AF.Silu,
                                                                bias=bfm[:, 48 + cc:49 + cc], scale=1.0),
                 reads=[prb], writes=['sgb%d' % cc])
        for oc in range(8):
            psa, pra = zr4.next()

            def fna(pe, psa=psa, oc=oc, w=w):
                ins = None
                for cc in range(4):
                    ins = pe.matmul(psa[:, :], lhsT=wpa[:, cc, oc * 128:(oc + 1) * 128], rhs=ogT[:, cc, w * 512:(w + 1) * 512],
                                    start=(cc == 0), stop=(cc == 3))
                return ins
            S.op('pe', fna, reads=['wpa'], writes=[pra])
            bma, rma = wload(6656 + oc * 128)
            psma, prma = zmm(bma, rma, w)
            sma, smres = smar.next()
            S.op('act', lambda e, sma=sma, psma=psma, oc=oc: e.activation(out=sma[:], in_=psma[:, :], func=AF.Sigmoid,
                                                                         bias=bfm[:, 52 + oc:53 + oc], scale=1.0),
                 reads=[prma], writes=[smres])
            S.op('dve', lambda e, psa=psa, sma=sma, oc=oc: e.tensor_tensor(out=mT[:, oc, :], in0=psa[:, :], in1=sma[:], op=ALU.mult),
                 reads=[pra, smres], writes=['mT%d' % oc])
        for cc in range(4):
            tn, tres = tnr.next()
            S.op('dve', lambda e, tn=tn, cc=cc: e.tensor_tensor(out=tn[:], in0=ybf[:, cc, :], in1=mean[:], op=ALU.subtract),
                 reads=['ybf%d' % cc, 'mean'], writes=[tres])
            S.op('dve', lambda e, tn=tn: e.tensor_tensor(out=tn[:], in0=tn[:], in1=rstd[:], op=ALU.mult),
                 reads=[tres, 'rstd'], writes=[tres])
            cn, cres = cnr.next()
            S.op('act', lambda e, tn=tn, cn=cn, cc=cc: e.activation(out=cn[:], in_=tn[:], func=AF.Silu,
                                                                   bias=cnbfm[:, cc:cc + 1], scale=cgfm[:, cc:cc + 1]),
                 reads=[tres], writes=[cres])
            S.op('dve', lambda e, cn=cn, cc=cc: e.tensor_tensor(out=cg[:, cc, :], in0=cn[:], in1=sgbt[:, cc, :], op=ALU.mult),
                 reads=[cres, 'sgb%d' % cc], writes=['cg%d' % cc])
        for oc in range(8):
            psb_, prb = zr4.next()

            def fnb(pe, psb_=psb_, oc=oc):
                ins = None
                for cc in range(4):
                    ins = pe.matmul(psb_[:, :], lhsT=wpb[:, cc, oc * 128:(oc + 1) * 128], rhs=cg[:, cc, :],
                                    start=(cc == 0), stop=(cc == 3))
                return ins
            bmb, rmb = wload(7680 + oc * 128)
            psmb, prmb = zmm(bmb, rmb, w)
            smb, sbres2 = smar.next()
            S.op('act', lambda e, smb=smb, psmb=psmb, oc=oc: e.activation(out=smb[:], in_=psmb[:, :], func=AF.Sigmoid,
                                                                         bias=bfm[:, 60 + oc:61 + oc], scale=1.0),
                 reads=[prmb], writes=[sbres2])
            S.op('pe', fnb, reads=['wpb'] + ['cg%d' % c for c in range(4)], writes=[prb])
            tb_, tbres = tbr.next()
            S.op('dve', lambda e, psb_=psb_, smb=smb, tb_=tb_: e.tensor_tensor(out=tb_[:], in0=psb_[:, :], in1=smb[:], op=ALU.mult),
                 reads=[prb, sbres2], writes=[tbres])
            S.op('dve', lambda e, oc=oc, tb_=tb_: e.tensor_tensor(out=mT[:, oc, :], in0=mT[:, oc, :], in1=tb_[:], op=ALU.add),
                 reads=['mT%d' % oc, tbres], writes=['mT%d' % oc])
    final_stage(3)
    S.op('sp', dma(uT_d, ulast[:]), reads=['ulast%d' % c for c in range(4)], dma=True)
    S.barrier()
    A.top = mark_persist

    zs = A("zs", [NS, 8704], F32)
    xs = A("xs", [NS, 1024], F32)
    cvec = A("cvec", [NS, 3, 512], F32)
    assert A.top <= mark_C
    A.top = mark_P4w
    snew = A("snew", [NS, 12], F32)
    acc = A("acc", [NS, 3, 512], F32)
    dens = A("dens", [NS, 12], F32)
    osb = A("osb", [NS, 512], F32)
    us = A("us", [NS, 512], F32)
    ycs = A("ycs", [NS, 512], F32)
    ptmp = A("ptmp", [NS, 512], F32)
    st5 = A("st5", [NS, 8], F32)
    sqb5 = A("sqb5", [NS, 1024], F32)
    wpa5, wpb5, wo5, lng5, lnb5 = wpa, wpb, wo, lng, lnb
    load('pool', wo5[:], wo_d, 'wo5')
    S.op('sp', lambda e: e.nop(), writes=['wpa5', 'wpb5', 'lng5', 'lnb5'])
    mark5 = A.top
    load('sp', xs[:], xstok_d, 'xs')
    load('sp', cvec[:], cvec_d, 'cvec')
    S.op('sp', dma(convs_d[:, 0:29, :], sconv_d[:, 1:30, :]), dma=True)

    binbc = A("binbc", [NS, 8704], F32)
    tmp16 = A("tmp16", [NS, 1536], F32)
    load('sp', zs[:], zs_d, 'zsraw')
    load('sp', binbc[:], binbc_d, 'binbc')
    for q4 in range(4):
        cs = slice(q4 * 2176, (q4 + 1) * 2176)
        S.op('dve', lambda e, cs=cs: e.tensor_tensor(out=zs[:, cs], in0=zs[:, cs], in1=binbc[:, cs], op=ALU.add),
             reads=['zsraw', 'binbc'], writes=['zs%d' % q4])
    ZALL = ['zs%d' % b for b in range(4)]
    S.op('sp', dma(ksn_d, zs[:, 1536:3072]), reads=ZALL, dma=True)
    S.op('sp', dma(vsn_d, zs[:, 3072:4608]), reads=ZALL, dma=True)
    S.op('dve', lambda e: e.tensor_scalar(out=zs[:, 0:1536], in0=zs[:, 0:1536], scalar1=QS, scalar2=0.0, op0=ALU.mult, op1=ALU.add),
         reads=ZALL, writes=['qs'])
    S.op('dve', lambda e: e.tensor_tensor(out=tmp16[:], in0=zs[:, 0:1536], in1=zs[:, 1536:3072], op=ALU.mult),
         reads=['qs'] + ZALL, writes=['tmp16'])
    S.op('dve', lambda e: e.reduce_sum(out=snew[:], in_=tmp16[:].rearrange("p (h d) -> p h d", d=128), axis=AX.X),
         reads=['tmp16'], writes=['snew'])
    S.op('act', lambda e: e.activation(out=snew[:], in_=snew[:], func=AF.Exp), reads=['snew'], writes=['pnew'])
    S.barrier()
    A.top = mark5

    sel = A("sel", [NS, NS, 128], F32)
    selb = A("selb", [NS, NS, 128], BF16)
    qsb = A("qsb", [NS, 1536], BF16)
    tbs = A("tbs", [128, 12], F32)
    pzz = A("pzz", [128, 12, NS, NS], BF16)
    onesf = A("onesf", [128, 1], BF16)
    kvbr = Ring("kvb", [A("kvb%d" % i, [128, 512], BF16) for i in range(3)])
    kvr = Ring("kv", [A("kv%d" % i, [128, 1024], F32) for i in range(4)])
    prodr = Ring("prod", [A("prod%d" % i, [128, 512], F32) for i in range(2)])
    scr = Ring("sc", [A("sc%d" % i, [128, 4], F32) for i in range(3)])
    load('sp', sel[:], sel_d, 'sel')
    load('sp', tbs[:], tbs_d, 'tbs')
    S.op('pool', lambda e: e.memset(pzz[:], 0.0), writes=['pzz'])
    S.op('pool', lambda e: e.memset(onesf[:], 1.0), writes=['onesf'])
    S.op('pool', lambda e: e.tensor_copy(out=selb[:], in_=sel[:]), reads=['sel'], writes=['selb'])
    S.op('pool', lambda e: e.tensor_copy(out=qsb[:], in_=zs[:, 0:1536]), writes=['qsb'])
    accps = [PS[2], PS[3], PS[4]]
    denps = PS[5]
    qbr = Ring("ps", PS[6:8], 6)
    first_acc = [True, True, True]
    first_den = [True]
    items = [(b, g) for b in range(NS) for g in range(3)]

    def stageA(b, g):
        d = GROUPS[g][1]
        kv, kres_ = kvr.next()
        load('sp', kv[:], ck_d[g][b, sl(0, 128, d), :], kres_)
        qb, qbres = qbr.next()
        S.op('pe', lambda pe, qb=qb, b=b, g=g: pe.matmul(qb[:, :], lhsT=selb[:, b, :], rhs=qsb[:, g * 512:(g + 1) * 512],
                                                        start=True, stop=True),
             reads=['selb', 'qsb'], writes=[qbres])
        kvb, kvbres = kvbr.next()
        S.op('act', lambda e, kvb=kvb, kv=kv: e.activation(out=kvb[:], in_=kv[:, 512:1024], func=AF.Identity), reads=[kres_], writes=[kvbres])
        prod, pres_ = prodr.next()
        sc, sres = scr.next()
        for hh in range(4):
            S.op('dve', lambda e, prod=prod, kv=kv, qb=qb, sc=sc, hh=hh: e.scalar_tensor_tensor(
                out=prod[:, hh * 128:(hh + 1) * 128], in0=kv[:, hh * 128:(hh + 1) * 128], scalar=1.0,
                in1=qb[:, hh * 128:(hh + 1) * 128], op0=ALU.mult, op1=ALU.mult, accum_out=sc[:, hh:hh + 1]),
                reads=[kres_, qbres], writes=[sres + '_%d' % hh, pres_ + '_%d' % hh] + ([sres] if hh == 0 else []))
        S.op('dve', lambda e, sc=sc, g=g: e.tensor_tensor(out=sc[:], in0=sc[:], in1=tbs[:, 4 * g:4 * g + 4], op=ALU.add),
             reads=[sres + '_%d' % hh for hh in range(4)] + ['tbs'], writes=[sres])
        pcol = 'pz%d_%d' % (b, g)
        S.op('act', lambda e, sc=sc, g=g, b=b: e.activation(out=pzz[:, 4 * g:4 * g + 4, b, b], in_=sc[:], func=AF.Exp),
             reads=[sres, 'pzz'], writes=[pcol])
        return (b, g, kvb, kvbres, pcol)

    def stageB(b, g, kv, kres_, pcol):
        def fnpv(pe, g=g, b=b, kv=kv, fa=first_acc[g], fd=first_den[0]):
            ins = None
            for hh in range(4):
                pe.matmul(accps[g][0:NS, hh * 128:(hh + 1) * 128], lhsT=pzz[:, 4 * g + hh, b, :],
                          rhs=kv[:, hh * 128:(hh + 1) * 128],
                          start=(fa and hh == 0), stop=(b == NS - 1 and hh == 3), skip_group_check=True)
                ins = pe.matmul(denps[0:NS, 4 * g + hh:4 * g + hh + 1], lhsT=pzz[:, 4 * g + hh, b, :], rhs=onesf[:, 0:1],
                                start=(fd and hh == 0), stop=(b == NS - 1 and g == 2 and hh == 3), skip_group_check=True)
            return ins
        S.op('pe', fnpv, reads=[pcol, kres_, 'onesf'], writes=['accps%d' % g, 'denps'])
        first_acc[g] = False
        first_den[0] = False


    class _DQ:
        def __init__(self):
            self.q = []

        def op(self, *a, **k):
            self.q.append((a, k))

        def load(self, out, in_, res):
            self.q.append((('sp', dma(out, in_)), dict(writes=[res], dma=True)))

        def flush(self, n):
            while n > 0 and self.q:
                a, k = self.q.pop(0)
                S.op(*a, **k)
                n -= 1
    DQ = _DQ()
    stc = A("stc", [NS, 5, 512], F32)
    cwb = A("cwb", [NS, 5, 512], F32)
    DQ.op('act', lambda e: e.activation(out=us[:], in_=zs[:, 5632:6144], func=AF.Sigmoid), writes=['us'])
    DQ.op('pool', lambda e: e.tensor_tensor(out=us[:], in0=us[:], in1=zs[:, 5120:5632], op=ALU.mult), reads=['us'], writes=['us'])
    DQ.op('sp', dma(convs_d[:, 29, :], us[:]), reads=['us'], dma=True)
    DQ.op('pool', lambda e: e.tensor_copy(out=ycs[:], in_=cvec[:, 0, :]), reads=['cvec'], writes=['ycs'])
    for hf in range(6):
        DQ.load(stc[:], sconv_d[:, hf * 5:(hf + 1) * 5, :], 'stc')
        DQ.load(cwb[:], cwbc_d[:, hf * 5:(hf + 1) * 5, :], 'cwb')
        DQ.op('pool', lambda e: e.tensor_tensor(out=stc[:], in0=stc[:], in1=cwb[:], op=ALU.mult), reads=['stc', 'cwb'], writes=['stc'])
        for jt in range(5):
            DQ.op('pool', lambda e, jt=jt: e.tensor_tensor(out=ycs[:], in0=ycs[:], in1=stc[:, jt, :], op=ALU.add),
                 reads=['stc', 'ycs'], writes=['ycs'])
    DQ.load(cwb[:, 0, :], cwbc_d[:, 30, :], 'cwb')
    DQ.op('pool', lambda e: e.tensor_tensor(out=ptmp[:], in0=us[:], in1=cwb[:, 0, :], op=ALU.mult), reads=['us', 'cwb'], writes=['ptmp'])
    DQ.op('pool', lambda e: e.tensor_tensor(out=ycs[:], in0=ycs[:], in1=ptmp[:], op=ALU.add), reads=['ptmp', 'ycs'], writes=['ycs'])
    emit_ln(DQ, ycs, 'ycs', NS, sqb5, st5, cvec[:, 1, :], cvec[:, 2, :], None, ['cvec'], width=512)
    DQ.op('act', lambda e: e.activation(out=ycs[:], in_=ycs[:], func=AF.Silu), reads=['ycs'], writes=['ycs'])
    DQ.op('act', lambda e: e.activation(out=ptmp[:], in_=zs[:, 6144:6656], func=AF.Silu), reads=['ptmp'], writes=['ptmp'])
    DQ.op('pool', lambda e: e.tensor_tensor(out=ycs[:], in0=ycs[:], in1=ptmp[:], op=ALU.mult), reads=['ycs', 'ptmp'], writes=['ycs'])

    pend = []
    for (b, g) in items:
        pend.append(stageA(b, g))
        if len(pend) > 2:
            stageB(*pend.pop(0))
        DQ.flush(2)
    while pend:
        stageB(*pend.pop(0))
    DQ.flush(10000)
    for g in range(3):
        for jh in range(4):
            h = 4 * g + jh
            S.op('dve', lambda e, g=g, jh=jh, h=h: e.scalar_tensor_tensor(
                out=acc[:, g, jh * 128:(jh + 1) * 128], in0=zs[:, 3072 + h * 128:3072 + (h + 1) * 128],
                scalar=snew[:, h:h + 1], in1=accps[g][0:NS, jh * 128:(jh + 1) * 128], op0=ALU.mult, op1=ALU.add),
                reads=['accps%d' % g], writes=['acc%d_%d' % (g, jh)])
    S.op('dve', lambda e: e.tensor_tensor(out=dens[:], in0=denps[0:NS, 0:12], in1=snew[:], op=ALU.add),
         reads=['denps'], writes=['dens'])
    S.barrier()
    A.top = mark5

    sgs = A("sgs", [NS, 1024], F32)
    sgs2 = A("sgs2", [NS, 1024], F32)
    ms = A("ms", [NS, 1024], F32)
    trT = A("trT", [128, 8, NS], BF16)
    trT2 = A("trT2", [128, 4, NS], BF16)
    rs = A("rs", [NS, 1024], F32)
    zr2 = Ring("ps", PS[0:2])
    S.op('dve', lambda e: e.tensor_tensor(out=acc[:, 0, :], in0=acc[:, 0, :], in1=acc[:, 1, :], op=ALU.add), writes=['accs'])
    S.op('dve', lambda e: e.tensor_tensor(out=acc[:, 0, :], in0=acc[:, 0, :], in1=acc[:, 2, :], op=ALU.add), reads=['accs'], writes=['accs'])
    S.op('dve', lambda e: e.tensor_tensor(out=dens[:, 0:4], in0=dens[:, 0:4], in1=dens[:, 4:8], op=ALU.add), writes=['dens'])
    S.op('dve', lambda e: e.tensor_tensor(out=dens[:, 0:4], in0=dens[:, 0:4], in1=dens[:, 8:12], op=ALU.add), reads=['dens'], writes=['dens'])
    S.op('dve', lambda e: e.reciprocal(out=dens[:, 0:4], in_=dens[:, 0:4]), reads=['dens'], writes=['dens'])
    S.op('act', lambda e: e.activation(out=ptmp[:], in_=zs[:, 4608:5120], func=AF.Silu), writes=['ptmp'])
    for jh in range(4):
        S.op('dve', lambda e, jh=jh: e.scalar_tensor_tensor(
            out=osb[:, jh * 128:(jh + 1) * 128], in0=acc[:, 0, jh * 128:(jh + 1) * 128], scalar=dens[:, jh:jh + 1],
            in1=ptmp[:, jh * 128:(jh + 1) * 128], op0=ALU.mult, op1=ALU.mult),
            reads=['accs', 'dens', 'ptmp'], writes=['osb%d' % jh])
    OSB = ['osb%d' % j for j in range(4)]

    def transp(src_fn, nch, dstT, tag, reads):
        ps, pres = zr2.next()

        def fn(pe):
            ins = None
            for c in range(nch):
                ins = pe.matmul(ps[:, c * NS:(c + 1) * NS], lhsT=src_fn(c), rhs=ident[0:NS, 0:NS], start=True, stop=True)
            return ins
        S.op('pe', fn, reads=reads, writes=[pres])
        S.op('dve', lambda e: e.tensor_copy(out=dstT[:, 0:nch, :], in_=ps[:, 0:nch * NS].rearrange("p (c n) -> p c n", n=NS)),
             reads=[pres], writes=[tag])

    transp(lambda c: osb[:, c * 128:(c + 1) * 128], 4, trT, 'ogsT', OSB)
    transp(lambda c: ycs[:, c * 128:(c + 1) * 128], 4, trT2, 'cgsT', ['ycs'])
    S.op('act', lambda e: e.activation(out=sgs[:], in_=zs[:, 6656:7680], func=AF.Sigmoid), writes=['sgs'])
    S.op('act', lambda e: e.activation(out=sgs2[:], in_=zs[:, 7680:8704], func=AF.Sigmoid), writes=['sgs2'])
    for half in range(2):
        hs = slice(half * 512, (half + 1) * 512)
        ps, pres = zr2.next()

        def fa5(pe, ps=ps, hs=hs):
            ins = None
            for cc in range(4):
                ins = pe.matmul(ps[0:NS, :], lhsT=trT[:, cc, :], rhs=wpa5[:, cc, hs], start=(cc == 0), stop=(cc == 3))
            return ins
        S.op('pe', fa5, reads=['ogsT', 'wpa5'], writes=[pres])
        S.op('dve', lambda e, ps=ps, hs=hs: e.tensor_tensor(out=ms[:, hs], in0=ps[0:NS, :], in1=sgs[:, hs], op=ALU.mult),
             reads=[pres, 'sgs'], writes=['msa%d' % half])
        ps, pres = zr2.next()

        def fb5(pe, ps=ps, hs=hs):
            ins = None
            for cc in range(4):
                ins = pe.matmul(ps[0:NS, :], lhsT=trT2[:, cc, :], rhs=wpb5[:, cc, hs], start=(cc == 0), stop=(cc == 3))
            return ins
        S.op('pe', fb5, reads=['cgsT', 'wpb5'], writes=[pres])
        S.op('dve', lambda e, ps=ps, hs=hs: e.tensor_tensor(out=sgs2[:, hs], in0=ps[0:NS, :], in1=sgs2[:, hs], op=ALU.mult),
             reads=[pres, 'sgs2'], writes=['msb%d' % half])
        S.op('dve', lambda e, hs=hs: e.tensor_tensor(out=ms[:, hs], in0=ms[:, hs], in1=sgs2[:, hs], op=ALU.add),
             reads=['msa%d' % half, 'msb%d' % half], writes=['ms%d' % half])
    transp(lambda c: ms[:, c * 128:(c + 1) * 128], 8, trT, 'msT', ['ms0', 'ms1', 'ogsT'])
    for half in range(2):
        hs = slice(half * 512, (half + 1) * 512)
        ps, pres = zr2.next()

        def fy5(pe, ps=ps, hs=hs):
            ins = None
            for oc in range(8):
                ins = pe.matmul(ps[0:NS, :], lhsT=trT[:, oc, :], rhs=wo5[:, oc, hs], start=(oc == 0), stop=(oc == 7))
            return ins
        S.op('pe', fy5, reads=['msT', 'wo5'], writes=[pres])
        S.op('dve', lambda e, ps=ps, hs=hs: e.tensor_tensor(out=rs[:, hs], in0=ps[0:NS, :], in1=gate_s[0:NS, hs], op=ALU.mult),
             reads=[pres], writes=['rs%d' % half])
    S.op('dve', lambda e: e.scalar_tensor_tensor(out=rs[:], in0=xs[:], scalar=ALPHA, in1=rs[:], op0=ALU.mult, op1=ALU.add),
         reads=['rs0', 'rs1', 'xs'], writes=['rs'])
    emit_ln(S, rs, 'rs', NS, sqb5, st5, lng5[0:NS, :], lnb5[0:NS, :], ys_d, ['lng5', 'lnb5'])
    S.barrier()
    return nc, S


def emit_ln(S, r, rres, np_, sqb, st, g_ap, b_ap, out_dram, extra_reads, width=1024, eng2='dve'):
    inv = 1.0 / width
    rv = r[0:np_, 0:width]
    S.op('act', lambda e: e.activation(out=sqb[0:np_, 0:width], in_=rv, func=AF.Identity, accum_out=st[0:np_, 0:1]),
         reads=[rres], writes=['st0', 'sqb'])
    S.op('act', lambda e: e.activation(out=sqb[0:np_, 0:width], in_=rv, func=AF.Square, accum_out=st[0:np_, 1:2]),
         reads=[rres, 'sqb'], writes=['st1', 'sqb'])
    S.op('dve', lambda e: e.tensor_scalar(out=st[0:np_, 2:3], in0=st[0:np_, 0:1], scalar1=inv, scalar2=0.0, op0=ALU.mult, op1=ALU.add),
         reads=['st0'], writes=['st2'])
    S.op('dve', lambda e: e.tensor_tensor(out=st[0:np_, 3:4], in0=st[0:np_, 2:3], in1=st[0:np_, 2:3], op=ALU.mult),
         reads=['st2'], writes=['st3'])
    S.op('dve', lambda e: e.scalar_tensor_tensor(out=st[0:np_, 4:5], in0=st[0:np_, 1:2], scalar=inv, in1=st[0:np_, 3:4],
                                                  op0=ALU.mult, op1=ALU.subtract),
         reads=['st1', 'st3'], writes=['st4'])
    S.op('act', lambda e: e.activation(out=st[0:np_, 5:6], in_=st[0:np_, 4:5], func=AF.Sqrt, bias=EPS, scale=1.0),
         reads=['st4'], writes=['st5'])
    S.op('dve', lambda e: e.reciprocal(out=st[0:np_, 5:6], in_=st[0:np_, 5:6]), reads=['st5'], writes=['st5'])
    S.op('dve', lambda e: e.scalar_tensor_tensor(out=st[0:np_, 6:7], in0=st[0:np_, 2:3], scalar=-1.0, in1=st[0:np_, 5:6],
                                                  op0=ALU.mult, op1=ALU.mult),
         reads=['st2', 'st5'], writes=['st6'])
    S.op('act', lambda e: e.activation(out=rv, in_=rv, func=AF.Identity, bias=st[0:np_, 6:7], scale=st[0:np_, 5:6]),
         reads=['st5', 'st6', rres, 'sqb'], writes=[rres])
    S.op(eng2, lambda e: e.tensor_tensor(out=rv, in0=rv, in1=g_ap, op=ALU.mult),
         reads=[rres] + list(extra_reads), writes=[rres])
    S.op(eng2, lambda e: e.tensor_tensor(out=rv, in0=rv, in1=b_ap, op=ALU.add), reads=[rres] + list(extra_reads), writes=[rres])
    if out_dram is not None:
        S.op('sp', lambda e: e.dma_start(out=out_dram, in_=rv), reads=[rres], dma=True)


_PROG = None


def _get_prog():
    global _PROG
    if _PROG is None:
        from contextlib import ExitStack
        es = ExitStack()
        nc, S = build_program()
        esems = {e: es.enter_context(nc.semaphore("s_" + e)) for e in ENGS}
        dsems = [es.enter_context(nc.semaphore("d%d" % i)) for i in range(ND)]
        block = es.enter_context(nc.Block())
        S.emit(nc, block, esems, dsems)
        es.close()
        _PROG = nc
    return _PROG


def _fm(v, nchunk):
    return np.ascontiguousarray(v.reshape(nchunk, 128).T)


def _wfm(w):
    K, N = w.shape
    return np.ascontiguousarray(w.reshape(K // 128, 128, N).transpose(1, 0, 2))


def kernel(x_prompt, x_sample, c_prompt, c_sample, cache_kv_w128, cache_kv_w512, cache_kv_w2048,
           state_conv, w_c, b_c, w_in, b_in, conv_w, conv_b, conv_norm_g, conv_norm_b,
           w_pa, w_pb, w_o, ln_g, ln_b):
    f32 = np.float32
    x_prompt = np.asarray(x_prompt, f32)
    x_sample = np.asarray(x_sample, f32)
    nc = _get_prog()
    hidx = np.arange(1, 13, dtype=np.float64)
    slopes = 2.0 ** (-8.0 * hidx / 12.0)
    kk = np.arange(128)[:, None]
    qq = np.arange(128)[None, :]
    tb = np.full((NCORES, 128, 12, 2, 256), NEGB, f32)
    for h in range(12):
        d = GROUPS[h // 4][1]
        prev = np.where(kk >= qq, -slopes[h] * d * (qq - kk + 128), NEGB)
        cur = np.where(kk <= qq, -slopes[h] * d * (qq - kk), NEGB)
        for c in range(NCORES):
            tb[c, :, h, 0, 0:128] = prev
            tb[c, :, h, 0, 128:256] = cur
            tb[c, :, h, 1, 0:128] = prev if c > 0 else NEGB
            tb[c, :, h, 1, 128:256] = cur
    tbs = np.zeros((128, 12), f32)
    for h in range(12):
        d = GROUPS[h // 4][1]
        tbs[:, h] = -slopes[h] * d * (128 - np.arange(128))
    ident = np.eye(128, dtype=f32)
    sel = np.zeros((NS, NS, 128), f32)
    for b in range(NS):
        sel[b, b, :] = 1.0

    w_c = np.asarray(w_c, f32); w_in = np.asarray(w_in, f32)
    b_c = np.asarray(b_c, f32); b_in = np.asarray(b_in, f32)
    conv_w = np.asarray(conv_w, f32)
    shared = {
        "wc": _wfm(w_c), "bcT": _fm(b_c[0:2048], 16),
        "bg": np.ascontiguousarray(np.broadcast_to(b_c[2048:3072], (128, 1024))),
        "win": np.ascontiguousarray(w_in.reshape(8, 128, 68, 128).transpose(2, 1, 0, 3)), "bfm": _fm(b_in, 68),
        "bvbc": np.ascontiguousarray(np.broadcast_to(b_in[3072:4608], (128, 1536))),
        "binbc": np.ascontiguousarray(np.broadcast_to(b_in, (NS, 8704))),
        "cwfm": np.ascontiguousarray(conv_w.T.reshape(4, 128, 31).transpose(1, 0, 2)),
        "cbfm": _fm(np.asarray(conv_b, f32), 4), "cgfm": _fm(np.asarray(conv_norm_g, f32), 4),
        "cnbfm": _fm(np.asarray(conv_norm_b, f32), 4),
        "cwbc": np.ascontiguousarray(np.broadcast_to(conv_w, (NS, 31, 512))),
        "cvec": np.ascontiguousarray(np.broadcast_to(
            np.stack([np.asarray(conv_b, f32), np.asarray(conv_norm_g, f32), np.asarray(conv_norm_b, f32)]), (NS, 3, 512))),
        "wpa": _wfm(np.asarray(w_pa, f32)), "wpb": _wfm(np.asarray(w_pb, f32)), "wo": _wfm(np.asarray(w_o, f32)),
        "lng": np.ascontiguousarray(np.broadcast_to(np.asarray(ln_g, f32), (128, 1024))),
        "lnb": np.ascontiguousarray(np.broadcast_to(np.asarray(ln_b, f32), (128, 1024))),
        "tbs": tbs, "ident": ident, "sel": sel,
    }
    xx = x_prompt[0]
    xpad = np.concatenate([np.zeros((TOK, 1024), f32), xx], axis=0)
    cp = np.asarray(c_prompt, f32)[0]
    cs = np.asarray(c_sample, f32)
    xs2 = x_sample[:, 0, :]
    caches = [np.asarray(cache_kv_w128, f32).reshape(128, 128, 1024), np.asarray(cache_kv_w512, f32).reshape(128, 512, 1024),
              np.asarray(cache_kv_w2048, f32).reshape(128, 2048, 1024)]
    sconv = np.asarray(state_conv, f32)
    in_maps = []
    for c in range(NCORES):
        seg = xpad[c * TOK:(c + 2) * TOK]
        m = dict(shared)
        m["xT"] = np.ascontiguousarray(seg.T.reshape(8, 128, 2 * TOK).transpose(1, 0, 2))
        m["xtok"] = np.ascontiguousarray(xx[c * TOK:(c + 1) * TOK])
        sb = slice(c * NS, (c + 1) * NS)
        m["xsT"] = np.ascontiguousarray(xs2[sb].T.reshape(8, 128, NS).transpose(1, 0, 2))
        m["xstok"] = np.ascontiguousarray(xs2[sb])
        call = np.concatenate([cp[None, :], cs[sb]], axis=0)
        m["cT"] = np.ascontiguousarray(call.T.reshape(8, 128, 17).transpose(1, 0, 2))
        m["cbc"] = np.ascontiguousarray(np.broadcast_to(cp.reshape(8, 128).T[:, :, None], (128, 8, 128)))
        m["tb"] = tb[c]
        m["valid"] = np.full((128, 1), 0.0 if c == 0 else 1.0, f32)
        m["ck0"] = np.ascontiguousarray(caches[0][sb])
        m["ck1"] = np.ascontiguousarray(caches[1][sb])
        m["ck2"] = np.ascontiguousarray(caches[2][sb])
        m["sconv"] = np.ascontiguousarray(sconv[sb])
        in_maps.append(m)
    res = run_bass_kernel_spmd(nc, in_maps, core_ids=list(range(NCORES)))
    R = res.results
    y = np.concatenate([R[c]["y"] for c in range(NCORES)], axis=0)[None]
    ys = np.concatenate([R[c]["ys"] for c in range(NCORES)], axis=0)[:, None, :]
    last = R[NCORES - 1]
    kvp = []
    for g in range(3):
        k = np.asarray(last["kT%d" % g]).transpose(2, 1, 0)
        v = np.asarray(last["v%d" % g]).reshape(-1, 4, 128)
        kvp.append(np.ascontiguousarray(np.stack([k, v], axis=1))[None].astype(f32))
    convp = np.ascontiguousarray(np.asarray(last["uT"]).transpose(2, 1, 0).reshape(30, 512))[None].astype(f32)
    ksn = np.concatenate([R[c]["ksn"] for c in range(NCORES)], axis=0).reshape(128, 3, 4, 128)
    vsn = np.concatenate([R[c]["vsn"] for c in range(NCORES)], axis=0).reshape(128, 3, 4, 128)
    kvs = [np.ascontiguousarray(np.stack([ksn[:, g], vsn[:, g]], axis=1))[:, None].astype(f32) for g in range(3)]
    convs = np.concatenate([R[c]["convs"] for c in range(NCORES)], axis=0).astype(f32)
    return (y.astype(f32), ys.astype(f32), kvp[0], kvp[1], kvp[2], convp, kvs[0], kvs[1], kvs[2], convs)
```

```python
import math
import numpy as np
import concourse.bass as bass
import concourse.mybir as mybir
from concourse.bass_utils import run_bass_kernel_spmd

F32 = mybir.dt.float32
BF16 = mybir.dt.bfloat16
AF = mybir.ActivationFunctionType
ALU = mybir.AluOpType
AX = mybir.AxisListType

ENGS = ['pe', 'act', 'dve', 'pool', 'sp']
ND = 24
NCORES = 8
TOK = 2048
NS = 16
QS = 128 ** -0.5
ALPHA = 2.0 ** 0.25
EPS = 1e-5
NEGB = -30000.0
GROUPS = ((128, 1), (512, 4), (2048, 16))


def sl(start, n, step=1):
    return slice(start, start + step * (n - 1) + 1, step)


class Sched:
    def __init__(self):
        self.ops = []
        self.last_w = {}
        self.readers = {}
        self.eng_last = {e: None for e in ENGS}
        self.pending_dma = []

    def op(self, eng, fn, reads=(), writes=(), dma=False, extra_deps=()):
        idx = len(self.ops)
        deps = set(extra_deps)
        for r in reads:
            if r in self.last_w:
                deps.add(self.last_w[r])
        for r in writes:
            if r in self.last_w:
                deps.add(self.last_w[r])
            for x in self.readers.get(r, ()):
                deps.add(x)
        deps.discard(idx)
        self.ops.append(dict(eng=eng, fn=fn, deps=sorted(deps), dma=dma, sig=False))
        for r in reads:
            self.readers.setdefault(r, []).append(idx)
        for r in writes:
            self.last_w[r] = idx
            self.readers[r] = []
        self.eng_last[eng] = idx
        if dma:
            self.pending_dma.append(idx)
        return idx

    def barrier(self):
        deps = [v for v in self.eng_last.values() if v is not None] + list(self.pending_dma)
        b = self.op('sp', lambda e: e.nop(), extra_deps=deps)
        for e in ENGS:
            if e != 'sp':
                self.op(e, lambda eng: eng.nop(), extra_deps=[b])
        self.pending_dma = []
        self.last_w = {}
        self.readers = {}

    def _needs_wait(self, op, dop):
        if dop['dma']:
            return True
        if dop['eng'] == op['eng']:
            return op['eng'] != 'pe'
        return True

    def emit(self, nc, block, esems, dsems):
        ops = self.ops
        for op in ops:
            for d in op['deps']:
                if self._needs_wait(op, ops[d]):
                    ops[d]['sig'] = True
        cnt = {e: 0 for e in ENGS}
        ndma = 0
        for op in ops:
            if op['dma']:
                op['dk'] = ndma
                ndma += 1
            elif op['sig']:
                cnt[op['eng']] += 1
                op['count'] = cnt[op['eng']]
        per_eng = {e: [] for e in ENGS}
        known = {}
        knownd = {}
        for op in ops:
            E = op['eng']
            waits = []
            if op['dma'] and op['dk'] >= ND:
                s = op['dk'] % ND
                v = 16 * (op['dk'] // ND)
                if knownd.get((E, s), 0) < v:
                    waits.append((dsems[s], v))
                    knownd[(E, s)] = v
            for d in op['deps']:
                dop = ops[d]
                if not self._needs_wait(op, dop):
                    continue
                if dop['dma']:
                    s = dop['dk'] % ND
                    v = 16 * (dop['dk'] // ND + 1)
                    if knownd.get((E, s), 0) < v:
                        waits.append((dsems[s], v))
                        knownd[(E, s)] = v
                else:
                    Fe = dop['eng']
                    v = dop['count']
                    if known.get((E, Fe), 0) < v:
                        known[(E, Fe)] = v
                        waits.append((esems[Fe], v))
            mw = {}
            for s, v in waits:
                k = id(s)
                if k not in mw or mw[k][1] < v:
                    mw[k] = (s, v)
            op['waits'] = list(mw.values())
            per_eng[E].append(op)

        def mk(E):
            def run(engh):
                for op in per_eng[E]:
                    for s, v in op['waits']:
                        engh.wait_ge(s, v)
                    ins = op['fn'](engh)
                    if op['dma']:
                        ins.then_inc(dsems[op['dk'] % ND], 16)
                    elif op['sig']:
                        ins.then_inc(esems[E], 1)
            return run

        block.tensor(mk('pe'))
        block.scalar(mk('act'))
        block.vector(mk('dve'))
        block.gpsimd(mk('pool'))
        block.sync(mk('sp'))


class Alloc:
    def __init__(self, nc):
        self.nc = nc
        self.top = 16512
        self.lim = 16481 + 212863
        self.n = 0

    def __call__(self, name, shape, dt):
        size = 1
        for s in shape[1:]:
            size *= s
        size *= 4 if dt == F32 else 2
        size = (size + 63) // 64 * 64
        off = self.top
        self.top += size
        assert self.top <= self.lim, (name, self.top, self.lim)
        self.n += 1
        return self.nc.alloc_sbuf_tensor_at("%s_%d" % (name, self.n), list(shape), dt, offset=off)


class Ring:
    def __init__(self, name, bufs, base=0, names=None):
        self.name = name
        self.bufs = bufs
        self.i = 0
        self.base = base
        self.names = names

    def next(self):
        k = self.i % len(self.bufs)
        self.i += 1
        if self.names is not None:
            return self.bufs[k], self.names[k]
        return self.bufs[k], "%s%d" % (self.name, k + self.base)


def build_program():
    nc = bass.Bass("TRN2", target_bir_lowering=False)

    def din(name, shape):
        return nc.dram_tensor(name, list(shape), F32, kind="ExternalInput").ap()

    def dout(name, shape):
        return nc.dram_tensor(name, list(shape), F32, kind="ExternalOutput").ap()

    xT_d = din("xT", [128, 8, 4096])
    xtok_d = din("xtok", [TOK, 1024])
    xsT_d = din("xsT", [128, 8, NS])
    xstok_d = din("xstok", [NS, 1024])
    cT_d = din("cT", [128, 8, 17])
    cbc_d = din("cbc", [128, 8, 128])
    wc_d = din("wc", [128, 8, 3072])
    bcT_d = din("bcT", [128, 16])
    bg_d = din("bg", [128, 1024])
    win_d = din("win", [68, 128, 8, 128])
    bfm_d = din("bfm", [128, 68])
    bv_d = din("bvbc", [128, 1536])
    binbc_d = din("binbc", [NS, 8704])
    cwfm_d = din("cwfm", [128, 4, 31])
    cbfm_d = din("cbfm", [128, 4])
    cgfm_d = din("cgfm", [128, 4])
    cnbfm_d = din("cnbfm", [128, 4])
    cwbc_d = din("cwbc", [NS, 31, 512])
    cvec_d = din("cvec", [NS, 3, 512])
    wpa_d = din("wpa", [128, 4, 1024])
    wpb_d = din("wpb", [128, 4, 1024])
    wo_d = din("wo", [128, 8, 1024])
    lng_d = din("lng", [128, 1024])
    lnb_d = din("lnb", [128, 1024])
    tb_d = din("tb", [128, 12, 2, 256])
    tbs_d = din("tbs", [128, 12])
    valid_d = din("valid", [128, 1])
    ident_d = din("ident", [128, 128])
    sel_d = din("sel", [NS, NS, 128])
    ck_d = [din("ck0", [NS, 128, 1024]), din("ck1", [NS, 512, 1024]), din("ck2", [NS, 2048, 1024])]
    sconv_d = din("sconv", [NS, 30, 512])

    y_d = dout("y", [TOK, 1024])
    ys_d = dout("ys", [NS, 1024])
    kTo_d = [dout("kT0", [128, 4, 128]), dout("kT1", [128, 4, 512]), dout("kT2", [128, 4, 2048])]
    vo_d = [dout("v0", [128, 512]), dout("v1", [512, 512]), dout("v2", [2048, 512])]
    uT_d = dout("uT", [128, 4, 30])
    ksn_d = dout("ksn", [NS, 1536])
    vsn_d = dout("vsn", [NS, 1536])
    convs_d = dout("convs", [NS, 30, 512])

    S = Sched()
    A = Alloc(nc)
    PS = [nc.alloc_psum_tensor("psb%d" % i, [128, 512], F32) for i in range(8)]
    PSR = ["ps%d" % i for i in range(8)]

    zs_d = nc.dram_tensor("zs_scr", [NS, 8704], F32).ap()

    def dma(out, in_):
        return lambda e: e.dma_start(out=out, in_=in_)

    def load(eng, out, in_, res):
        S.op(eng, dma(out, in_), writes=[res], dma=True)

    modT = A("modT", [128, 16, 17], F32)
    gate_bc = A("gate_bc", [128, 1024], F32)
    gate_s = A("gate_s", [128, 1024], F32)
    bfm = A("bfm", [128, 68], F32)
    bqs = A("bqs", [128, 12], F32)
    hsT = A("hsT", [128, 8, NS], BF16)
    ones_bf = A("ones_bf", [128, 128], BF16)
    onesm = A("onesm", [128, 128], BF16)
    ident = A("ident", [128, 128], F32)
    ident_bf = A("ident_bf", [128, 128], BF16)
    validt = A("valid", [128, 1], F32)
    cwfm = A("cwfm", [128, 4, 31], F32)
    cbfm = A("cbfm", [128, 4], F32)
    cgfm = A("cgfm", [128, 4], F32)
    cnbfm = A("cnbfm", [128, 4], F32)
    uhalo = A("uhalo", [128, 4, 32], BF16)
    zstr = Ring("zst", [A("zst%d" % i, [NS, 128], F32) for i in range(2)])
    mark_persist = A.top

    def piggy(chunk, w_fn, res_list, ring):
        ps, pres = ring.next()
        S.op('pe', mm8(ps[0:NS, 0:128], lambda kc: hsT[:, kc, :], w_fn), reads=res_list, writes=[pres])
        zst, zres = zstr.next()
        S.op('act', lambda e, zst=zst, ps=ps: e.activation(out=zst[:], in_=ps[0:NS, 0:128], func=AF.Identity),
             reads=[pres], writes=[zres])
        S.op('sp', dma(zs_d[:, chunk * 128:(chunk + 1) * 128], zst[:]), reads=[zres], dma=True)
    hTm = A("hTm", [128, 8, TOK], BF16)
    ogT = A("ogT", [128, 4, TOK], BF16)
    mark_C = A.top

    load('sp', bfm[:], bfm_d, 'bfm')
    load('sp', ident[:], ident_d, 'ident')
    load('sp', validt[:], valid_d, 'valid')
    load('sp', cwfm[:], cwfm_d, 'cw')
    load('sp', cbfm[:], cbfm_d, 'cw2')
    load('sp', cgfm[:], cgfm_d, 'cw3')
    load('sp', cnbfm[:], cnbfm_d, 'cw4')
    S.op('dve', lambda e: e.tensor_scalar(out=bqs[:], in0=bfm[:, 0:12], scalar1=QS, scalar2=0.0, op0=ALU.mult, op1=ALU.add),
         reads=['bfm'], writes=['bqs'])
    S.op('pool', lambda e: e.memset(ones_bf[:], 1.0), writes=['ones'])
    S.op('pool', lambda e: e.memset(onesm[:], 1.0 / 512.0), writes=['onesm'])
    S.op('pool', lambda e: e.tensor_copy(out=ident_bf[:], in_=ident[:]), reads=['ident'], writes=['identbf'])

    hTh = A("hTh", [128, 8, TOK], BF16)
    mark_B = A.top
    cT_sb = A("cT", [128, 8, 17], F32)
    cbc_sb = A("cbc", [128, 8, 128], F32)
    bcT_sb = A("bcT", [128, 16], F32)
    bg_sb = A("bg", [128, 1024], F32)
    wcr = Ring("wc", [A("wc%d" % i, [128, 8, 512], BF16) for i in range(3)])
    cT_bf = A("cT_bf", [128, 8, 17], BF16)
    cbc_bf = A("cbc_bf", [128, 8, 128], BF16)
    load('sp', cT_sb[:], cT_d, 'cT')
    load('sp', cbc_sb[:], cbc_d, 'cbc')
    load('sp', bcT_sb[:], bcT_d, 'bcT')
    load('sp', bg_sb[:], bg_d, 'bg')
    S.op('dve', lambda e: e.tensor_copy(out=cT_bf[:], in_=cT_sb[:]), reads=['cT'], writes=['cTb'])
    S.op('dve', lambda e: e.tensor_copy(out=cbc_bf[:], in_=cbc_sb[:]), reads=['cbc'], writes=['cbcb'])
    zr = Ring("ps", PS[0:2])

    def mm8(out, lhs_fn, rhs_fn):
        def fn(pe):
            ins = None
            for kc in range(8):
                ins = pe.matmul(out, lhsT=lhs_fn(kc), rhs=rhs_fn(kc), start=(kc == 0), stop=(kc == 7))
            return ins
        return fn

    for blk in range(6):
        buf, res = wcr.next()
        load('pool', buf[:], wc_d[:, :, blk * 512:(blk + 1) * 512], res)
        if blk < 4:
            for sub in range(4):
                cc = blk * 4 + sub
                ps, pres = zr.next()
                S.op('pe', mm8(ps[:, 0:17], lambda kc, b=buf, s=sub: b[:, kc, s * 128:(s + 1) * 128],
                               lambda kc: cT_bf[:, kc, :]), reads=[res, 'cTb'], writes=[pres])
                S.op('dve', lambda e, ps=ps, cc=cc: e.tensor_scalar(
                    out=modT[:, cc, :], in0=ps[:, 0:17], scalar1=bcT_sb[:, cc:cc + 1],
                    scalar2=(1.0 if cc >= 8 else 0.0), op0=ALU.add, op1=ALU.add),
                    reads=[pres, 'bcT'], writes=['modT%d' % cc])
        else:
            half = blk - 4
            hs = slice(half * 512, (half + 1) * 512)
            ps, pres = zr.next()
            S.op('pe', mm8(ps[:, :], lambda kc: cbc_bf[:, kc, :], lambda kc, b=buf: b[:, kc, :]),
                 reads=[res, 'cbcb'], writes=[pres])
            S.op('dve', lambda e, ps=ps, hs=hs: e.tensor_tensor(out=gate_bc[:, hs], in0=ps[:, :], in1=bg_sb[:, hs], op=ALU.add),
                 reads=[pres, 'bg'], writes=['gate%d' % half])
            ps, pres = zr.next()
            S.op('pe', mm8(ps[0:NS, :], lambda kc: cT_bf[:, kc, 1:17], lambda kc, b=buf: b[:, kc, :]),
                 reads=[res, 'cTb'], writes=[pres])
            S.op('dve', lambda e, ps=ps, hs=hs: e.tensor_tensor(out=gate_s[0:NS, hs], in0=ps[0:NS, :], in1=bg_sb[0:NS, hs], op=ALU.add),
                 reads=[pres, 'bg'], writes=['gates%d' % half])

    xtr = Ring("xt", [A("xt%d" % i, [128, 8, 512], F32) for i in range(4)])
    xsT_sb = A("xsT", [128, 8, NS], F32)
    xs_tmp = A("xs_tmp", [128, 8, NS], F32)
    load('sp', xsT_sb[:], xsT_d, 'xsT')
    for tt in range(8):
        buf, res = xtr.next()
        load('sp', buf[:], xT_d[:, :, tt * 512:(tt + 1) * 512], res)
        for kc in range(8):
            dst = hTh[:, kc, tt * 512:(tt + 1) * 512] if tt < 4 else hTm[:, kc, (tt - 4) * 512:(tt - 3) * 512]
            if kc % 2 == 0:
                S.op('act', lambda e, dst=dst, buf=buf, kc=kc: e.activation(
                    out=dst, in_=buf[:, kc, :], func=AF.Identity, bias=modT[:, kc, 0:1], scale=modT[:, 8 + kc, 0:1]),
                    reads=[res, 'modT%d' % kc, 'modT%d' % (8 + kc)], writes=['hT%d_%d' % (tt, kc)])
            else:
                S.op('dve', lambda e, dst=dst, buf=buf, kc=kc: e.tensor_scalar(
                    out=dst, in0=buf[:, kc, :], scalar1=modT[:, 8 + kc, 0:1], scalar2=modT[:, kc, 0:1],
                    op0=ALU.mult, op1=ALU.add),
                    reads=[res, 'modT%d' % kc, 'modT%d' % (8 + kc)], writes=['hT%d_%d' % (tt, kc)])
    S.op('dve', lambda e: e.tensor_tensor(out=xs_tmp[:], in0=xsT_sb[:], in1=modT[:, 8:16, 1:17], op=ALU.mult),
         reads=['xsT'] + ['modT%d' % c for c in range(16)], writes=['xs_tmp'])
    S.op('dve', lambda e: e.tensor_tensor(out=hsT[:], in0=xs_tmp[:], in1=modT[:, 0:8, 1:17], op=ALU.add),
         reads=['xs_tmp'], writes=['hsT'])
    S.barrier()
    A.top = mark_B

    def hsl(kc, tok0, n, step=1):
        if tok0 < TOK:
            return hTh[:, kc, sl(tok0, n, step)]
        return hTm[:, kc, sl(tok0 - TOK, n, step)]

    hwr = Ring("hw", [A("hw%d" % i, [128, 8, 128], BF16) for i in range(4)])
    hsg = Ring("hsg", [A("hsg%d" % i, [128, 64], F32) for i in range(2)])
    zrh = Ring("ps", PS[0:4])
    for cc in range(4):
        ba, ra = hwr.next()
        load('pool', ba[:], win_d[40 + cc], ra)
        bgl, rg = hwr.next()
        load('pool', bgl[:], win_d[44 + cc], rg)
        psa, pra = zrh.next()
        S.op('pe', mm8(psa[:, 0:30], lambda kc, ba=ba: ba[:, kc, :], lambda kc: hTh[:, kc, TOK - 30:TOK]), reads=[ra], writes=[pra])
        psg, prg = zrh.next()
        S.op('pe', mm8(psg[:, 0:30], lambda kc, bgl=bgl: bgl[:, kc, :], lambda kc: hTh[:, kc, TOK - 30:TOK]), reads=[rg], writes=[prg])
        sg, sres = hsg.next()
        S.op('act', lambda e, sg=sg, psg=psg, cc=cc: e.activation(out=sg[:, 0:30], in_=psg[:, 0:30], func=AF.Sigmoid,
                                                                 bias=bfm[:, 44 + cc:45 + cc], scale=1.0),
             reads=[prg], writes=[sres])
        S.op('dve', lambda e, sg=sg, psa=psa, cc=cc: e.scalar_tensor_tensor(
            out=sg[:, 32:62], in0=psa[:, 0:30], scalar=bfm[:, 40 + cc:41 + cc], in1=sg[:, 0:30], op0=ALU.add, op1=ALU.mult),
            reads=[pra, sres], writes=[sres])
        S.op('dve', lambda e, sg=sg, cc=cc: e.tensor_scalar(
            out=uhalo[:, cc, 0:30], in0=sg[:, 32:62], scalar1=validt[:, 0:1], scalar2=0.0, op0=ALU.mult, op1=ALU.add),
            reads=[sres], writes=['uhalo%d' % cc])
    S.barrier()
    A.top = mark_B

    vtiles = [[], [], []]
    for a in range(17):
        vtiles[0].append((a, 1920 + 128 * a, 1, 0 if a == 16 else None, 1))
    for r in range(4):
        for n_ in range(5):
            vtiles[1].append((r * 5 + n_, 1536 + 512 * n_ + r, 4, r if n_ == 4 else None, 4))
    for r in range(16):
        for b in range(2):
            vtiles[2].append((r * 2 + b, 2048 * b + r, 16, r if b == 1 else None, 16))
    nvt = [17, 20, 32]
    Vt = [A("V%d" % g, [128, nvt[g], 256], BF16) for g in range(3)]
    mark_V = A.top

    for pp in range(2):
        A.top = mark_V
        wv = [A("wv%d" % g, [128, 2, 8, 128], BF16) for g in range(3)]
        bv_sb = A("bv", [128, 1536], F32)
        vstr = Ring("vst", [A("vst0", [128, 256], F32), A("vst1", [128, 256], F32)])
        load('sp', bv_sb[:], bv_d, 'bv')
        for g in range(3):
            c0 = 3072 + g * 512 + pp * 256
            for ci in range(2):
                S.op('pool', dma(wv[g][:, ci, :, :], win_d[c0 // 128 + ci]), writes=['wv%d_%d' % (g, ci)], dma=True)
        zr4 = Ring("ps", PS[0:4])
        for g in range(3):
            bsl = slice(g * 512 + pp * 256, g * 512 + pp * 256 + 256)
            for (vidx, tok0, step, row0, rstep) in vtiles[g]:
                ps, pres = zr4.next()
                S.op('pe', mm8(ps[:, 0:256], lambda kc, tok0=tok0, step=step: hsl(kc, tok0, 128, step),
                               lambda kc, g=g: wv[g][:, :, kc, :]), reads=['wv%d_0' % g, 'wv%d_1' % g], writes=[pres])
                if row0 is None:
                    S.op('dve', lambda e, ps=ps, g=g, vidx=vidx, bsl=bsl: e.tensor_tensor(
                        out=Vt[g][:, vidx, :], in0=ps[:, 0:256], in1=bv_sb[:, bsl], op=ALU.add),
                        reads=[pres, 'bv'], writes=['V%d_%d' % (g, vidx)])
                else:
                    vst, vres = vstr.next()
                    S.op('dve', lambda e, ps=ps, vst=vst, bsl=bsl: e.tensor_tensor(
                        out=vst[:], in0=ps[:, 0:256], in1=bv_sb[:, bsl], op=ALU.add),
                        reads=[pres, 'bv'], writes=[vres])
                    S.op('pool', lambda e, vst=vst, g=g, vidx=vidx: e.tensor_copy(out=Vt[g][:, vidx, :], in_=vst[:]),
                         reads=[vres], writes=['V%d_%d' % (g, vidx)])
                    S.op('sp', dma(vo_d[g][sl(row0, 128, rstep), pp * 256:(pp + 1) * 256], vst[:]),
                         reads=[vres], dma=True)
        for g in range(3):
            for ci in range(2):
                piggy((3072 + g * 512 + pp * 256) // 128 + ci, lambda kc, g=g, ci=ci: wv[g][:, ci, kc, :],
                      ['wv%d_%d' % (g, ci)], zr4)
        S.barrier()
        A.top = mark_V

        wr = [A("wr%d" % i, [128, 8, 128], BF16) for i in range(7)]
        qT = A("qT", [128, 3, TOK], BF16)
        kTl = [A("kT0", [128, 128 + TOK], BF16), A("kT1", [128, 512 + TOK], BF16), A("kT2", [128, 2 * TOK], BF16)]
        sga = A("sga", [128, TOK], BF16)
        kstr = Ring("kst", [A("kst0", [128, 512], F32), A("kst1", [128, 512], F32)])
        tbst = Ring("tbst", [A("tbst0", [128, 2, 256], F32), A("tbst1", [128, 2, 256], F32)])
        MTs = A("MTs", [128, 3, 2, 256], BF16)
        Er = Ring("E", [A("E%d" % i, [128, 512], BF16) for i in range(3)])
        Pr = Ring("P", [A("P%d" % i, [128, 512], BF16) for i in range(4)])
        P2 = A("P2", [128, 16, 256], BF16)
        rden = A("rden", [128, 512], F32)
        t1 = A("t1", [128, 512], F32)
        for jj in range(2):
            j = 2 * pp + jj
            cols = [g * 512 + j * 128 for g in range(3)] + [1536 + g * 512 + j * 128 for g in range(3)] + [4608 + j * 128]
            for i, c0 in enumerate(cols):
                load('pool', wr[i][:], win_d[c0 // 128], 'wr%d' % i)
            for g in range(3):
                h = 4 * g + j
                tbuf, tres = tbst.next()
                load('sp', tbuf[:], tb_d[:, h, :, :], tres)
                S.op('act', lambda e, tbuf=tbuf, g=g: e.activation(out=MTs[:, g, :, :], in_=tbuf[:], func=AF.Exp),
                     reads=[tres], writes=['MT%d' % g])
            zr2 = Ring("ps", PS[0:2])
            for g in range(3):
                ch = cols[g] // 128
                for tt in range(4):
                    ps, pres = zr2.next()
                    S.op('pe', mm8(ps[:, :], lambda kc, g=g: wr[g][:, kc, :],
                                   lambda kc, tt=tt: hTm[:, kc, tt * 512:(tt + 1) * 512]),
                         reads=['wr%d' % g], writes=[pres])
                    S.op('act', lambda e, ps=ps, g=g, tt=tt, ch=ch: e.activation(
                        out=qT[:, g, tt * 512:(tt + 1) * 512], in_=ps[:, :], func=AF.Identity,
                        bias=bqs[:, ch:ch + 1], scale=QS),
                        reads=[pres], writes=['qT%d_%d' % (g, tt)])
            kres = [[], [], []]
            for g in range(3):
                ch = cols[3 + g] // 128
                halo = [128, 512, 2048][g]
                tl = []
                if g == 0:
                    tl.append((1920, 128, 0, None))
                elif g == 1:
                    tl.append((1536, 512, 0, None))
                else:
                    for tt in range(4):
                        tl.append((tt * 512, 512, tt * 512, None))
                for tt in range(4):
                    tl.append((TOK + tt * 512, 512, halo + tt * 512, tt))
                for ti, (tok0, n, dst, mt) in enumerate(tl):
                    ps, pres = zr2.next()
                    S.op('pe', mm8(ps[:, 0:n], lambda kc, g=g: wr[3 + g][:, kc, :],
                                   lambda kc, tok0=tok0, n=n: hsl(kc, tok0, n)),
                         reads=['wr%d' % (3 + g)], writes=[pres])
                    kst, ksres = kstr.next()
                    S.op('dve', lambda e, ps=ps, kst=kst, n=n, ch=ch: e.tensor_scalar(
                        out=kst[:, 0:n], in0=ps[:, 0:n], scalar1=bfm[:, ch:ch + 1], scalar2=0.0,
                        op0=ALU.add, op1=ALU.add), reads=[pres], writes=[ksres])
                    kr = 'kT%d_%d' % (g, ti)
                    kres[g].append(kr)
                    S.op('act', lambda e, kst=kst, g=g, dst=dst, n=n: e.activation(
                        out=kTl[g][:, dst:dst + n], in_=kst[:, 0:n], func=AF.Identity),
                        reads=[ksres], writes=[kr])
                    if mt is not None:
                        if g == 2:
                            S.op('sp', dma(kTo_d[2][:, j, mt * 512:(mt + 1) * 512], kst[:, :]), reads=[ksres], dma=True)
                        elif g == 1 and mt == 3:
                            S.op('sp', dma(kTo_d[1][:, j, :], kst[:, :]), reads=[ksres], dma=True)
                        elif g == 0 and mt == 3:
                            S.op('sp', dma(kTo_d[0][:, j, :], kst[:, 384:512]), reads=[ksres], dma=True)
            ch = cols[6] // 128
            for tt in range(4):
                ps, pres = zr2.next()
                S.op('pe', mm8(ps[:, :], lambda kc: wr[6][:, kc, :],
                               lambda kc, tt=tt: hTm[:, kc, tt * 512:(tt + 1) * 512]),
                     reads=['wr6'], writes=[pres])
                S.op('act', lambda e, ps=ps, tt=tt, ch=ch: e.activation(
                    out=sga[:, tt * 512:(tt + 1) * 512], in_=ps[:, :], func=AF.Silu, bias=bfm[:, ch:ch + 1], scale=1.0),
                    reads=[pres], writes=['sga%d' % tt])
            for i in range(7):
                piggy(cols[i] // 128, lambda kc, i=i: wr[i][:, kc, :], ['wr%d' % i], zr2)
            qres = [['qT%d_%d' % (g, tt) for tt in range(4)] for g in range(3)]

            def tile_g0(t):
                return (0, kTl[0][:, 128 * t:128 * t + 128], kTl[0][:, 128 * (t + 1):128 * (t + 2)],
                        qT[:, 0, 128 * t:128 * (t + 1)], 1 if t == 0 else 0, t, t + 1,
                        slice(128 * (t % 4), 128 * (t % 4) + 128))

            def tile_g1(r, n_):
                return (1, kTl[1][:, sl(512 * n_ + r, 128, 4)], kTl[1][:, sl(512 + 512 * n_ + r, 128, 4)],
                        qT[:, 1, sl(512 * n_ + r, 128, 4)], 1 if n_ == 0 else 0, r * 5 + n_, r * 5 + n_ + 1,
                        sl(r, 128, 4))

            def tile_g2(r):
                return (2, kTl[2][:, sl(r, 128, 16)], kTl[2][:, sl(TOK + r, 128, 16)],
                        qT[:, 2, sl(r, 128, 16)], 1, 2 * r, 2 * r + 1, None)

            Sr = Ring("ps", PS[2:4], 2)

            def emit_S(pair, dests):
                ps, pres = Sr.next()

                def fn(pe, pair=pair, ps=ps):
                    ins = None
                    for ti, td in enumerate(pair):
                        off = ti * 256
                        pe.matmul(ps[:, off:off + 128], lhsT=td[1], rhs=td[3], start=True, stop=True)
                        ins = pe.matmul(ps[:, off + 128:off + 256], lhsT=td[2], rhs=td[3], start=True, stop=True)
                    return ins
                g = pair[0][0]
                S.op('pe', fn, reads=qres[g] + kres[g], writes=[pres])
                E, eres = Er.next()
                S.op('act', lambda e, E=E, ps=ps: e.activation(out=E[:, :], in_=ps[:, :], func=AF.Exp),
                     reads=[pres], writes=[eres])
                for ti, td in enumerate(pair):
                    pap, prs = dests[ti]
                    S.op('pool', lambda e, E=E, ti=ti, td=td, pap=pap: e.tensor_tensor(
                        out=pap, in0=E[:, ti * 256:(ti + 1) * 256], in1=MTs[:, td[0], td[4], :], op=ALU.mult),
                        reads=[eres, 'MT%d' % td[0]], writes=[prs])

            for u in range(8):
                pair = [tile_g2(2 * u), tile_g2(2 * u + 1)]
                emit_S(pair, [(P2[:, 2 * u, :], 'P2_%d' % (2 * u)), (P2[:, 2 * u + 1, :], 'P2_%d' % (2 * u + 1))])

            for w in range(4):
                num, nres = PS[4 + (w % 2)], PSR[4 + (w % 2)]
                den, dres = PS[6 + (w % 2)], PSR[6 + (w % 2)]
                pairs = [[tile_g0(4 * w), tile_g0(4 * w + 1)], [tile_g0(4 * w + 2), tile_g0(4 * w + 3)],
                         [tile_g1(0, w), tile_g1(1, w)], [tile_g1(2, w), tile_g1(3, w)]]
                pbufs = []
                state = {'first': True}

                def emit_PV(pair, pb, pres_, num=num, den=den, nres=nres, dres=dres, state=state):
                    first = state['first']
                    state['first'] = False

                    def fn(pe, pair=pair, pb=pb, first=first, jj=jj):
                        ins = None
                        st = first
                        for ti, td in enumerate(pair):
                            g = td[0]
                            off = ti * 256
                            oc = td[7]
                            pe.matmul(num[:, oc], lhsT=Vt[g][:, td[5], jj * 128:(jj + 1) * 128], rhs=pb[:, off:off + 128],
                                      start=st, stop=False, skip_group_check=True)
                            pe.matmul(num[:, oc], lhsT=Vt[g][:, td[6], jj * 128:(jj + 1) * 128], rhs=pb[:, off + 128:off + 256],
                                      start=False, stop=False, skip_group_check=True)
                            pe.matmul(den[:, oc], lhsT=ones_bf[:], rhs=pb[:, off:off + 128],
                                      start=st, stop=False, skip_group_check=True)
                            ins = pe.matmul(den[:, oc], lhsT=ones_bf[:], rhs=pb[:, off + 128:off + 256],
                                            start=False, stop=False, skip_group_check=True)
                            st = False
                        return ins
                    S.op('pe', fn, reads=[pres_ + 'a', pres_ + 'b'], writes=[nres, dres])

                prev = None
                for pi, pair in enumerate(pairs):
                    pb, pres_ = Pr.next()
                    emit_S(pair, [(pb[:, 0:256], pres_ + 'a'), (pb[:, 256:512], pres_ + 'b')])
                    if prev is not None:
                        emit_PV(*prev)
                    prev = (pair, pb, pres_)
                emit_PV(*prev)

                def fn2(pe, w=w, num=num, den=den, jj=jj):
                    ins = None
                    for r in range(16):
                        oc = sl(r, 32, 16)
                        for blk in range(2):
                            rhs = P2[:, r, blk * 128 + 32 * w:blk * 128 + 32 * w + 32]
                            pe.matmul(num[:, oc], lhsT=Vt[2][:, 2 * r + blk, jj * 128:(jj + 1) * 128], rhs=rhs,
                                      start=False, stop=False, skip_group_check=True)
                            ins = pe.matmul(den[:, oc], lhsT=ones_bf[:], rhs=rhs,
                                            start=False, stop=(r == 15 and blk == 1), skip_group_check=True)
                    return ins
                S.op('pe', fn2, reads=['P2_%d' % r for r in range(16)], writes=[nres, dres])
                ws = slice(w * 512, (w + 1) * 512)
                S.op('dve', lambda e, den=den: e.reciprocal(out=rden[:], in_=den[:, :]), reads=[dres], writes=['rden'])
                S.op('dve', lambda e, num=num: e.tensor_tensor(out=t1[:], in0=num[:, :], in1=rden[:], op=ALU.mult),
                     reads=[nres, 'rden'], writes=['t1'])
                S.op('dve', lambda e, ws=ws, j=j: e.tensor_tensor(out=ogT[:, j, ws], in0=t1[:], in1=sga[:, ws], op=ALU.mult),
                     reads=['t1', 'sga%d' % w], writes=['og%d_%d' % (j, w)])
        S.barrier()
    A.top = mark_C

    wpa = A("wpa", [128, 4, 1024], BF16)
    wpb = A("wpb", [128, 4, 1024], BF16)
    wo = A("wo", [128, 8, 1024], BF16)
    lng = A("lng", [128, 1024], F32)
    lnb = A("lnb", [128, 1024], F32)
    mark_P4w = A.top
    load('pool', wpa[:], wpa_d, 'wpa')
    load('pool', wpb[:], wpb_d, 'wpb')
    load('pool', wo[:], wo_d, 'wo')
    load('sp', lng[:], lng_d, 'lng')
    load('sp', lnb[:], lnb_d, 'lnb')
    wrr = Ring("wr", [A("wr%d" % i, [128, 8, 128], BF16) for i in range(7)])
    diagr = Ring("dg", [A("dg%d" % i, [128, 31, 128], BF16) for i in range(2)])
    uTr = [A("uT%d" % i, [128, 4, 542], BF16) for i in range(2)]
    sgr = Ring("sg", [A("sg%d" % i, [128, 512], F32) for i in range(2)])
    ybf = A("ybf", [128, 4, 512], BF16)
    ysq = A("ysq", [128, 4, 512], BF16)
    mean = A("mean", [128, 512], F32)
    var = A("var", [128, 512], F32)
    rstd = A("rstd", [128, 512], F32)
    tnr = Ring("tn", [A("tn%d" % i, [128, 512], F32) for i in range(2)])
    cnr = Ring("cn", [A("cn%d" % i, [128, 512], BF16) for i in range(2)])
    sgbt = A("sgbt", [128, 4, 512], BF16)
    cg = A("cg", [128, 4, 512], BF16)
    smar = Ring("sma", [A("sma%d" % i, [128, 512], F32) for i in range(2)])
    tbr = Ring("tbb", [A("tbb%d" % i, [128, 512], F32) for i in range(2)])
    mT = A("mT", [128, 8, 512], BF16)
    rr = Ring("r", [A("r%d" % i, [128, 1024], F32) for i in range(2)])
    xkr = Ring("xk", [A("xk%d" % i, [128, 1024], F32) for i in range(2)])
    sqb = A("sqb", [128, 1024], F32)
    st = A("st", [128, 8], F32)
    ulast = A("ulast", [128, 4, 30], F32)
    zr4 = Ring("ps", [PS[i] for i in (0, 1, 2, 3, 6, 7)], names=["ps%d" % i for i in (0, 1, 2, 3, 6, 7)])
    cvr = Ring("ps", PS[4:6], 4)

    def wload(c0):
        buf, res = wrr.next()
        load('pool', buf[:], win_d[c0 // 128], res)
        wl_chunk[id(buf)] = c0 // 128
        return buf, res

    wl_chunk = {}

    def zmm(buf, res, w):
        ps, pres = zr4.next()
        t0 = TOK + w * 512
        S.op('pe', mm8(ps[:, :], lambda kc, buf=buf: buf[:, kc, :],
                       lambda kc, t0=t0: hsl(kc, t0, 512)), reads=[res], writes=[pres])
        if w == 0:
            piggy(wl_chunk[id(buf)], lambda kc, buf=buf: buf[:, kc, :], [res], zr4)
        return ps, pres

    def final_stage(w):
        xks = {}

        def xload(ts):
            xk, xres = xkr.next()
            r0 = w * 512 + ts * 128
            load('sp', xk[:], xtok_d[r0:r0 + 128, :], xres)
            xks[ts] = (xk, xres)
        xload(0)
        xload(1)
        for ts in range(4):
            r, rres = rr.next()
            xk, xres = xks[ts]
            row0 = w * 512 + ts * 128
            for half in range(2):
                hs = slice(half * 512, (half + 1) * 512)
                psy, pry = zr4.next()

                def fny(pe, psy=psy, ts=ts, hs=hs):
                    ins = None
                    for oc in range(8):
                        ins = pe.matmul(psy[:, :], lhsT=mT[:, oc, ts * 128:(ts + 1) * 128], rhs=wo[:, oc, hs],
                                        start=(oc == 0), stop=(oc == 7))
                    return ins
                S.op('pe', fny, reads=['wo'] + ['mT%d' % c for c in range(8)], writes=[pry])
                S.op('dve', lambda e, psy=psy, r=r, hs=hs, xk=xk: e.scalar_tensor_tensor(
                    out=r[:, hs], in0=xk[:, hs], scalar=ALPHA, in1=psy[:, :], op0=ALU.mult, op1=ALU.add),
                    reads=[pry, xres], writes=[rres])
            emit_ln(S, r, rres, 128, sqb, st, lng[:, :], lnb[:, :], y_d[row0:row0 + 128, :], ['lng', 'lnb'])
            if ts + 2 < 4:
                xload(ts + 2)

    for w in range(4):
        uT = uTr[w % 2]
        uTn = uTr[(w + 1) % 2]
        up = w % 2
        if w == 0:
            for cc in range(4):
                S.op('pool', lambda e, cc=cc: e.tensor_copy(out=uTr[0][:, cc, 0:30], in_=uhalo[:, cc, 0:30]),
                     writes=['uh0_%d' % cc])
        for cc in range(4):
            ba, ra = wload(5120 + cc * 128)
            bgl, rg = wload(5632 + cc * 128)
            psa, pra = zmm(ba, ra, w)
            psg, prg = zmm(bgl, rg, w)
            sg, sres = sgr.next()
            S.op('act', lambda e, sg=sg, psg=psg, cc=cc: e.activation(out=sg[:, :], in_=psg[:, :], func=AF.Sigmoid,
                                                                     bias=bfm[:, 44 + cc:45 + cc], scale=1.0),
                 reads=[prg], writes=[sres])
            S.op('dve', lambda e, sg=sg, psa=psa, cc=cc, uT=uT: e.scalar_tensor_tensor(
                out=uT[:, cc, 30:542], in0=psa[:, :], scalar=bfm[:, 40 + cc:41 + cc], in1=sg[:, :], op0=ALU.add, op1=ALU.mult),
                reads=[pra, sres, 'uh%d_%d' % (up, cc)], writes=['u%d_%d' % (up, cc)])
            if w == 3:
                S.op('dve', lambda e, sg=sg, psa=psa, cc=cc: e.scalar_tensor_tensor(
                    out=ulast[:, cc, :], in0=psa[:, 482:512], scalar=bfm[:, 40 + cc:41 + cc], in1=sg[:, 482:512],
                    op0=ALU.add, op1=ALU.mult), reads=[pra, sres], writes=['ulast%d' % cc])
            else:
                S.op('dve', lambda e, uT=uT, uTn=uTn, cc=cc: e.tensor_copy(out=uTn[:, cc, 0:30], in_=uT[:, cc, 512:542]),
                     reads=['u%d_%d' % (up, cc)], writes=['uh%d_%d' % (1 - up, cc)])
        for cc in range(4):
            dg, dgres = diagr.next()
            S.op('pool', lambda e, dg=dg, cc=cc: e.tensor_tensor(
                out=dg[:, :, :], in0=ident_bf[:, :].unsqueeze(1).broadcast_to([128, 31, 128]),
                in1=cwfm[:, cc, :].unsqueeze(2).broadcast_to([128, 31, 128]), op=ALU.mult), writes=[dgres])
            psc, prc = cvr.next()

            def fnc(pe, psc=psc, dg=dg, uT=uT, cc=cc):
                ins = None
                for jt in range(31):
                    ins = pe.matmul(psc[:, :], lhsT=dg[:, jt, :], rhs=uT[:, cc, jt:jt + 512], start=(jt == 0), stop=(jt == 30))
                return ins
            S.op('pe', fnc, reads=[dgres, 'u%d_%d' % (up, cc), 'uh%d_%d' % (up, cc)],
                 writes=[prc])
            S.op('act', lambda e, psc=psc, cc=cc: e.activation(out=ybf[:, cc, :], in_=psc[:, :], func=AF.Identity,
                                                               bias=cbfm[:, cc:cc + 1], scale=1.0),
                 reads=[prc], writes=['ybf%d' % cc])
            S.op('act', lambda e, psc=psc, cc=cc: e.activation(out=ysq[:, cc, :], in_=psc[:, :], func=AF.Square,
                                                               bias=cbfm[:, cc:cc + 1], scale=1.0),
                 reads=[prc], writes=['ysq%d' % cc])
        if w == 0:
            S.op('dve', lambda e: e.tensor_tensor(out=wo[:, :, :], in0=wo[:, :, :],
                                                  in1=gate_bc[:, :].unsqueeze(1).broadcast_to([128, 8, 1024]), op=ALU.mult),
                 reads=['wo'], writes=['wo'])
        if w > 0:
            final_stage(w - 1)
        psm, prm = zr4.next()

        def fnm(pe, psm=psm):
            ins = None
            for cc in range(4):
                ins = pe.matmul(psm[:, :], lhsT=onesm[:], rhs=ybf[:, cc, :], start=(cc == 0), stop=(cc == 3))
            return ins
        S.op('pe', fnm, reads=['ybf%d' % c for c in range(4)], writes=[prm])
        psq, prq = zr4.next()

        def fnq(pe, psq=psq):
            ins = None
            for cc in range(4):
                ins = pe.matmul(psq[:, :], lhsT=onesm[:], rhs=ysq[:, cc, :], start=(cc == 0), stop=(cc == 3))
            return ins
        S.op('pe', fnq, reads=['ysq%d' % c for c in range(4)], writes=[prq])
        S.op('dve', lambda e, psm=psm: e.tensor_copy(out=mean[:], in_=psm[:, :]), reads=[prm], writes=['mean'])
        S.op('dve', lambda e: e.tensor_tensor(out=var[:], in0=mean[:], in1=mean[:], op=ALU.mult), reads=['mean'], writes=['var'])
        S.op('dve', lambda e, psq=psq: e.tensor_tensor(out=var[:], in0=psq[:, :], in1=var[:], op=ALU.subtract),
             reads=[prq, 'var'], writes=['var'])
        S.op('act', lambda e: e.activation(out=rstd[:], in_=var[:], func=AF.Sqrt, bias=EPS, scale=1.0),
             reads=['var'], writes=['rstd'])
        S.op('dve', lambda e: e.reciprocal(out=rstd[:], in_=rstd[:]), reads=['rstd'], writes=['rstd'])
        for cc in range(4):
            bgb, rgb = wload(6144 + cc * 128)
            psb_, prb = zmm(bgb, rgb, w)
            S.op('act', lambda e, psb_=psb_, cc=cc: e.activation(out=sgbt[:, cc, :], in_=psb_[:, :], func=AF.Silu,
                                                                bias=bfm[:, 48 + cc:49 + cc], scale=1.0),
                 reads=[prb], writes=['sgb%d' % cc])
        for oc in range(8):
            psa, pra = zr4.next()

            def fna(pe, psa=psa, oc=oc, w=w):
                ins = None
                for cc in range(4):
                    ins = pe.matmul(psa[:, :], lhsT=wpa[:, cc, oc * 128:(oc + 1) * 128], rhs=ogT[:, cc, w * 512:(w + 1) * 512],
                                    start=(cc == 0), stop=(cc == 3))
                return ins
            S.op('pe', fna, reads=['wpa'], writes=[pra])
            bma, rma = wload(6656 + oc * 128)
            psma, prma = zmm(bma, rma, w)
            sma, smres = smar.next()
            S.op('act', lambda e, sma=sma, psma=psma, oc=oc: e.activation(out=sma[:], in_=psma[:, :], func=AF.Sigmoid,
                                                                         bias=bfm[:, 52 + oc:53 + oc], scale=1.0),
                 reads=[prma], writes=[smres])
            S.op('dve', lambda e, psa=psa, sma=sma, oc=oc: e.tensor_tensor(out=mT[:, oc, :], in0=psa[:, :], in1=sma[:], op=ALU.mult),
                 reads=[pra, smres], writes=['mT%d' % oc])
        for cc in range(4):
            tn, tres = tnr.next()
            S.op('dve', lambda e, tn=tn, cc=cc: e.tensor_tensor(out=tn[:], in0=ybf[:, cc, :], in1=mean[:], op=ALU.subtract),
                 reads=['ybf%d' % cc, 'mean'], writes=[tres])
            S.op('dve', lambda e, tn=tn: e.tensor_tensor(out=tn[:], in0=tn[:], in1=rstd[:], op=ALU.mult),
                 reads=[tres, 'rstd'], writes=[tres])
            cn, cres = cnr.next()
            S.op('act', lambda e, tn=tn, cn=cn, cc=cc: e.activation(out=cn[:], in_=tn[:], func=AF.Silu,
                                                                   bias=cnbfm[:, cc:cc + 1], scale=cgfm[:, cc:cc + 1]),
                 reads=[tres], writes=[cres])
            S.op('dve', lambda e, cn=cn, cc=cc: e.tensor_tensor(out=cg[:, cc, :], in0=cn[:], in1=sgbt[:, cc, :], op=ALU.mult),
                 reads=[cres, 'sgb%d' % cc], writes=['cg%d' % cc])
        for oc in range(8):
            psb_, prb = zr4.next()

            def fnb(pe, psb_=psb_, oc=oc):
                ins = None
                for cc in range(4):
                    ins = pe.matmul(psb_[:, :], lhsT=wpb[:, cc, oc * 128:(oc + 1) * 128], rhs=cg[:, cc, :],
                                    start=(cc == 0), stop=(cc == 3))
                return ins
            bmb, rmb = wload(7680 + oc * 128)
            psmb, prmb = zmm(bmb, rmb, w)
            smb, sbres2 = smar.next()
            S.op('act', lambda e, smb=smb, psmb=psmb, oc=oc: e.activation(out=smb[:], in_=psmb[:, :], func=AF.Sigmoid,
                                                                         bias=bfm[:, 60 + oc:61 + oc], scale=1.0),
                 reads=[prmb], writes=[sbres2])
            S.op('pe', fnb, reads=['wpb'] + ['cg%d' % c for c in range(4)], writes=[prb])
            tb_, tbres = tbr.next()
            S.op('dve', lambda e, psb_=psb_, smb=smb, tb_=tb_: e.tensor_tensor(out=tb_[:], in0=psb_[:, :], in1=smb[:], op=ALU.mult),
                 reads=[prb, sbres2], writes=[tbres])
            S.op('dve', lambda e, oc=oc, tb_=tb_: e.tensor_tensor(out=mT[:, oc, :], in0=mT[:, oc, :], in1=tb_[:], op=ALU.add),
                 reads=['mT%d' % oc, tbres], writes=['mT%d' % oc])
    final_stage(3)
    S.op('sp', dma(uT_d, ulast[:]), reads=['ulast%d' % c for c in range(4)], dma=True)
    S.barrier()
    A.top = mark_persist

    zs = A("zs", [NS, 8704], F32)
    xs = A("xs", [NS, 1024], F32)
    cvec = A("cvec", [NS, 3, 512], F32)
    assert A.top <= mark_C
    A.top = mark_P4w
    snew = A("snew", [NS, 12], F32)
    acc = A("acc", [NS, 3, 512], F32)
    dens = A("dens", [NS, 12], F32)
    osb = A("osb", [NS, 512], F32)
    us = A("us", [NS, 512], F32)
    ycs = A("ycs", [NS, 512], F32)
    ptmp = A("ptmp", [NS, 512], F32)
    st5 = A("st5", [NS, 8], F32)
    sqb5 = A("sqb5", [NS, 1024], F32)
    wpa5, wpb5, wo5, lng5, lnb5 = wpa, wpb, wo, lng, lnb
    load('pool', wo5[:], wo_d, 'wo5')
    S.op('sp', lambda e: e.nop(), writes=['wpa5', 'wpb5', 'lng5', 'lnb5'])
    mark5 = A.top
    load('sp', xs[:], xstok_d, 'xs')
    load('sp', cvec[:], cvec_d, 'cvec')
    S.op('sp', dma(convs_d[:, 0:29, :], sconv_d[:, 1:30, :]), dma=True)

    binbc = A("binbc", [NS, 8704], F32)
    tmp16 = A("tmp16", [NS, 1536], F32)
    load('sp', zs[:], zs_d, 'zsraw')
    load('sp', binbc[:], binbc_d, 'binbc')
    for q4 in range(4):
        cs = slice(q4 * 2176, (q4 + 1) * 2176)
        S.op('dve', lambda e, cs=cs: e.tensor_tensor(out=zs[:, cs], in0=zs[:, cs], in1=binbc[:, cs], op=ALU.add),
             reads=['zsraw', 'binbc'], writes=['zs%d' % q4])
    ZALL = ['zs%d' % b for b in range(4)]
    S.op('sp', dma(ksn_d, zs[:, 1536:3072]), reads=ZALL, dma=True)
    S.op('sp', dma(vsn_d, zs[:, 3072:4608]), reads=ZALL, dma=True)
    S.op('dve', lambda e: e.tensor_scalar(out=zs[:, 0:1536], in0=zs[:, 0:1536], scalar1=QS, scalar2=0.0, op0=ALU.mult, op1=ALU.add),
         reads=ZALL, writes=['qs'])
    S.op('dve', lambda e: e.tensor_tensor(out=tmp16[:], in0=zs[:, 0:1536], in1=zs[:, 1536:3072], op=ALU.mult),
         reads=['qs'] + ZALL, writes=['tmp16'])
    S.op('dve', lambda e: e.reduce_sum(out=snew[:], in_=tmp16[:].rearrange("p (h d) -> p h d", d=128), axis=AX.X),
         reads=['tmp16'], writes=['snew'])
    S.op('act', lambda e: e.activation(out=snew[:], in_=snew[:], func=AF.Exp), reads=['snew'], writes=['pnew'])
    S.barrier()
    A.top = mark5

    sel = A("sel", [NS, NS, 128], F32)
    selb = A("selb", [NS, NS, 128], BF16)
    qsb = A("qsb", [NS, 1536], BF16)
    tbs = A("tbs", [128, 12], F32)
    pzz = A("pzz", [128, 12, NS, NS], BF16)
    onesf = A("onesf", [128, 1], BF16)
    kvbr = Ring("kvb", [A("kvb%d" % i, [128, 512], BF16) for i in range(6)])
    kvr = Ring("kv", [A("kv%d" % i, [128, 1024], F32) for i in range(7)])
    prodr = Ring("prod", [A("prod%d" % i, [128, 512], F32) for i in range(3)])
    scr = Ring("sc", [A("sc%d" % i, [128, 4], F32) for i in range(6)])
    load('sp', sel[:], sel_d, 'sel')
    load('sp', tbs[:], tbs_d, 'tbs')
    S.op('pool', lambda e: e.memset(pzz[:, 0:4], 0.0), writes=['pzz'])
    S.op('dve', lambda e: e.memset(pzz[:, 4:8], 0.0), writes=['pzz1'])
    S.op('dve', lambda e: e.memset(pzz[:, 8:12], 0.0), writes=['pzz2'])
    S.op('pool', lambda e: e.memset(onesf[:], 1.0), writes=['onesf'])
    S.op('pool', lambda e: e.tensor_copy(out=selb[:], in_=sel[:]), reads=['sel'], writes=['selb'])
    S.op('pool', lambda e: e.tensor_copy(out=qsb[:], in_=zs[:, 0:1536]), writes=['qsb'])
    accps = [PS[2], PS[3], PS[4]]
    denps = PS[5]
    qbr = Ring("ps", PS[6:8], 6)
    first_acc = [True, True, True]
    first_den = [True]
    items = [(b, g) for b in range(NS) for g in range(3)]

    def stageA(b, g):
        d = GROUPS[g][1]
        kv, kres_ = kvr.next()
        load('sp', kv[:], ck_d[g][b, sl(0, 128, d), :], kres_)
        qb, qbres = qbr.next()
        S.op('pe', lambda pe, qb=qb, b=b, g=g: pe.matmul(qb[:, :], lhsT=selb[:, b, :], rhs=qsb[:, g * 512:(g + 1) * 512],
                                                        start=True, stop=True),
             reads=['selb', 'qsb'], writes=[qbres])
        kvb, kvbres = kvbr.next()
        S.op('act', lambda e, kvb=kvb, kv=kv: e.activation(out=kvb[:], in_=kv[:, 512:1024], func=AF.Identity), reads=[kres_], writes=[kvbres])
        prod, pres_ = prodr.next()
        sc, sres = scr.next()
        for hh in range(4):
            S.op('dve', lambda e, prod=prod, kv=kv, qb=qb, sc=sc, hh=hh: e.scalar_tensor_tensor(
                out=prod[:, hh * 128:(hh + 1) * 128], in0=kv[:, hh * 128:(hh + 1) * 128], scalar=1.0,
                in1=qb[:, hh * 128:(hh + 1) * 128], op0=ALU.mult, op1=ALU.mult, accum_out=sc[:, hh:hh + 1]),
                reads=[kres_, qbres], writes=[sres + '_%d' % hh, pres_ + '_%d' % hh] + ([sres] if hh == 0 else []))
        S.op('dve', lambda e, sc=sc, g=g: e.tensor_tensor(out=sc[:], in0=sc[:], in1=tbs[:, 4 * g:4 * g + 4], op=ALU.add),
             reads=[sres + '_%d' % hh for hh in range(4)] + ['tbs'], writes=[sres])
        pcol = 'pz%d_%d' % (b, g)
        S.op('act', lambda e, sc=sc, g=g, b=b: e.activation(out=pzz[:, 4 * g:4 * g + 4, b, b], in_=sc[:], func=AF.Exp),
             reads=[sres, 'pzz', 'pzz1', 'pzz2'], writes=[pcol])
        return (b, g, kvb, kvbres, pcol)

    def stageB(b, g, kv, kres_, pcol):
        def fnpv(pe, g=g, b=b, kv=kv, fa=first_acc[g], fd=first_den[0]):
            ins = None
            for hh in range(4):
                pe.matmul(accps[g][0:NS, hh * 128:(hh + 1) * 128], lhsT=pzz[:, 4 * g + hh, b, :],
                          rhs=kv[:, hh * 128:(hh + 1) * 128],
                          start=(fa and hh == 0), stop=(b == NS - 1 and hh == 3), skip_group_check=True)
                ins = pe.matmul(denps[0:NS, 4 * g + hh:4 * g + hh + 1], lhsT=pzz[:, 4 * g + hh, b, :], rhs=onesf[:, 0:1],
                                start=(fd and hh == 0), stop=(b == NS - 1 and g == 2 and hh == 3), skip_group_check=True)
            return ins
        S.op('pe', fnpv, reads=[pcol, kres_, 'onesf'], writes=['accps%d' % g, 'denps'])
        first_acc[g] = False
        first_den[0] = False


    class _DQ:
        def __init__(self):
            self.q = []

        def op(self, *a, **k):
            self.q.append((a, k))

        def load(self, out, in_, res):
            self.q.append((('sp', dma(out, in_)), dict(writes=[res], dma=True)))

        def flush(self, n):
            while n > 0 and self.q:
                a, k = self.q.pop(0)
                S.op(*a, **k)
                n -= 1
    DQ = _DQ()
    stc = A("stc", [NS, 5, 512], F32)
    cwb = A("cwb", [NS, 5, 512], F32)
    DQ.op('act', lambda e: e.activation(out=us[:], in_=zs[:, 5632:6144], func=AF.Sigmoid), writes=['us'])
    DQ.op('pool', lambda e: e.tensor_tensor(out=us[:], in0=us[:], in1=zs[:, 5120:5632], op=ALU.mult), reads=['us'], writes=['us'])
    DQ.op('sp', dma(convs_d[:, 29, :], us[:]), reads=['us'], dma=True)
    DQ.op('pool', lambda e: e.tensor_copy(out=ycs[:], in_=cvec[:, 0, :]), reads=['cvec'], writes=['ycs'])
    for hf in range(6):
        DQ.load(stc[:], sconv_d[:, hf * 5:(hf + 1) * 5, :], 'stc')
        DQ.load(cwb[:], cwbc_d[:, hf * 5:(hf + 1) * 5, :], 'cwb')
        DQ.op('pool', lambda e: e.tensor_tensor(out=stc[:], in0=stc[:], in1=cwb[:], op=ALU.mult), reads=['stc', 'cwb'], writes=['stc'])
        for jt in range(5):
            DQ.op('pool', lambda e, jt=jt: e.tensor_tensor(out=ycs[:], in0=ycs[:], in1=stc[:, jt, :], op=ALU.add),
                 reads=['stc', 'ycs'], writes=['ycs'])
    DQ.load(cwb[:, 0, :], cwbc_d[:, 30, :], 'cwb')
    DQ.op('pool', lambda e: e.tensor_tensor(out=ptmp[:], in0=us[:], in1=cwb[:, 0, :], op=ALU.mult), reads=['us', 'cwb'], writes=['ptmp'])
    DQ.op('pool', lambda e: e.tensor_tensor(out=ycs[:], in0=ycs[:], in1=ptmp[:], op=ALU.add), reads=['ptmp', 'ycs'], writes=['ycs'])
    emit_ln(DQ, ycs, 'ycs', NS, sqb5, st5, cvec[:, 1, :], cvec[:, 2, :], None, ['cvec'], width=512)
    DQ.op('act', lambda e: e.activation(out=ycs[:], in_=ycs[:], func=AF.Silu), reads=['ycs'], writes=['ycs'])
    DQ.op('act', lambda e: e.activation(out=ptmp[:], in_=zs[:, 6144:6656], func=AF.Silu), reads=['ptmp'], writes=['ptmp'])
    DQ.op('pool', lambda e: e.tensor_tensor(out=ycs[:], in0=ycs[:], in1=ptmp[:], op=ALU.mult), reads=['ycs', 'ptmp'], writes=['ycs'])

    pend = []
    for (b, g) in items:
        pend.append(stageA(b, g))
        if len(pend) > 4:
            stageB(*pend.pop(0))
        DQ.flush(2)
    while pend:
        stageB(*pend.pop(0))
    DQ.flush(10000)
    for g in range(3):
        for jh in range(4):
            h = 4 * g + jh
            S.op('dve', lambda e, g=g, jh=jh, h=h: e.scalar_tensor_tensor(
                out=acc[:, g, jh * 128:(jh + 1) * 128], in0=zs[:, 3072 + h * 128:3072 + (h + 1) * 128],
                scalar=snew[:, h:h + 1], in1=accps[g][0:NS, jh * 128:(jh + 1) * 128], op0=ALU.mult, op1=ALU.add),
                reads=['accps%d' % g], writes=['acc%d_%d' % (g, jh)])
    S.op('dve', lambda e: e.tensor_tensor(out=dens[:], in0=denps[0:NS, 0:12], in1=snew[:], op=ALU.add),
         reads=['denps'], writes=['dens'])
    S.barrier()
    A.top = mark5

    sgs = A("sgs", [NS, 1024], F32)
    sgs2 = A("sgs2", [NS, 1024], F32)
    ms = A("ms", [NS, 1024], F32)
    trT = A("trT", [128, 8, NS], BF16)
    trT2 = A("trT2", [128, 4, NS], BF16)
    rs = A("rs", [NS, 1024], F32)
    zr2 = Ring("ps", PS[0:2])
    S.op('dve', lambda e: e.tensor_tensor(out=acc[:, 0, :], in0=acc[:, 0, :], in1=acc[:, 1, :], op=ALU.add), writes=['accs'])
    S.op('dve', lambda e: e.tensor_tensor(out=acc[:, 0, :], in0=acc[:, 0, :], in1=acc[:, 2, :], op=ALU.add), reads=['accs'], writes=['accs'])
    S.op('dve', lambda e: e.tensor_tensor(out=dens[:, 0:4], in0=dens[:, 0:4], in1=dens[:, 4:8], op=ALU.add), writes=['dens'])
    S.op('dve', lambda e: e.tensor_tensor(out=dens[:, 0:4], in0=dens[:, 0:4], in1=dens[:, 8:12], op=ALU.add), reads=['dens'], writes=['dens'])
    S.op('dve', lambda e: e.reciprocal(out=dens[:, 0:4], in_=dens[:, 0:4]), reads=['dens'], writes=['dens'])
    S.op('act', lambda e: e.activation(out=ptmp[:], in_=zs[:, 4608:5120], func=AF.Silu), writes=['ptmp'])
    for jh in range(4):
        S.op('dve', lambda e, jh=jh: e.scalar_tensor_tensor(
            out=osb[:, jh * 128:(jh + 1) * 128], in0=acc[:, 0, jh * 128:(jh + 1) * 128], scalar=dens[:, jh:jh + 1],
            in1=ptmp[:, jh * 128:(jh + 1) * 128], op0=ALU.mult, op1=ALU.mult),
            reads=['accs', 'dens', 'ptmp'], writes=['osb%d' % jh])
    OSB = ['osb%d' % j for j in range(4)]

    def transp(src_fn, nch, dstT, tag, reads):
        ps, pres = zr2.next()

        def fn(pe):
            ins = None
            for c in range(nch):
                ins = pe.matmul(ps[:, c * NS:(c + 1) * NS], lhsT=src_fn(c), rhs=ident[0:NS, 0:NS], start=True, stop=True)
            return ins
        S.op('pe', fn, reads=reads, writes=[pres])
        S.op('dve', lambda e: e.tensor_copy(out=dstT[:, 0:nch, :], in_=ps[:, 0:nch * NS].rearrange("p (c n) -> p c n", n=NS)),
             reads=[pres], writes=[tag])

    transp(lambda c: osb[:, c * 128:(c + 1) * 128], 4, trT, 'ogsT', OSB)
    transp(lambda c: ycs[:, c * 128:(c + 1) * 128], 4, trT2, 'cgsT', ['ycs'])
    S.op('act', lambda e: e.activation(out=sgs[:], in_=zs[:, 6656:7680], func=AF.Sigmoid), writes=['sgs'])
    S.op('act', lambda e: e.activation(out=sgs2[:], in_=zs[:, 7680:8704], func=AF.Sigmoid), writes=['sgs2'])
    for half in range(2):
        hs = slice(half * 512, (half + 1) * 512)
        ps, pres = zr2.next()

        def fa5(pe, ps=ps, hs=hs):
            ins = None
            for cc in range(4):
                ins = pe.matmul(ps[0:NS, :], lhsT=trT[:, cc, :], rhs=wpa5[:, cc, hs], start=(cc == 0), stop=(cc == 3))
            return ins
        S.op('pe', fa5, reads=['ogsT', 'wpa5'], writes=[pres])
        S.op('dve', lambda e, ps=ps, hs=hs: e.tensor_tensor(out=ms[:, hs], in0=ps[0:NS, :], in1=sgs[:, hs], op=ALU.mult),
             reads=[pres, 'sgs'], writes=['msa%d' % half])
        ps, pres = zr2.next()

        def fb5(pe, ps=ps, hs=hs):
            ins = None
            for cc in range(4):
                ins = pe.matmul(ps[0:NS, :], lhsT=trT2[:, cc, :], rhs=wpb5[:, cc, hs], start=(cc == 0), stop=(cc == 3))
            return ins
        S.op('pe', fb5, reads=['cgsT', 'wpb5'], writes=[pres])
        S.op('dve', lambda e, ps=ps, hs=hs: e.tensor_tensor(out=sgs2[:, hs], in0=ps[0:NS, :], in1=sgs2[:, hs], op=ALU.mult),
             reads=[pres, 'sgs2'], writes=['msb%d' % half])
        S.op('dve', lambda e, hs=hs: e.tensor_tensor(out=ms[:, hs], in0=ms[:, hs], in1=sgs2[:, hs], op=ALU.add),
             reads=['msa%d' % half, 'msb%d' % half], writes=['ms%d' % half])
    transp(lambda c: ms[:, c * 128:(c + 1) * 128], 8, trT, 'msT', ['ms0', 'ms1', 'ogsT'])
    for half in range(2):
        hs = slice(half * 512, (half + 1) * 512)
        ps, pres = zr2.next()

        def fy5(pe, ps=ps, hs=hs):
            ins = None
            for oc in range(8):
                ins = pe.matmul(ps[0:NS, :], lhsT=trT[:, oc, :], rhs=wo5[:, oc, hs], start=(oc == 0), stop=(oc == 7))
            return ins
        S.op('pe', fy5, reads=['msT', 'wo5'], writes=[pres])
        S.op('dve', lambda e, ps=ps, hs=hs: e.tensor_tensor(out=rs[:, hs], in0=ps[0:NS, :], in1=gate_s[0:NS, hs], op=ALU.mult),
             reads=[pres], writes=['rs%d' % half])
    S.op('dve', lambda e: e.scalar_tensor_tensor(out=rs[:], in0=xs[:], scalar=ALPHA, in1=rs[:], op0=ALU.mult, op1=ALU.add),
         reads=['rs0', 'rs1', 'xs'], writes=['rs'])
    emit_ln(S, rs, 'rs', NS, sqb5, st5, lng5[0:NS, :], lnb5[0:NS, :], ys_d, ['lng5', 'lnb5'])
    S.barrier()
    return nc, S


def emit_ln(S, r, rres, np_, sqb, st, g_ap, b_ap, out_dram, extra_reads, width=1024, eng2='dve'):
    inv = 1.0 / width
    rv = r[0:np_, 0:width]
    S.op('act', lambda e: e.activation(out=sqb[0:np_, 0:width], in_=rv, func=AF.Identity, accum_out=st[0:np_, 0:1]),
         reads=[rres], writes=['st0', 'sqb'])
    S.op('act', lambda e: e.activation(out=sqb[0:np_, 0:width], in_=rv, func=AF.Square, accum_out=st[0:np_, 1:2]),
         reads=[rres, 'sqb'], writes=['st1', 'sqb'])
    S.op('dve', lambda e: e.tensor_scalar(out=st[0:np_, 2:3], in0=st[0:np_, 0:1], scalar1=inv, scalar2=0.0, op0=ALU.mult, op1=ALU.add),
         reads=['st0'], writes=['st2'])
    S.op('dve', lambda e: e.tensor_tensor(out=st[0:np_, 3:4], in0=st[0:np_, 2:3], in1=st[0:np_, 2:3], op=ALU.mult),
         reads=['st2'], writes=['st3'])
    S.op('dve', lambda e: e.scalar_tensor_tensor(out=st[0:np_, 4:5], in0=st[0:np_, 1:2], scalar=inv, in1=st[0:np_, 3:4],
                                                  op0=ALU.mult, op1=ALU.subtract),
         reads=['st1', 'st3'], writes=['st4'])
    S.op('act', lambda e: e.activation(out=st[0:np_, 5:6], in_=st[0:np_, 4:5], func=AF.Sqrt, bias=EPS, scale=1.0),
         reads=['st4'], writes=['st5'])
    S.op('dve', lambda e: e.reciprocal(out=st[0:np_, 5:6], in_=st[0:np_, 5:6]), reads=['st5'], writes=['st5'])
    S.op('dve', lambda e: e.scalar_tensor_tensor(out=st[0:np_, 6:7], in0=st[0:np_, 2:3], scalar=-1.0, in1=st[0:np_, 5:6],
                                                  op0=ALU.mult, op1=ALU.mult),
         reads=['st2', 'st5'], writes=['st6'])
    S.op('act', lambda e: e.activation(out=rv, in_=rv, func=AF.Identity, bias=st[0:np_, 6:7], scale=st[0:np_, 5:6]),
         reads=['st5', 'st6', rres, 'sqb'], writes=[rres])
    S.op(eng2, lambda e: e.tensor_tensor(out=rv, in0=rv, in1=g_ap, op=ALU.mult),
         reads=[rres] + list(extra_reads), writes=[rres])
    S.op(eng2, lambda e: e.tensor_tensor(out=rv, in0=rv, in1=b_ap, op=ALU.add), reads=[rres] + list(extra_reads), writes=[rres])
    if out_dram is not None:
        S.op('sp', lambda e: e.dma_start(out=out_dram, in_=rv), reads=[rres], dma=True)


_PROG = None


def _get_prog():
    global _PROG
    if _PROG is None:
        from contextlib import ExitStack
        es = ExitStack()
        nc, S = build_program()
        esems = {e: es.enter_context(nc.semaphore("s_" + e)) for e in ENGS}
        dsems = [es.enter_context(nc.semaphore("d%d" % i)) for i in range(ND)]
        block = es.enter_context(nc.Block())
        S.emit(nc, block, esems, dsems)
        es.close()
        _PROG = nc
    return _PROG


def _fm(v, nchunk):
    return np.ascontiguousarray(v.reshape(nchunk, 128).T)


def _wfm(w):
    K, N = w.shape
    return np.ascontiguousarray(w.reshape(K // 128, 128, N).transpose(1, 0, 2))


def kernel(x_prompt, x_sample, c_prompt, c_sample, cache_kv_w128, cache_kv_w512, cache_kv_w2048,
           state_conv, w_c, b_c, w_in, b_in, conv_w, conv_b, conv_norm_g, conv_norm_b,
           w_pa, w_pb, w_o, ln_g, ln_b):
    f32 = np.float32
    x_prompt = np.asarray(x_prompt, f32)
    x_sample = np.asarray(x_sample, f32)
    nc = _get_prog()
    hidx = np.arange(1, 13, dtype=np.float64)
    slopes = 2.0 ** (-8.0 * hidx / 12.0)
    kk = np.arange(128)[:, None]
    qq = np.arange(128)[None, :]
    tb = np.full((NCORES, 128, 12, 2, 256), NEGB, f32)
    for h in range(12):
        d = GROUPS[h // 4][1]
        prev = np.where(kk >= qq, -slopes[h] * d * (qq - kk + 128), NEGB)
        cur = np.where(kk <= qq, -slopes[h] * d * (qq - kk), NEGB)
        for c in range(NCORES):
            tb[c, :, h, 0, 0:128] = prev
            tb[c, :, h, 0, 128:256] = cur
            tb[c, :, h, 1, 0:128] = prev if c > 0 else NEGB
            tb[c, :, h, 1, 128:256] = cur
    tbs = np.zeros((128, 12), f32)
    for h in range(12):
        d = GROUPS[h // 4][1]
        tbs[:, h] = -slopes[h] * d * (128 - np.arange(128))
    ident = np.eye(128, dtype=f32)
    sel = np.zeros((NS, NS, 128), f32)
    for b in range(NS):
        sel[b, b, :] = 1.0

    w_c = np.asarray(w_c, f32); w_in = np.asarray(w_in, f32)
    b_c = np.asarray(b_c, f32); b_in = np.asarray(b_in, f32)
    conv_w = np.asarray(conv_w, f32)
    shared = {
        "wc": _wfm(w_c), "bcT": _fm(b_c[0:2048], 16),
        "bg": np.ascontiguousarray(np.broadcast_to(b_c[2048:3072], (128, 1024))),
        "win": np.ascontiguousarray(w_in.reshape(8, 128, 68, 128).transpose(2, 1, 0, 3)), "bfm": _fm(b_in, 68),
        "bvbc": np.ascontiguousarray(np.broadcast_to(b_in[3072:4608], (128, 1536))),
        "binbc": np.ascontiguousarray(np.broadcast_to(b_in, (NS, 8704))),
        "cwfm": np.ascontiguousarray(conv_w.T.reshape(4, 128, 31).transpose(1, 0, 2)),
        "cbfm": _fm(np.asarray(conv_b, f32), 4), "cgfm": _fm(np.asarray(conv_norm_g, f32), 4),
        "cnbfm": _fm(np.asarray(conv_norm_b, f32), 4),
        "cwbc": np.ascontiguousarray(np.broadcast_to(conv_w, (NS, 31, 512))),
        "cvec": np.ascontiguousarray(np.broadcast_to(
            np.stack([np.asarray(conv_b, f32), np.asarray(conv_norm_g, f32), np.asarray(conv_norm_b, f32)]), (NS, 3, 512))),
        "wpa": _wfm(np.asarray(w_pa, f32)), "wpb": _wfm(np.asarray(w_pb, f32)), "wo": _wfm(np.asarray(w_o, f32)),
        "lng": np.ascontiguousarray(np.broadcast_to(np.asarray(ln_g, f32), (128, 1024))),
        "lnb": np.ascontiguousarray(np.broadcast_to(np.asarray(ln_b, f32), (128, 1024))),
        "tbs": tbs, "ident": ident, "sel": sel,
    }
    xx = x_prompt[0]
    xpad = np.concatenate([np.zeros((TOK, 1024), f32), xx], axis=0)
    cp = np.asarray(c_prompt, f32)[0]
    cs = np.asarray(c_sample, f32)
    xs2 = x_sample[:, 0, :]
    caches = [np.asarray(cache_kv_w128, f32).reshape(128, 128, 1024), np.asarray(cache_kv_w512, f32).reshape(128, 512, 1024),
              np.asarray(cache_kv_w2048, f32).reshape(128, 2048, 1024)]
    sconv = np.asarray(state_conv, f32)
    in_maps = []
    for c in range(NCORES):
        seg = xpad[c * TOK:(c + 2) * TOK]
        m = dict(shared)
        m["xT"] = np.ascontiguousarray(seg.T.reshape(8, 128, 2 * TOK).transpose(1, 0, 2))
        m["xtok"] = np.ascontiguousarray(xx[c * TOK:(c + 1) * TOK])
        sb = slice(c * NS, (c + 1) * NS)
        m["xsT"] = np.ascontiguousarray(xs2[sb].T.reshape(8, 128, NS).transpose(1, 0, 2))
        m["xstok"] = np.ascontiguousarray(xs2[sb])
        call = np.concatenate([cp[None, :], cs[sb]], axis=0)
        m["cT"] = np.ascontiguousarray(call.T.reshape(8, 128, 17).transpose(1, 0, 2))
        m["cbc"] = np.ascontiguousarray(np.broadcast_to(cp.reshape(8, 128).T[:, :, None], (128, 8, 128)))
        m["tb"] = tb[c]
        m["valid"] = np.full((128, 1), 0.0 if c == 0 else 1.0, f32)
        m["ck0"] = np.ascontiguousarray(caches[0][sb])
        m["ck1"] = np.ascontiguousarray(caches[1][sb])
        m["ck2"] = np.ascontiguousarray(caches[2][sb])
        m["sconv"] = np.ascontiguousarray(sconv[sb])
        in_maps.append(m)
    res = run_bass_kernel_spmd(nc, in_maps, core_ids=list(range(NCORES)))
    R = res.results
    y = np.concatenate([R[c]["y"] for c in range(NCORES)], axis=0)[None]
    ys = np.concatenate([R[c]["ys"] for c in range(NCORES)], axis=0)[:, None, :]
    last = R[NCORES - 1]
    kvp = []
    for g in range(3):
        k = np.asarray(last["kT%d" % g]).transpose(2, 1, 0)
        v = np.asarray(last["v%d" % g]).reshape(-1, 4, 128)
        kvp.append(np.ascontiguousarray(np.stack([k, v], axis=1))[None].astype(f32))
    convp = np.ascontiguousarray(np.asarray(last["uT"]).transpose(2, 1, 0).reshape(30, 512))[None].astype(f32)
    ksn = np.concatenate([R[c]["ksn"] for c in range(NCORES)], axis=0).reshape(128, 3, 4, 128)
    vsn = np.concatenate([R[c]["vsn"] for c in range(NCORES)], axis=0).reshape(128, 3, 4, 128)
    kvs = [np.ascontiguousarray(np.stack([ksn[:, g], vsn[:, g]], axis=1))[:, None].astype(f32) for g in range(3)]
    convs = np.concatenate([R[c]["convs"] for c in range(NCORES)], axis=0).astype(f32)
    return (y.astype(f32), ys.astype(f32), kvp[0], kvp[1], kvp[2], convp, kvs[0], kvs[1], kvs[2], convs)
```

```python
import math
import numpy as np
import concourse.bass as bass
import concourse.mybir as mybir
from concourse.bass_utils import run_bass_kernel_spmd

F32 = mybir.dt.float32
BF16 = mybir.dt.bfloat16
AF = mybir.ActivationFunctionType
ALU = mybir.AluOpType
AX = mybir.AxisListType

ENGS = ['pe', 'act', 'dve', 'pool', 'sp']
ND = 24
NCORES = 8
TOK = 2048
NS = 16
QS = 128 ** -0.5
ALPHA = 2.0 ** 0.25
EPS = 1e-5
NEGB = -30000.0
GROUPS = ((128, 1), (512, 4), (2048, 16))


def sl(start, n, step=1):
    return slice(start, start + step * (n - 1) + 1, step)


class Sched:
    def __init__(self):
        self.ops = []
        self.last_w = {}
        self.readers = {}
        self.eng_last = {e: None for e in ENGS}
        self.pending_dma = []

    def op(self, eng, fn, reads=(), writes=(), dma=False, extra_deps=()):
        idx = len(self.ops)
        deps = set(extra_deps)
        for r in reads:
            if r in self.last_w:
                deps.add(self.last_w[r])
        for r in writes:
            if r in self.last_w:
                deps.add(self.last_w[r])
            for x in self.readers.get(r, ()):
                deps.add(x)
        deps.discard(idx)
        self.ops.append(dict(eng=eng, fn=fn, deps=sorted(deps), dma=dma, sig=False))
        for r in reads:
            self.readers.setdefault(r, []).append(idx)
        for r in writes:
            self.last_w[r] = idx
            self.readers[r] = []
        self.eng_last[eng] = idx
        if dma:
            self.pending_dma.append(idx)
        return idx

    def barrier(self):
        deps = [v for v in self.eng_last.values() if v is not None] + list(self.pending_dma)
        b = self.op('sp', lambda e: e.nop(), extra_deps=deps)
        for e in ENGS:
            if e != 'sp':
                self.op(e, lambda eng: eng.nop(), extra_deps=[b])
        self.pending_dma = []
        self.last_w = {}
        self.readers = {}

    def _needs_wait(self, op, dop):
        if dop['dma']:
            return True
        if dop['eng'] == op['eng']:
            return op['eng'] != 'pe'
        return True

    def emit(self, nc, block, esems, dsems):
        ops = self.ops
        for op in ops:
            for d in op['deps']:
                if self._needs_wait(op, ops[d]):
                    ops[d]['sig'] = True
        cnt = {e: 0 for e in ENGS}
        ndma = 0
        for op in ops:
            if op['dma']:
                op['dk'] = ndma
                ndma += 1
            elif op['sig']:
                cnt[op['eng']] += 1
                op['count'] = cnt[op['eng']]
        per_eng = {e: [] for e in ENGS}
        known = {}
        knownd = {}
        for op in ops:
            E = op['eng']
            waits = []
            if op['dma'] and op['dk'] >= ND:
                s = op['dk'] % ND
                v = 16 * (op['dk'] // ND)
                if knownd.get((E, s), 0) < v:
                    waits.append((dsems[s], v))
                    knownd[(E, s)] = v
            for d in op['deps']:
                dop = ops[d]
                if not self._needs_wait(op, dop):
                    continue
                if dop['dma']:
                    s = dop['dk'] % ND
                    v = 16 * (dop['dk'] // ND + 1)
                    if knownd.get((E, s), 0) < v:
                        waits.append((dsems[s], v))
                        knownd[(E, s)] = v
                else:
                    Fe = dop['eng']
                    v = dop['count']
                    if known.get((E, Fe), 0) < v:
                        known[(E, Fe)] = v
                        waits.append((esems[Fe], v))
            mw = {}
            for s, v in waits:
                k = id(s)
                if k not in mw or mw[k][1] < v:
                    mw[k] = (s, v)
            op['waits'] = list(mw.values())
            per_eng[E].append(op)

        def mk(E):
            def run(engh):
                for op in per_eng[E]:
                    for s, v in op['waits']:
                        engh.wait_ge(s, v)
                    ins = op['fn'](engh)
                    if op['dma']:
                        ins.then_inc(dsems[op['dk'] % ND], 16)
                    elif op['sig']:
                        ins.then_inc(esems[E], 1)
            return run

        block.tensor(mk('pe'))
        block.scalar(mk('act'))
        block.vector(mk('dve'))
        block.gpsimd(mk('pool'))
        block.sync(mk('sp'))


class Alloc:
    def __init__(self, nc):
        self.nc = nc
        self.top = 16512
        self.lim = 16481 + 212863
        self.n = 0

    def __call__(self, name, shape, dt):
        size = 1
        for s in shape[1:]:
            size *= s
        size *= 4 if dt == F32 else 2
        size = (size + 63) // 64 * 64
        off = self.top
        self.top += size
        assert self.top <= self.lim, (name, self.top, self.lim)
        self.n += 1
        return self.nc.alloc_sbuf_tensor_at("%s_%d" % (name, self.n), list(shape), dt, offset=off)


class Ring:
    def __init__(self, name, bufs, base=0, names=None):
        self.name = name
        self.bufs = bufs
        self.i = 0
        self.base = base
        self.names = names

    def next(self):
        k = self.i % len(self.bufs)
        self.i += 1
        if self.names is not None:
            return self.bufs[k], self.names[k]
        return self.bufs[k], "%s%d" % (self.name, k + self.base)


def build_program():
    nc = bass.Bass("TRN2", target_bir_lowering=False)

    def din(name, shape):
        return nc.dram_tensor(name, list(shape), F32, kind="ExternalInput").ap()

    def dout(name, shape):
        return nc.dram_tensor(name, list(shape), F32, kind="ExternalOutput").ap()

    xT_d = din("xT", [128, 8, 4096])
    xtok_d = din("xtok", [TOK, 1024])
    xsT_d = din("xsT", [128, 8, NS])
    xstok_d = din("xstok", [NS, 1024])
    cT_d = din("cT", [128, 8, 17])
    cbc_d = din("cbc", [128, 8, 128])
    wc_d = din("wc", [128, 8, 3072])
    bcT_d = din("bcT", [128, 16])
    bg_d = din("bg", [128, 1024])
    win_d = din("win", [68, 128, 8, 128])
    bfm_d = din("bfm", [128, 68])
    bv_d = din("bvbc", [128, 1536])
    binbc_d = din("binbc", [NS, 8704])
    cwfm_d = din("cwfm", [128, 4, 31])
    cbfm_d = din("cbfm", [128, 4])
    cgfm_d = din("cgfm", [128, 4])
    cnbfm_d = din("cnbfm", [128, 4])
    cwbc_d = din("cwbc", [NS, 31, 512])
    cvec_d = din("cvec", [NS, 3, 512])
    wpa_d = din("wpa", [128, 4, 1024])
    wpb_d = din("wpb", [128, 4, 1024])
    wo_d = din("wo", [128, 8, 1024])
    lng_d = din("lng", [128, 1024])
    lnb_d = din("lnb", [128, 1024])
    tb_d = din("tb", [128, 12, 2, 256])
    tbs_d = din("tbs", [128, 12])
    valid_d = din("valid", [128, 1])
    ident_d = din("ident", [128, 128])
    sel_d = din("sel", [NS, NS, 128])
    ck_d = [din("ck0", [NS, 128, 1024]), din("ck1", [NS, 512, 1024]), din("ck2", [NS, 2048, 1024])]
    sconv_d = din("sconv", [NS, 30, 512])

    y_d = dout("y", [TOK, 1024])
    ys_d = dout("ys", [NS, 1024])
    kTo_d = [dout("kT0", [128, 4, 128]), dout("kT1", [128, 4, 512]), dout("kT2", [128, 4, 2048])]
    vo_d = [dout("v0", [128, 512]), dout("v1", [512, 512]), dout("v2", [2048, 512])]
    uT_d = dout("uT", [128, 4, 30])
    ksn_d = dout("ksn", [NS, 1536])
    vsn_d = dout("vsn", [NS, 1536])
    convs_d = dout("convs", [NS, 30, 512])

    S = Sched()
    A = Alloc(nc)
    PS = [nc.alloc_psum_tensor("psb%d" % i, [128, 512], F32) for i in range(8)]
    PSR = ["ps%d" % i for i in range(8)]

    zs_d = nc.dram_tensor("zs_scr", [NS, 8704], F32).ap()

    def dma(out, in_):
        return lambda e: e.dma_start(out=out, in_=in_)

    def load(eng, out, in_, res):
        S.op(eng, dma(out, in_), writes=[res], dma=True)

    modT = A("modT", [128, 16, 17], F32)
    gate_bc = A("gate_bc", [128, 1024], F32)
    gate_s = A("gate_s", [128, 1024], F32)
    bfm = A("bfm", [128, 68], F32)
    bqs = A("bqs", [128, 12], F32)
    hsT = A("hsT", [128, 8, NS], BF16)
    ones_bf = A("ones_bf", [128, 128], BF16)
    onesm = A("onesm", [128, 128], BF16)
    ident = A("ident", [128, 128], F32)
    ident_bf = A("ident_bf", [128, 128], BF16)
    validt = A("valid", [128, 1], F32)
    cwfm = A("cwfm", [128, 4, 31], F32)
    cbfm = A("cbfm", [128, 4], F32)
    cgfm = A("cgfm", [128, 4], F32)
    cnbfm = A("cnbfm", [128, 4], F32)
    uhalo = A("uhalo", [128, 4, 32], BF16)
    zstr = Ring("zst", [A("zst%d" % i, [NS, 128], F32) for i in range(2)])
    mark_persist = A.top

    def piggy(chunk, w_fn, res_list, ring):
        ps, pres = ring.next()
        S.op('pe', mm8(ps[0:NS, 0:128], lambda kc: hsT[:, kc, :], w_fn), reads=res_list, writes=[pres])
        zst, zres = zstr.next()
        S.op('act', lambda e, zst=zst, ps=ps: e.activation(out=zst[:], in_=ps[0:NS, 0:128], func=AF.Identity),
             reads=[pres], writes=[zres])
        S.op('sp', dma(zs_d[:, chunk * 128:(chunk + 1) * 128], zst[:]), reads=[zres], dma=True)
    hTm = A("hTm", [128, 8, TOK], BF16)
    ogT = A("ogT", [128, 4, TOK], BF16)
    mark_C = A.top

    load('sp', bfm[:], bfm_d, 'bfm')
    load('sp', ident[:], ident_d, 'ident')
    load('sp', validt[:], valid_d, 'valid')
    load('sp', cwfm[:], cwfm_d, 'cw')
    load('sp', cbfm[:], cbfm_d, 'cw2')
    load('sp', cgfm[:], cgfm_d, 'cw3')
    load('sp', cnbfm[:], cnbfm_d, 'cw4')
    S.op('dve', lambda e: e.tensor_scalar(out=bqs[:], in0=bfm[:, 0:12], scalar1=QS, scalar2=0.0, op0=ALU.mult, op1=ALU.add),
         reads=['bfm'], writes=['bqs'])
    S.op('pool', lambda e: e.memset(ones_bf[:], 1.0), writes=['ones'])
    S.op('pool', lambda e: e.memset(onesm[:], 1.0 / 512.0), writes=['onesm'])
    S.op('pool', lambda e: e.tensor_copy(out=ident_bf[:], in_=ident[:]), reads=['ident'], writes=['identbf'])

    hTh = A("hTh", [128, 8, TOK], BF16)
    mark_B = A.top
    cT_sb = A("cT", [128, 8, 17], F32)
    cbc_sb = A("cbc", [128, 8, 128], F32)
    bcT_sb = A("bcT", [128, 16], F32)
    bg_sb = A("bg", [128, 1024], F32)
    wcr = Ring("wc", [A("wc%d" % i, [128, 8, 512], BF16) for i in range(3)])
    cT_bf = A("cT_bf", [128, 8, 17], BF16)
    cbc_bf = A("cbc_bf", [128, 8, 128], BF16)
    load('sp', cT_sb[:], cT_d, 'cT')
    load('sp', cbc_sb[:], cbc_d, 'cbc')
    load('sp', bcT_sb[:], bcT_d, 'bcT')
    load('sp', bg_sb[:], bg_d, 'bg')
    S.op('dve', lambda e: e.tensor_copy(out=cT_bf[:], in_=cT_sb[:]), reads=['cT'], writes=['cTb'])
    S.op('dve', lambda e: e.tensor_copy(out=cbc_bf[:], in_=cbc_sb[:]), reads=['cbc'], writes=['cbcb'])
    zr = Ring("ps", PS[0:2])

    def mm8(out, lhs_fn, rhs_fn):
        def fn(pe):
            ins = None
            for kc in range(8):
                ins = pe.matmul(out, lhsT=lhs_fn(kc), rhs=rhs_fn(kc), start=(kc == 0), stop=(kc == 7))
            return ins
        return fn

    for blk in range(6):
        buf, res = wcr.next()
        load('pool', buf[:], wc_d[:, :, blk * 512:(blk + 1) * 512], res)
        if blk < 4:
            for sub in range(4):
                cc = blk * 4 + sub
                ps, pres = zr.next()
                S.op('pe', mm8(ps[:, 0:17], lambda kc, b=buf, s=sub: b[:, kc, s * 128:(s + 1) * 128],
                               lambda kc: cT_bf[:, kc, :]), reads=[res, 'cTb'], writes=[pres])
                S.op('dve', lambda e, ps=ps, cc=cc: e.tensor_scalar(
                    out=modT[:, cc, :], in0=ps[:, 0:17], scalar1=bcT_sb[:, cc:cc + 1],
                    scalar2=(1.0 if cc >= 8 else 0.0), op0=ALU.add, op1=ALU.add),
                    reads=[pres, 'bcT'], writes=['modT%d' % cc])
        else:
            half = blk - 4
            hs = slice(half * 512, (half + 1) * 512)
            ps, pres = zr.next()
            S.op('pe', mm8(ps[:, :], lambda kc: cbc_bf[:, kc, :], lambda kc, b=buf: b[:, kc, :]),
                 reads=[res, 'cbcb'], writes=[pres])
            S.op('dve', lambda e, ps=ps, hs=hs: e.tensor_tensor(out=gate_bc[:, hs], in0=ps[:, :], in1=bg_sb[:, hs], op=ALU.add),
                 reads=[pres, 'bg'], writes=['gate%d' % half])
            ps, pres = zr.next()
            S.op('pe', mm8(ps[0:NS, :], lambda kc: cT_bf[:, kc, 1:17], lambda kc, b=buf: b[:, kc, :]),
                 reads=[res, 'cTb'], writes=[pres])
            S.op('dve', lambda e, ps=ps, hs=hs: e.tensor_tensor(out=gate_s[0:NS, hs], in0=ps[0:NS, :], in1=bg_sb[0:NS, hs], op=ALU.add),
                 reads=[pres, 'bg'], writes=['gates%d' % half])

    xtr = Ring("xt", [A("xt%d" % i, [128, 8, 512], F32) for i in range(4)])
    xsT_sb = A("xsT", [128, 8, NS], F32)
    xs_tmp = A("xs_tmp", [128, 8, NS], F32)
    load('sp', xsT_sb[:], xsT_d, 'xsT')
    for tt in range(8):
        buf, res = xtr.next()
        load('sp', buf[:], xT_d[:, :, tt * 512:(tt + 1) * 512], res)
        for kc in range(8):
            dst = hTh[:, kc, tt * 512:(tt + 1) * 512] if tt < 4 else hTm[:, kc, (tt - 4) * 512:(tt - 3) * 512]
            if kc % 2 == 0:
                S.op('act', lambda e, dst=dst, buf=buf, kc=kc: e.activation(
                    out=dst, in_=buf[:, kc, :], func=AF.Identity, bias=modT[:, kc, 0:1], scale=modT[:, 8 + kc, 0:1]),
                    reads=[res, 'modT%d' % kc, 'modT%d' % (8 + kc)], writes=['hT%d_%d' % (tt, kc)])
            else:
                S.op('dve', lambda e, dst=dst, buf=buf, kc=kc: e.tensor_scalar(
                    out=dst, in0=buf[:, kc, :], scalar1=modT[:, 8 + kc, 0:1], scalar2=modT[:, kc, 0:1],
                    op0=ALU.mult, op1=ALU.add),
                    reads=[res, 'modT%d' % kc, 'modT%d' % (8 + kc)], writes=['hT%d_%d' % (tt, kc)])
    S.op('dve', lambda e: e.tensor_tensor(out=xs_tmp[:], in0=xsT_sb[:], in1=modT[:, 8:16, 1:17], op=ALU.mult),
         reads=['xsT'] + ['modT%d' % c for c in range(16)], writes=['xs_tmp'])
    S.op('dve', lambda e: e.tensor_tensor(out=hsT[:], in0=xs_tmp[:], in1=modT[:, 0:8, 1:17], op=ALU.add),
         reads=['xs_tmp'], writes=['hsT'])
    S.barrier()
    A.top = mark_B

    def hsl(kc, tok0, n, step=1):
        if tok0 < TOK:
            return hTh[:, kc, sl(tok0, n, step)]
        return hTm[:, kc, sl(tok0 - TOK, n, step)]

    hwr = Ring("hw", [A("hw%d" % i, [128, 8, 128], BF16) for i in range(4)])
    hsg = Ring("hsg", [A("hsg%d" % i, [128, 64], F32) for i in range(2)])
    zrh = Ring("ps", PS[0:4])
    for cc in range(4):
        ba, ra = hwr.next()
        load('pool', ba[:], win_d[40 + cc], ra)
        bgl, rg = hwr.next()
        load('pool', bgl[:], win_d[44 + cc], rg)
        psa, pra = zrh.next()
        S.op('pe', mm8(psa[:, 0:30], lambda kc, ba=ba: ba[:, kc, :], lambda kc: hTh[:, kc, TOK - 30:TOK]), reads=[ra], writes=[pra])
        psg, prg = zrh.next()
        S.op('pe', mm8(psg[:, 0:30], lambda kc, bgl=bgl: bgl[:, kc, :], lambda kc: hTh[:, kc, TOK - 30:TOK]), reads=[rg], writes=[prg])
        sg, sres = hsg.next()
        S.op('act', lambda e, sg=sg, psg=psg, cc=cc: e.activation(out=sg[:, 0:30], in_=psg[:, 0:30], func=AF.Sigmoid,
                                                                 bias=bfm[:, 44 + cc:45 + cc], scale=1.0),
             reads=[prg], writes=[sres])
        S.op('dve', lambda e, sg=sg, psa=psa, cc=cc: e.scalar_tensor_tensor(
            out=sg[:, 32:62], in0=psa[:, 0:30], scalar=bfm[:, 40 + cc:41 + cc], in1=sg[:, 0:30], op0=ALU.add, op1=ALU.mult),
            reads=[pra, sres], writes=[sres])
        S.op('dve', lambda e, sg=sg, cc=cc: e.tensor_scalar(
            out=uhalo[:, cc, 0:30], in0=sg[:, 32:62], scalar1=validt[:, 0:1], scalar2=0.0, op0=ALU.mult, op1=ALU.add),
            reads=[sres], writes=['uhalo%d' % cc])
    S.barrier()
    A.top = mark_B

    vtiles = [[], [], []]
    for a in range(17):
        vtiles[0].append((a, 1920 + 128 * a, 1, 0 if a == 16 else None, 1))
    for r in range(4):
        for n_ in range(5):
            vtiles[1].append((r * 5 + n_, 1536 + 512 * n_ + r, 4, r if n_ == 4 else None, 4))
    for r in range(16):
        for b in range(2):
            vtiles[2].append((r * 2 + b, 2048 * b + r, 16, r if b == 1 else None, 16))
    nvt = [17, 20, 32]
    Vt = [A("V%d" % g, [128, nvt[g], 256], BF16) for g in range(3)]
    mark_V = A.top

    for pp in range(2):
        A.top = mark_V
        wv = [A("wv%d" % g, [128, 2, 8, 128], BF16) for g in range(3)]
        bv_sb = A("bv", [128, 1536], F32)
        vstr = Ring("vst", [A("vst0", [128, 256], F32), A("vst1", [128, 256], F32)])
        load('sp', bv_sb[:], bv_d, 'bv')
        for g in range(3):
            c0 = 3072 + g * 512 + pp * 256
            for ci in range(2):
                S.op('pool', dma(wv[g][:, ci, :, :], win_d[c0 // 128 + ci]), writes=['wv%d_%d' % (g, ci)], dma=True)
        zr4 = Ring("ps", PS[0:4])
        for g in range(3):
            bsl = slice(g * 512 + pp * 256, g * 512 + pp * 256 + 256)
            for (vidx, tok0, step, row0, rstep) in vtiles[g]:
                ps, pres = zr4.next()
                S.op('pe', mm8(ps[:, 0:256], lambda kc, tok0=tok0, step=step: hsl(kc, tok0, 128, step),
                               lambda kc, g=g: wv[g][:, :, kc, :]), reads=['wv%d_0' % g, 'wv%d_1' % g], writes=[pres])
                if row0 is None:
                    S.op('dve', lambda e, ps=ps, g=g, vidx=vidx, bsl=bsl: e.tensor_tensor(
                        out=Vt[g][:, vidx, :], in0=ps[:, 0:256], in1=bv_sb[:, bsl], op=ALU.add),
                        reads=[pres, 'bv'], writes=['V%d_%d' % (g, vidx)])
                else:
                    vst, vres = vstr.next()
                    S.op('dve', lambda e, ps=ps, vst=vst, bsl=bsl: e.tensor_tensor(
                        out=vst[:], in0=ps[:, 0:256], in1=bv_sb[:, bsl], op=ALU.add),
                        reads=[pres, 'bv'], writes=[vres])
                    S.op('pool', lambda e, vst=vst, g=g, vidx=vidx: e.tensor_copy(out=Vt[g][:, vidx, :], in_=vst[:]),
                         reads=[vres], writes=['V%d_%d' % (g, vidx)])
                    S.op('sp', dma(vo_d[g][sl(row0, 128, rstep), pp * 256:(pp + 1) * 256], vst[:]),
                         reads=[vres], dma=True)
        for g in range(3):
            for ci in range(2):
                piggy((3072 + g * 512 + pp * 256) // 128 + ci, lambda kc, g=g, ci=ci: wv[g][:, ci, kc, :],
                      ['wv%d_%d' % (g, ci)], zr4)
        S.barrier()
        A.top = mark_V

        wr = [A("wr%d" % i, [128, 8, 128], BF16) for i in range(7)]
        qT = A("qT", [128, 3, TOK], BF16)
        kTl = [A("kT0", [128, 128 + TOK], BF16), A("kT1", [128, 512 + TOK], BF16), A("kT2", [128, 2 * TOK], BF16)]
        sga = A("sga", [128, TOK], BF16)
        kstr = Ring("kst", [A("kst0", [128, 512], F32), A("kst1", [128, 512], F32)])
        tbst = Ring("tbst", [A("tbst0", [128, 2, 256], F32), A("tbst1", [128, 2, 256], F32)])
        MTs = A("MTs", [128, 3, 2, 256], BF16)
        Er = Ring("E", [A("E%d" % i, [128, 512], BF16) for i in range(3)])
        Pr = Ring("P", [A("P%d" % i, [128, 512], BF16) for i in range(4)])
        P2 = A("P2", [128, 16, 256], BF16)
        rden = A("rden", [128, 512], F32)
        t1 = A("t1", [128, 512], F32)
        for jj in range(2):
            j = 2 * pp + jj
            cols = [g * 512 + j * 128 for g in range(3)] + [1536 + g * 512 + j * 128 for g in range(3)] + [4608 + j * 128]
            for i, c0 in enumerate(cols):
                load('pool', wr[i][:], win_d[c0 // 128], 'wr%d' % i)
            for g in range(3):
                h = 4 * g + j
                tbuf, tres = tbst.next()
                load('sp', tbuf[:], tb_d[:, h, :, :], tres)
                S.op('act', lambda e, tbuf=tbuf, g=g: e.activation(out=MTs[:, g, :, :], in_=tbuf[:], func=AF.Exp),
                     reads=[tres], writes=['MT%d' % g])
            zr2 = Ring("ps", PS[0:2])
            for g in range(3):
                ch = cols[g] // 128
                for tt in range(4):
                    ps, pres = zr2.next()
                    S.op('pe', mm8(ps[:, :], lambda kc, g=g: wr[g][:, kc, :],
                                   lambda kc, tt=tt: hTm[:, kc, tt * 512:(tt + 1) * 512]),
                         reads=['wr%d' % g], writes=[pres])
                    S.op('act', lambda e, ps=ps, g=g, tt=tt, ch=ch: e.activation(
                        out=qT[:, g, tt * 512:(tt + 1) * 512], in_=ps[:, :], func=AF.Identity,
                        bias=bqs[:, ch:ch + 1], scale=QS),
                        reads=[pres], writes=['qT%d_%d' % (g, tt)])
            kres = [[], [], []]
            for g in range(3):
                ch = cols[3 + g] // 128
                halo = [128, 512, 2048][g]
                tl = []
                if g == 0:
                    tl.append((1920, 128, 0, None))
                elif g == 1:
                    tl.append((1536, 512, 0, None))
                else:
                    for tt in range(4):
                        tl.append((tt * 512, 512, tt * 512, None))
                for tt in range(4):
                    tl.append((TOK + tt * 512, 512, halo + tt * 512, tt))
                for ti, (tok0, n, dst, mt) in enumerate(tl):
                    ps, pres = zr2.next()
                    S.op('pe', mm8(ps[:, 0:n], lambda kc, g=g: wr[3 + g][:, kc, :],
                                   lambda kc, tok0=tok0, n=n: hsl(kc, tok0, n)),
                         reads=['wr%d' % (3 + g)], writes=[pres])
                    kst, ksres = kstr.next()
                    S.op('dve', lambda e, ps=ps, kst=kst, n=n, ch=ch: e.tensor_scalar(
                        out=kst[:, 0:n], in0=ps[:, 0:n], scalar1=bfm[:, ch:ch + 1], scalar2=0.0,
                        op0=ALU.add, op1=ALU.add), reads=[pres], writes=[ksres])
                    kr = 'kT%d_%d' % (g, ti)
                    kres[g].append(kr)
                    S.op('act', lambda e, kst=kst, g=g, dst=dst, n=n: e.activation(
                        out=kTl[g][:, dst:dst + n], in_=kst[:, 0:n], func=AF.Identity),
                        reads=[ksres], writes=[kr])
                    if mt is not None:
                        if g == 2:
                            S.op('sp', dma(kTo_d[2][:, j, mt * 512:(mt + 1) * 512], kst[:, :]), reads=[ksres], dma=True)
                        elif g == 1 and mt == 3:
                            S.op('sp', dma(kTo_d[1][:, j, :], kst[:, :]), reads=[ksres], dma=True)
                        elif g == 0 and mt == 3:
                            S.op('sp', dma(kTo_d[0][:, j, :], kst[:, 384:512]), reads=[ksres], dma=True)
            ch = cols[6] // 128
            for tt in range(4):
                ps, pres = zr2.next()
                S.op('pe', mm8(ps[:, :], lambda kc: wr[6][:, kc, :],
                               lambda kc, tt=tt: hTm[:, kc, tt * 512:(tt + 1) * 512]),
                     reads=['wr6'], writes=[pres])
                S.op('act', lambda e, ps=ps, tt=tt, ch=ch: e.activation(
                    out=sga[:, tt * 512:(tt + 1) * 512], in_=ps[:, :], func=AF.Silu, bias=bfm[:, ch:ch + 1], scale=1.0),
                    reads=[pres], writes=['sga%d' % tt])
            for i in range(7):
                piggy(cols[i] // 128, lambda kc, i=i: wr[i][:, kc, :], ['wr%d' % i], zr2)
            qres = [['qT%d_%d' % (g, tt) for tt in range(4)] for g in range(3)]

            def tile_g0(t):
                return (0, kTl[0][:, 128 * t:128 * t + 128], kTl[0][:, 128 * (t + 1):128 * (t + 2)],
                        qT[:, 0, 128 * t:128 * (t + 1)], 1 if t == 0 else 0, t, t + 1,
                        slice(128 * (t % 4), 128 * (t % 4) + 128))

            def tile_g1(r, n_):
                return (1, kTl[1][:, sl(512 * n_ + r, 128, 4)], kTl[1][:, sl(512 + 512 * n_ + r, 128, 4)],
                        qT[:, 1, sl(512 * n_ + r, 128, 4)], 1 if n_ == 0 else 0, r * 5 + n_, r * 5 + n_ + 1,
                        sl(r, 128, 4))

            def tile_g2(r):
                return (2, kTl[2][:, sl(r, 128, 16)], kTl[2][:, sl(TOK + r, 128, 16)],
                        qT[:, 2, sl(r, 128, 16)], 1, 2 * r, 2 * r + 1, None)

            Sr = Ring("ps", PS[2:4], 2)

            def emit_S(pair, dests):
                ps, pres = Sr.next()

                def fn(pe, pair=pair, ps=ps):
                    ins = None
                    for ti, td in enumerate(pair):
                        off = ti * 256
                        pe.matmul(ps[:, off:off + 128], lhsT=td[1], rhs=td[3], start=True, stop=True)
                        ins = pe.matmul(ps[:, off + 128:off + 256], lhsT=td[2], rhs=td[3], start=True, stop=True)
                    return ins
                g = pair[0][0]
                S.op('pe', fn, reads=qres[g] + kres[g], writes=[pres])
                E, eres = Er.next()
                S.op('act', lambda e, E=E, ps=ps: e.activation(out=E[:, :], in_=ps[:, :], func=AF.Exp),
                     reads=[pres], writes=[eres])
                for ti, td in enumerate(pair):
                    pap, prs = dests[ti]
                    S.op('pool', lambda e, E=E, ti=ti, td=td, pap=pap: e.tensor_tensor(
                        out=pap, in0=E[:, ti * 256:(ti + 1) * 256], in1=MTs[:, td[0], td[4], :], op=ALU.mult),
                        reads=[eres, 'MT%d' % td[0]], writes=[prs])

            for u in range(8):
                pair = [tile_g2(2 * u), tile_g2(2 * u + 1)]
                emit_S(pair, [(P2[:, 2 * u, :], 'P2_%d' % (2 * u)), (P2[:, 2 * u + 1, :], 'P2_%d' % (2 * u + 1))])

            for w in range(4):
                num, nres = PS[4 + (w % 2)], PSR[4 + (w % 2)]
                den, dres = PS[6 + (w % 2)], PSR[6 + (w % 2)]
                pairs = [[tile_g0(4 * w), tile_g0(4 * w + 1)], [tile_g0(4 * w + 2), tile_g0(4 * w + 3)],
                         [tile_g1(0, w), tile_g1(1, w)], [tile_g1(2, w), tile_g1(3, w)]]
                pbufs = []
                state = {'first': True}

                def emit_PV(pair, pb, pres_, num=num, den=den, nres=nres, dres=dres, state=state):
                    first = state['first']
                    state['first'] = False

                    def fn(pe, pair=pair, pb=pb, first=first, jj=jj):
                        ins = None
                        st = first
                        for ti, td in enumerate(pair):
                            g = td[0]
                            off = ti * 256
                            oc = td[7]
                            pe.matmul(num[:, oc], lhsT=Vt[g][:, td[5], jj * 128:(jj + 1) * 128], rhs=pb[:, off:off + 128],
                                      start=st, stop=False, skip_group_check=True)
                            pe.matmul(num[:, oc], lhsT=Vt[g][:, td[6], jj * 128:(jj + 1) * 128], rhs=pb[:, off + 128:off + 256],
                                      start=False, stop=False, skip_group_check=True)
                            pe.matmul(den[:, oc], lhsT=ones_bf[:], rhs=pb[:, off:off + 128],
                                      start=st, stop=False, skip_group_check=True)
                            ins = pe.matmul(den[:, oc], lhsT=ones_bf[:], rhs=pb[:, off + 128:off + 256],
                                            start=False, stop=False, skip_group_check=True)
                            st = False
                        return ins
                    S.op('pe', fn, reads=[pres_ + 'a', pres_ + 'b'], writes=[nres, dres])

                prev = None
                for pi, pair in enumerate(pairs):
                    pb, pres_ = Pr.next()
                    emit_S(pair, [(pb[:, 0:256], pres_ + 'a'), (pb[:, 256:512], pres_ + 'b')])
                    if prev is not None:
                        emit_PV(*prev)
                    prev = (pair, pb, pres_)
                emit_PV(*prev)

                def fn2(pe, w=w, num=num, den=den, jj=jj):
                    ins = None
                    for r in range(16):
                        oc = sl(r, 32, 16)
                        for blk in range(2):
                            rhs = P2[:, r, blk * 128 + 32 * w:blk * 128 + 32 * w + 32]
                            pe.matmul(num[:, oc], lhsT=Vt[2][:, 2 * r + blk, jj * 128:(jj + 1) * 128], rhs=rhs,
                                      start=False, stop=False, skip_group_check=True)
                            ins = pe.matmul(den[:, oc], lhsT=ones_bf[:], rhs=rhs,
                                            start=False, stop=(r == 15 and blk == 1), skip_group_check=True)
                    return ins
                S.op('pe', fn2, reads=['P2_%d' % r for r in range(16)], writes=[nres, dres])
                ws = slice(w * 512, (w + 1) * 512)
                S.op('dve', lambda e, den=den: e.reciprocal(out=rden[:], in_=den[:, :]), reads=[dres], writes=['rden'])
                S.op('dve', lambda e, num=num: e.tensor_tensor(out=t1[:], in0=num[:, :], in1=rden[:], op=ALU.mult),
                     reads=[nres, 'rden'], writes=['t1'])
                S.op('dve', lambda e, ws=ws, j=j: e.tensor_tensor(out=ogT[:, j, ws], in0=t1[:], in1=sga[:, ws], op=ALU.mult),
                     reads=['t1', 'sga%d' % w], writes=['og%d_%d' % (j, w)])
        S.barrier()
    A.top = mark_C

    wpa = A("wpa", [128, 4, 1024], BF16)
    wpb = A("wpb", [128, 4, 1024], BF16)
    wo = A("wo", [128, 8, 1024], BF16)
    lng = A("lng", [128, 1024], F32)
    lnb = A("lnb", [128, 1024], F32)
    load('pool', wpa[:], wpa_d, 'wpa')
    load('pool', wpb[:], wpb_d, 'wpb')
    load('pool', wo[:], wo_d, 'wo')
    load('sp', lng[:], lng_d, 'lng')
    load('sp', lnb[:], lnb_d, 'lnb')
    wrr = Ring("wr", [A("wr%d" % i, [128, 8, 128], BF16) for i in range(7)])
    diagr = Ring("dg", [A("dg%d" % i, [128, 31, 128], BF16) for i in range(2)])
    uTr = [A("uT%d" % i, [128, 4, 542], BF16) for i in range(2)]
    sgr = Ring("sg", [A("sg%d" % i, [128, 512], F32) for i in range(2)])
    ybf = A("ybf", [128, 4, 512], BF16)
    ysq = A("ysq", [128, 4, 512], BF16)
    mean = A("mean", [128, 512], F32)
    var = A("var", [128, 512], F32)
    rstd = A("rstd", [128, 512], F32)
    tnr = Ring("tn", [A("tn%d" % i, [128, 512], F32) for i in range(2)])
    cnr = Ring("cn", [A("cn%d" % i, [128, 512], BF16) for i in range(2)])
    sgbt = A("sgbt", [128, 4, 512], BF16)
    cg = A("cg", [128, 4, 512], BF16)
    smar = Ring("sma", [A("sma%d" % i, [128, 512], F32) for i in range(2)])
    tbr = Ring("tbb", [A("tbb%d" % i, [128, 512], F32) for i in range(2)])
    mT = A("mT", [128, 8, 512], BF16)
    rr = Ring("r", [A("r%d" % i, [128, 1024], F32) for i in range(2)])
    xkr = Ring("xk", [A("xk%d" % i, [128, 1024], F32) for i in range(2)])
    sqb = A("sqb", [128, 1024], F32)
    st = A("st", [128, 8], F32)
    ulast = A("ulast", [128, 4, 30], F32)
    zr4 = Ring("ps", [PS[i] for i in (0, 1, 2, 3, 6, 7)], names=["ps%d" % i for i in (0, 1, 2, 3, 6, 7)])
    cvr = Ring("ps", PS[4:6], 4)

    def wload(c0):
        buf, res = wrr.next()
        load('pool', buf[:], win_d[c0 // 128], res)
        wl_chunk[id(buf)] = c0 // 128
        return buf, res

    wl_chunk = {}

    def zmm(buf, res, w):
        ps, pres = zr4.next()
        t0 = TOK + w * 512
        S.op('pe', mm8(ps[:, :], lambda kc, buf=buf: buf[:, kc, :],
                       lambda kc, t0=t0: hsl(kc, t0, 512)), reads=[res], writes=[pres])
        if w == 0:
            piggy(wl_chunk[id(buf)], lambda kc, buf=buf: buf[:, kc, :], [res], zr4)
        return ps, pres

    pend_tail = [None]

    def final_stage(w):
        xks = {}

        def xload(ts):
            xk, xres = xkr.next()
            r0 = w * 512 + ts * 128
            load('sp', xk[:], xtok_d[r0:r0 + 128, :], xres)
            xks[ts] = (xk, xres)
        xload(0)
        xload(1)
        for ts in range(4):
            r, rres = rr.next()
            xk, xres = xks[ts]
            row0 = w * 512 + ts * 128
            for half in range(2):
                hs = slice(half * 512, (half + 1) * 512)
                psy, pry = zr4.next()

                def fny(pe, psy=psy, ts=ts, hs=hs):
                    ins = None
                    for oc in range(8):
                        ins = pe.matmul(psy[:, :], lhsT=mT[:, oc, ts * 128:(ts + 1) * 128], rhs=wo[:, oc, hs],
                                        start=(oc == 0), stop=(oc == 7))
                    return ins
                S.op('pe', fny, reads=['wo'] + ['mT%d' % c for c in range(8)], writes=[pry])
                S.op('dve', lambda e, psy=psy, r=r, hs=hs, xk=xk: e.scalar_tensor_tensor(
                    out=r[:, hs], in0=xk[:, hs], scalar=ALPHA, in1=psy[:, :], op0=ALU.mult, op1=ALU.add),
                    reads=[pry, xres], writes=[rres])
            if ts + 2 < 4:
                xload(ts + 2)
            if pend_tail[0] is not None:
                pend_tail[0]()
            pend_tail[0] = emit_ln(S, r, rres, 128, sqb, st, lng[:, :], lnb[:, :], y_d[row0:row0 + 128, :], ['lng', 'lnb'], split=True)
        pend_tail[0]()
        pend_tail[0] = None

    for w in range(4):
        uT = uTr[w % 2]
        uTn = uTr[(w + 1) % 2]
        up = w % 2
        if w == 0:
            for cc in range(4):
                S.op('pool', lambda e, cc=cc: e.tensor_copy(out=uTr[0][:, cc, 0:30], in_=uhalo[:, cc, 0:30]),
                     writes=['uh0_%d' % cc])
        for cc in range(4):
            ba, ra = wload(5120 + cc * 128)
            bgl, rg = wload(5632 + cc * 128)
            psa, pra = zmm(ba, ra, w)
            psg, prg = zmm(bgl, rg, w)
            sg, sres = sgr.next()
            S.op('act', lambda e, sg=sg, psg=psg, cc=cc: e.activation(out=sg[:, :], in_=psg[:, :], func=AF.Sigmoid,
                                                                     bias=bfm[:, 44 + cc:45 + cc], scale=1.0),
                 reads=[prg], writes=[sres])
            S.op('dve', lambda e, sg=sg, psa=psa, cc=cc, uT=uT: e.scalar_tensor_tensor(
                out=uT[:, cc, 30:542], in0=psa[:, :], scalar=bfm[:, 40 + cc:41 + cc], in1=sg[:, :], op0=ALU.add, op1=ALU.mult),
                reads=[pra, sres, 'uh%d_%d' % (up, cc)], writes=['u%d_%d' % (up, cc)])
            if w == 3:
                S.op('dve', lambda e, sg=sg, psa=psa, cc=cc: e.scalar_tensor_tensor(
                    out=ulast[:, cc, :], in0=psa[:, 482:512], scalar=bfm[:, 40 + cc:41 + cc], in1=sg[:, 482:512],
                    op0=ALU.add, op1=ALU.mult), reads=[pra, sres], writes=['ulast%d' % cc])
            else:
                S.op('dve', lambda e, uT=uT, uTn=uTn, cc=cc: e.tensor_copy(out=uTn[:, cc, 0:30], in_=uT[:, cc, 512:542]),
                     reads=['u%d_%d' % (up, cc)], writes=['uh%d_%d' % (1 - up, cc)])
        for cc in range(4):
            dg, dgres = diagr.next()
            S.op('pool', lambda e, dg=dg, cc=cc: e.tensor_tensor(
                out=dg[:, :, :], in0=ident_bf[:, :].unsqueeze(1).broadcast_to([128, 31, 128]),
                in1=cwfm[:, cc, :].unsqueeze(2).broadcast_to([128, 31, 128]), op=ALU.mult), writes=[dgres])
            psc, prc = cvr.next()

            def fnc(pe, psc=psc, dg=dg, uT=uT, cc=cc):
                ins = None
                for jt in range(31):
                    ins = pe.matmul(psc[:, :], lhsT=dg[:, jt, :], rhs=uT[:, cc, jt:jt + 512], start=(jt == 0), stop=(jt == 30))
                return ins
            S.op('pe', fnc, reads=[dgres, 'u%d_%d' % (up, cc), 'uh%d_%d' % (up, cc)],
                 writes=[prc])
            S.op('act', lambda e, psc=psc, cc=cc: e.activation(out=ybf[:, cc, :], in_=psc[:, :], func=AF.Identity,
                                                               bias=cbfm[:, cc:cc + 1], scale=1.0),
                 reads=[prc], writes=['ybf%d' % cc])
            S.op('act', lambda e, psc=psc, cc=cc: e.activation(out=ysq[:, cc, :], in_=psc[:, :], func=AF.Square,
                                                               bias=cbfm[:, cc:cc + 1], scale=1.0),
                 reads=[prc], writes=['ysq%d' % cc])
        if w == 0:
            S.op('dve', lambda e: e.tensor_tensor(out=wo[:, :, :], in0=wo[:, :, :],
                                                  in1=gate_bc[:, :].unsqueeze(1).broadcast_to([128, 8, 1024]), op=ALU.mult),
                 reads=['wo'], writes=['wo'])
        if w > 0:
            final_stage(w - 1)
        psm, prm = zr4.next()

        def fnm(pe, psm=psm):
            ins = None
            for cc in range(4):
                ins = pe.matmul(psm[:, :], lhsT=onesm[:], rhs=ybf[:, cc, :], start=(cc == 0), stop=(cc == 3))
            return ins
        S.op('pe', fnm, reads=['ybf%d' % c for c in range(4)], writes=[prm])
        psq, prq = zr4.next()

        def fnq(pe, psq=psq):
            ins = None
            for cc in range(4):
                ins = pe.matmul(psq[:, :], lhsT=onesm[:], rhs=ysq[:, cc, :], start=(cc == 0), stop=(cc == 3))
            return ins
        S.op('pe', fnq, reads=['ysq%d' % c for c in range(4)], writes=[prq])
        S.op('dve', lambda e, psm=psm: e.tensor_copy(out=mean[:], in_=psm[:, :]), reads=[prm], writes=['mean'])
        S.op('dve', lambda e: e.tensor_tensor(out=var[:], in0=mean[:], in1=mean[:], op=ALU.mult), reads=['mean'], writes=['var'])
        S.op('dve', lambda e, psq=psq: e.tensor_tensor(out=var[:], in0=psq[:, :], in1=var[:], op=ALU.subtract),
             reads=[prq, 'var'], writes=['var'])
        S.op('act', lambda e: e.activation(out=rstd[:], in_=var[:], func=AF.Sqrt, bias=EPS, scale=1.0),
             reads=['var'], writes=['rstd'])
        S.op('dve', lambda e: e.reciprocal(out=rstd[:], in_=rstd[:]), reads=['rstd'], writes=['rstd'])
        for cc in range(4):
            bgb, rgb = wload(6144 + cc * 128)
            psb_, prb = zmm(bgb, rgb, w)
            S.op('act', lambda e, psb_=psb_, cc=cc: e.activation(out=sgbt[:, cc, :], in_=psb_[:, :], func=AF.Silu,
                                                                bias=bfm[:, 48 + cc:49 + cc], scale=1.0),
                 reads=[prb], writes=['sgb%d' % cc])
        for oc in range(8):
            psa, pra = zr4.next()

            def fna(pe, psa=psa, oc=oc, w=w):
                ins = None
                for cc in range(4):
                    ins = pe.matmul(psa[:, :], lhsT=wpa[:, cc, oc * 128:(oc + 1) * 128], rhs=ogT[:, cc, w * 512:(w + 1) * 512],
                                    start=(cc == 0), stop=(cc == 3))
                return ins
            S.op('pe', fna, reads=['wpa'], writes=[pra])
            bma, rma = wload(6656 + oc * 128)
            psma, prma = zmm(bma, rma, w)
            sma, smres = smar.next()
            S.op('act', lambda e, sma=sma, psma=psma, oc=oc: e.activation(out=sma[:], in_=psma[:, :], func=AF.Sigmoid,
                                                                         bias=bfm[:, 52 + oc:53 + oc], scale=1.0),
                 reads=[prma], writes=[smres])
            S.op('dve', lambda e, psa=psa, sma=sma, oc=oc: e.tensor_tensor(out=mT[:, oc, :], in0=psa[:, :], in1=sma[:], op=ALU.mult),
                 reads=[pra, smres], writes=['mT%d' % oc])
        for cc in range(4):
            tn, tres = tnr.next()
            S.op('dve', lambda e, tn=tn, cc=cc: e.tensor_tensor(out=tn[:], in0=ybf[:, cc, :], in1=mean[:], op=ALU.subtract),
                 reads=['ybf%d' % cc, 'mean'], writes=[tres])
            S.op('dve', lambda e, tn=tn: e.tensor_tensor(out=tn[:], in0=tn[:], in1=rstd[:], op=ALU.mult),
                 reads=[tres, 'rstd'], writes=[tres])
            cn, cres = cnr.next()
            S.op('act', lambda e, tn=tn, cn=cn, cc=cc: e.activation(out=cn[:], in_=tn[:], func=AF.Silu,
                                                                   bias=cnbfm[:, cc:cc + 1], scale=cgfm[:, cc:cc + 1]),
                 reads=[tres], writes=[cres])
            S.op('dve', lambda e, cn=cn, cc=cc: e.tensor_tensor(out=cg[:, cc, :], in0=cn[:], in1=sgbt[:, cc, :], op=ALU.mult),
                 reads=[cres, 'sgb%d' % cc], writes=['cg%d' % cc])
        for oc in range(8):
            psb_, prb = zr4.next()

            def fnb(pe, psb_=psb_, oc=oc):
                ins = None
                for cc in range(4):
                    ins = pe.matmul(psb_[:, :], lhsT=wpb[:, cc, oc * 128:(oc + 1) * 128], rhs=cg[:, cc, :],
                                    start=(cc == 0), stop=(cc == 3))
                return ins
            bmb, rmb = wload(7680 + oc * 128)
            psmb, prmb = zmm(bmb, rmb, w)
            smb, sbres2 = smar.next()
            S.op('act', lambda e, smb=smb, psmb=psmb, oc=oc: e.activation(out=smb[:], in_=psmb[:, :], func=AF.Sigmoid,
                                                                         bias=bfm[:, 60 + oc:61 + oc], scale=1.0),
                 reads=[prmb], writes=[sbres2])
            S.op('pe', fnb, reads=['wpb'] + ['cg%d' % c for c in range(4)], writes=[prb])
            tb_, tbres = tbr.next()
            S.op('dve', lambda e, psb_=psb_, smb=smb, tb_=tb_: e.tensor_tensor(out=tb_[:], in0=psb_[:, :], in1=smb[:], op=ALU.mult),
                 reads=[prb, sbres2], writes=[tbres])
            S.op('dve', lambda e, oc=oc, tb_=tb_: e.tensor_tensor(out=mT[:, oc, :], in0=mT[:, oc, :], in1=tb_[:], op=ALU.add),
                 reads=['mT%d' % oc, tbres], writes=['mT%d' % oc])
    final_stage(3)
    S.op('sp', dma(uT_d, ulast[:]), reads=['ulast%d' % c for c in range(4)], dma=True)
    S.barrier()
    A.top = mark_persist

    zs = A("zs", [NS, 8704], F32)
    xs = A("xs", [NS, 1024], F32)
    cvec = A("cvec", [NS, 3, 512], F32)
    snew = A("snew", [NS, 12], F32)
    acc = A("acc", [NS, 3, 512], F32)
    dens = A("dens", [NS, 12], F32)
    osb = A("osb", [NS, 512], F32)
    us = A("us", [NS, 512], F32)
    ycs = A("ycs", [NS, 512], F32)
    ptmp = A("ptmp", [NS, 512], F32)
    st5 = A("st5", [NS, 8], F32)
    sqb5 = A("sqb5", [NS, 1024], F32)
    wpa5 = A("wpa5", [128, 4, 1024], BF16)
    wpb5 = A("wpb5", [128, 4, 1024], BF16)
    wo5 = A("wo5", [128, 8, 1024], BF16)
    lng5 = A("lng5", [128, 1024], F32)
    lnb5 = A("lnb5", [128, 1024], F32)
    load('pool', wpa5[:], wpa_d, 'wpa5')
    load('pool', wpb5[:], wpb_d, 'wpb5')
    load('pool', wo5[:], wo_d, 'wo5')
    load('sp', lng5[:], lng_d, 'lng5')
    load('sp', lnb5[:], lnb_d, 'lnb5')
    mark5 = A.top
    load('sp', xs[:], xstok_d, 'xs')
    load('sp', cvec[:], cvec_d, 'cvec')
    S.op('sp', dma(convs_d[:, 0:29, :], sconv_d[:, 1:30, :]), dma=True)

    binbc = A("binbc", [NS, 8704], F32)
    tmp16 = A("tmp16", [NS, 1536], F32)
    load('sp', zs[:], zs_d, 'zsraw')
    load('sp', binbc[:], binbc_d, 'binbc')
    for q4 in range(4):
        cs = slice(q4 * 2176, (q4 + 1) * 2176)
        S.op('dve', lambda e, cs=cs: e.tensor_tensor(out=zs[:, cs], in0=zs[:, cs], in1=binbc[:, cs], op=ALU.add),
             reads=['zsraw', 'binbc'], writes=['zs%d' % q4])
    ZALL = ['zs%d' % b for b in range(4)]
    S.op('sp', dma(ksn_d, zs[:, 1536:3072]), reads=ZALL, dma=True)
    S.op('sp', dma(vsn_d, zs[:, 3072:4608]), reads=ZALL, dma=True)
    S.op('dve', lambda e: e.tensor_scalar(out=zs[:, 0:1536], in0=zs[:, 0:1536], scalar1=QS, scalar2=0.0, op0=ALU.mult, op1=ALU.add),
         reads=ZALL, writes=['qs'])
    S.op('dve', lambda e: e.tensor_tensor(out=tmp16[:], in0=zs[:, 0:1536], in1=zs[:, 1536:3072], op=ALU.mult),
         reads=['qs'] + ZALL, writes=['tmp16'])
    S.op('dve', lambda e: e.reduce_sum(out=snew[:], in_=tmp16[:].rearrange("p (h d) -> p h d", d=128), axis=AX.X),
         reads=['tmp16'], writes=['snew'])
    S.op('act', lambda e: e.activation(out=snew[:], in_=snew[:], func=AF.Exp), reads=['snew'], writes=['pnew'])
    S.barrier()
    A.top = mark5

    sel = A("sel", [NS, NS, 128], F32)
    selb = A("selb", [NS, NS, 128], BF16)
    qsb = A("qsb", [NS, 1536], BF16)
    tbs = A("tbs", [128, 12], F32)
    pzz = A("pzz", [128, 12, NS, NS], BF16)
    onesf = A("onesf", [128, 1], BF16)
    kvbr = Ring("kvb", [A("kvb%d" % i, [128, 512], BF16) for i in range(3)])
    kvr = Ring("kv", [A("kv%d" % i, [128, 1024], F32) for i in range(4)])
    prodr = Ring("prod", [A("prod%d" % i, [128, 512], F32) for i in range(2)])
    scr = Ring("sc", [A("sc%d" % i, [128, 4], F32) for i in range(3)])
    load('sp', sel[:], sel_d, 'sel')
    load('sp', tbs[:], tbs_d, 'tbs')
    S.op('pool', lambda e: e.memset(pzz[:], 0.0), writes=['pzz'])
    S.op('pool', lambda e: e.memset(onesf[:], 1.0), writes=['onesf'])
    S.op('pool', lambda e: e.tensor_copy(out=selb[:], in_=sel[:]), reads=['sel'], writes=['selb'])
    S.op('pool', lambda e: e.tensor_copy(out=qsb[:], in_=zs[:, 0:1536]), writes=['qsb'])
    accps = [PS[2], PS[3], PS[4]]
    denps = PS[5]
    qbr = Ring("ps", PS[6:8], 6)
    first_acc = [True, True, True]
    first_den = [True]
    items = [(b, g) for b in range(NS) for g in range(3)]

    def stageA(b, g):
        d = GROUPS[g][1]
        kv, kres_ = kvr.next()
        load('sp', kv[:], ck_d[g][b, sl(0, 128, d), :], kres_)
        qb, qbres = qbr.next()
        S.op('pe', lambda pe, qb=qb, b=b, g=g: pe.matmul(qb[:, :], lhsT=selb[:, b, :], rhs=qsb[:, g * 512:(g + 1) * 512],
                                                        start=True, stop=True),
             reads=['selb', 'qsb'], writes=[qbres])
        kvb, kvbres = kvbr.next()
        S.op('act', lambda e, kvb=kvb, kv=kv: e.activation(out=kvb[:], in_=kv[:, 512:1024], func=AF.Identity), reads=[kres_], writes=[kvbres])
        prod, pres_ = prodr.next()
        sc, sres = scr.next()
        for hh in range(4):
            S.op('dve', lambda e, prod=prod, kv=kv, qb=qb, sc=sc, hh=hh: e.scalar_tensor_tensor(
                out=prod[:, hh * 128:(hh + 1) * 128], in0=kv[:, hh * 128:(hh + 1) * 128], scalar=1.0,
                in1=qb[:, hh * 128:(hh + 1) * 128], op0=ALU.mult, op1=ALU.mult, accum_out=sc[:, hh:hh + 1]),
                reads=[kres_, qbres], writes=[sres + '_%d' % hh, pres_ + '_%d' % hh] + ([sres] if hh == 0 else []))
        S.op('dve', lambda e, sc=sc, g=g: e.tensor_tensor(out=sc[:], in0=sc[:], in1=tbs[:, 4 * g:4 * g + 4], op=ALU.add),
             reads=[sres + '_%d' % hh for hh in range(4)] + ['tbs'], writes=[sres])
        pcol = 'pz%d_%d' % (b, g)
        S.op('act', lambda e, sc=sc, g=g, b=b: e.activation(out=pzz[:, 4 * g:4 * g + 4, b, b], in_=sc[:], func=AF.Exp),
             reads=[sres, 'pzz'], writes=[pcol])
        return (b, g, kvb, kvbres, pcol)

    def stageB(b, g, kv, kres_, pcol):
        def fnpv(pe, g=g, b=b, kv=kv, fa=first_acc[g], fd=first_den[0]):
            ins = None
            for hh in range(4):
                pe.matmul(accps[g][0:NS, hh * 128:(hh + 1) * 128], lhsT=pzz[:, 4 * g + hh, b, :],
                          rhs=kv[:, hh * 128:(hh + 1) * 128],
                          start=(fa and hh == 0), stop=(b == NS - 1 and hh == 3), skip_group_check=True)
                ins = pe.matmul(denps[0:NS, 4 * g + hh:4 * g + hh + 1], lhsT=pzz[:, 4 * g + hh, b, :], rhs=onesf[:, 0:1],
                                start=(fd and hh == 0), stop=(b == NS - 1 and g == 2 and hh == 3), skip_group_check=True)
            return ins
        S.op('pe', fnpv, reads=[pcol, kres_, 'onesf'], writes=['accps%d' % g, 'denps'])
        first_acc[g] = False
        first_den[0] = False


    class _DQ:
        def __init__(self):
            self.q = []

        def op(self, *a, **k):
            self.q.append((a, k))

        def load(self, out, in_, res):
            self.q.append((('sp', dma(out, in_)), dict(writes=[res], dma=True)))

        def flush(self, n):
            while n > 0 and self.q:
                a, k = self.q.pop(0)
                S.op(*a, **k)
                n -= 1
    DQ = _DQ()
    stc = A("stc", [NS, 5, 512], F32)
    cwb = A("cwb", [NS, 5, 512], F32)
    DQ.op('act', lambda e: e.activation(out=us[:], in_=zs[:, 5632:6144], func=AF.Sigmoid), writes=['us'])
    DQ.op('dve', lambda e: e.tensor_tensor(out=us[:], in0=us[:], in1=zs[:, 5120:5632], op=ALU.mult), reads=['us'], writes=['us'])
    DQ.op('sp', dma(convs_d[:, 29, :], us[:]), reads=['us'], dma=True)
    DQ.op('dve', lambda e: e.tensor_copy(out=ycs[:], in_=cvec[:, 0, :]), reads=['cvec'], writes=['ycs'])
    for hf in range(6):
        DQ.load(stc[:], sconv_d[:, hf * 5:(hf + 1) * 5, :], 'stc')
        DQ.load(cwb[:], cwbc_d[:, hf * 5:(hf + 1) * 5, :], 'cwb')
        DQ.op('dve', lambda e: e.tensor_tensor(out=stc[:], in0=stc[:], in1=cwb[:], op=ALU.mult), reads=['stc', 'cwb'], writes=['stc'])
        for jt in range(5):
            DQ.op('dve', lambda e, jt=jt: e.tensor_tensor(out=ycs[:], in0=ycs[:], in1=stc[:, jt, :], op=ALU.add),
                 reads=['stc', 'ycs'], writes=['ycs'])
    DQ.load(cwb[:, 0, :], cwbc_d[:, 30, :], 'cwb')
    DQ.op('dve', lambda e: e.tensor_tensor(out=ptmp[:], in0=us[:], in1=cwb[:, 0, :], op=ALU.mult), reads=['us', 'cwb'], writes=['ptmp'])
    DQ.op('dve', lambda e: e.tensor_tensor(out=ycs[:], in0=ycs[:], in1=ptmp[:], op=ALU.add), reads=['ptmp', 'ycs'], writes=['ycs'])
    emit_ln(DQ, ycs, 'ycs', NS, sqb5, st5, cvec[:, 1, :], cvec[:, 2, :], None, ['cvec'], width=512)
    DQ.op('act', lambda e: e.activation(out=ycs[:], in_=ycs[:], func=AF.Silu), reads=['ycs'], writes=['ycs'])
    DQ.op('act', lambda e: e.activation(out=ptmp[:], in_=zs[:, 6144:6656], func=AF.Silu), reads=['ptmp'], writes=['ptmp'])
    DQ.op('dve', lambda e: e.tensor_tensor(out=ycs[:], in0=ycs[:], in1=ptmp[:], op=ALU.mult), reads=['ycs', 'ptmp'], writes=['ycs'])

    pend = []
    for (b, g) in items:
        pend.append(stageA(b, g))
        if len(pend) > 2:
            stageB(*pend.pop(0))
        DQ.flush(2)
    while pend:
        stageB(*pend.pop(0))
    DQ.flush(10000)
    for g in range(3):
        for jh in range(4):
            h = 4 * g + jh
            S.op('dve', lambda e, g=g, jh=jh, h=h: e.scalar_tensor_tensor(
                out=acc[:, g, jh * 128:(jh + 1) * 128], in0=zs[:, 3072 + h * 128:3072 + (h + 1) * 128],
                scalar=snew[:, h:h + 1], in1=accps[g][0:NS, jh * 128:(jh + 1) * 128], op0=ALU.mult, op1=ALU.add),
                reads=['accps%d' % g], writes=['acc%d_%d' % (g, jh)])
    S.op('dve', lambda e: e.tensor_tensor(out=dens[:], in0=denps[0:NS, 0:12], in1=snew[:], op=ALU.add),
         reads=['denps'], writes=['dens'])
    S.barrier()
    A.top = mark5

    sgs = A("sgs", [NS, 1024], F32)
    sgs2 = A("sgs2", [NS, 1024], F32)
    ms = A("ms", [NS, 1024], F32)
    trT = A("trT", [128, 8, NS], BF16)
    trT2 = A("trT2", [128, 4, NS], BF16)
    rs = A("rs", [NS, 1024], F32)
    zr2 = Ring("ps", PS[0:2])
    S.op('dve', lambda e: e.tensor_tensor(out=acc[:, 0, :], in0=acc[:, 0, :], in1=acc[:, 1, :], op=ALU.add), writes=['accs'])
    S.op('dve', lambda e: e.tensor_tensor(out=acc[:, 0, :], in0=acc[:, 0, :], in1=acc[:, 2, :], op=ALU.add), reads=['accs'], writes=['accs'])
    S.op('dve', lambda e: e.tensor_tensor(out=dens[:, 0:4], in0=dens[:, 0:4], in1=dens[:, 4:8], op=ALU.add), writes=['dens'])
    S.op('dve', lambda e: e.tensor_tensor(out=dens[:, 0:4], in0=dens[:, 0:4], in1=dens[:, 8:12], op=ALU.add), reads=['dens'], writes=['dens'])
    S.op('dve', lambda e: e.reciprocal(out=dens[:, 0:4], in_=dens[:, 0:4]), reads=['dens'], writes=['dens'])
    S.op('act', lambda e: e.activation(out=ptmp[:], in_=zs[:, 4608:5120], func=AF.Silu), writes=['ptmp'])
    for jh in range(4):
        S.op('dve', lambda e, jh=jh: e.scalar_tensor_tensor(
            out=osb[:, jh * 128:(jh + 1) * 128], in0=acc[:, 0, jh * 128:(jh + 1) * 128], scalar=dens[:, jh:jh + 1],
            in1=ptmp[:, jh * 128:(jh + 1) * 128], op0=ALU.mult, op1=ALU.mult),
            reads=['accs', 'dens', 'ptmp'], writes=['osb%d' % jh])
    OSB = ['osb%d' % j for j in range(4)]

    def transp(src_fn, nch, dstT, tag, reads):
        ps, pres = zr2.next()

        def fn(pe):
            ins = None
            for c in range(nch):
                ins = pe.matmul(ps[:, c * NS:(c + 1) * NS], lhsT=src_fn(c), rhs=ident[0:NS, 0:NS], start=True, stop=True)
            return ins
        S.op('pe', fn, reads=reads, writes=[pres])
        S.op('dve', lambda e: e.tensor_copy(out=dstT[:, 0:nch, :], in_=ps[:, 0:nch * NS].rearrange("p (c n) -> p c n", n=NS)),
             reads=[pres], writes=[tag])

    transp(lambda c: osb[:, c * 128:(c + 1) * 128], 4, trT, 'ogsT', OSB)
    transp(lambda c: ycs[:, c * 128:(c + 1) * 128], 4, trT2, 'cgsT', ['ycs'])
    S.op('act', lambda e: e.activation(out=sgs[:], in_=zs[:, 6656:7680], func=AF.Sigmoid), writes=['sgs'])
    S.op('act', lambda e: e.activation(out=sgs2[:], in_=zs[:, 7680:8704], func=AF.Sigmoid), writes=['sgs2'])
    for half in range(2):
        hs = slice(half * 512, (half + 1) * 512)
        ps, pres = zr2.next()

        def fa5(pe, ps=ps, hs=hs):
            ins = None
            for cc in range(4):
                ins = pe.matmul(ps[0:NS, :], lhsT=trT[:, cc, :], rhs=wpa5[:, cc, hs], start=(cc == 0), stop=(cc == 3))
            return ins
        S.op('pe', fa5, reads=['ogsT', 'wpa5'], writes=[pres])
        S.op('dve', lambda e, ps=ps, hs=hs: e.tensor_tensor(out=ms[:, hs], in0=ps[0:NS, :], in1=sgs[:, hs], op=ALU.mult),
             reads=[pres, 'sgs'], writes=['msa%d' % half])
        ps, pres = zr2.next()

        def fb5(pe, ps=ps, hs=hs):
            ins = None
            for cc in range(4):
                ins = pe.matmul(ps[0:NS, :], lhsT=trT2[:, cc, :], rhs=wpb5[:, cc, hs], start=(cc == 0), stop=(cc == 3))
            return ins
        S.op('pe', fb5, reads=['cgsT', 'wpb5'], writes=[pres])
        S.op('dve', lambda e, ps=ps, hs=hs: e.tensor_tensor(out=sgs2[:, hs], in0=ps[0:NS, :], in1=sgs2[:, hs], op=ALU.mult),
             reads=[pres, 'sgs2'], writes=['msb%d' % half])
        S.op('dve', lambda e, hs=hs: e.tensor_tensor(out=ms[:, hs], in0=ms[:, hs], in1=sgs2[:, hs], op=ALU.add),
             reads=['msa%d' % half, 'msb%d' % half], writes=['ms%d' % half])
    transp(lambda c: ms[:, c * 128:(c + 1) * 128], 8, trT, 'msT', ['ms0', 'ms1', 'ogsT'])
    for half in range(2):
        hs = slice(half * 512, (half + 1) * 512)
        ps, pres = zr2.next()

        def fy5(pe, ps=ps, hs=hs):
            ins = None
            for oc in range(8):
                ins = pe.matmul(ps[0:NS, :], lhsT=trT[:, oc, :], rhs=wo5[:, oc, hs], start=(oc == 0), stop=(oc == 7))
            return ins
        S.op('pe', fy5, reads=['msT', 'wo5'], writes=[pres])
        S.op('dve', lambda e, ps=ps, hs=hs: e.tensor_tensor(out=rs[:, hs], in0=ps[0:NS, :], in1=gate_s[0:NS, hs], op=ALU.mult),
             reads=[pres], writes=['rs%d' % half])
    S.op('dve', lambda e: e.scalar_tensor_tensor(out=rs[:], in0=xs[:], scalar=ALPHA, in1=rs[:], op0=ALU.mult, op1=ALU.add),
         reads=['rs0', 'rs1', 'xs'], writes=['rs'])
    emit_ln(S, rs, 'rs', NS, sqb5, st5, lng5[0:NS, :], lnb5[0:NS, :], ys_d, ['lng5', 'lnb5'])
    S.barrier()
    return nc, S


def emit_ln(S, r, rres, np_, sqb, st, g_ap, b_ap, out_dram, extra_reads, width=1024, eng2='dve', split=False):
    inv = 1.0 / width
    rv = r[0:np_, 0:width]
    S.op('act', lambda e: e.activation(out=sqb[0:np_, 0:width], in_=rv, func=AF.Identity, accum_out=st[0:np_, 0:1]),
         reads=[rres], writes=['st0', 'sqb'])
    S.op('act', lambda e: e.activation(out=sqb[0:np_, 0:width], in_=rv, func=AF.Square, accum_out=st[0:np_, 1:2]),
         reads=[rres, 'sqb'], writes=['st1', 'sqb'])
    S.op('dve', lambda e: e.tensor_scalar(out=st[0:np_, 2:3], in0=st[0:np_, 0:1], scalar1=inv, scalar2=0.0, op0=ALU.mult, op1=ALU.add),
         reads=['st0'], writes=['st2'])
    S.op('dve', lambda e: e.tensor_tensor(out=st[0:np_, 3:4], in0=st[0:np_, 2:3], in1=st[0:np_, 2:3], op=ALU.mult),
         reads=['st2'], writes=['st3'])
    S.op('dve', lambda e: e.scalar_tensor_tensor(out=st[0:np_, 4:5], in0=st[0:np_, 1:2], scalar=inv, in1=st[0:np_, 3:4],
                                                  op0=ALU.mult, op1=ALU.subtract),
         reads=['st1', 'st3'], writes=['st4'])
    S.op('act', lambda e: e.activation(out=st[0:np_, 5:6], in_=st[0:np_, 4:5], func=AF.Sqrt, bias=EPS, scale=1.0),
         reads=['st4'], writes=['st5'])
    S.op('dve', lambda e: e.reciprocal(out=st[0:np_, 5:6], in_=st[0:np_, 5:6]), reads=['st5'], writes=['st5'])
    S.op('dve', lambda e: e.scalar_tensor_tensor(out=st[0:np_, 6:7], in0=st[0:np_, 2:3], scalar=-1.0, in1=st[0:np_, 5:6],
                                                  op0=ALU.mult, op1=ALU.mult),
         reads=['st2', 'st5'], writes=['st6'])
    S.op('act', lambda e: e.activation(out=rv, in_=rv, func=AF.Identity, bias=st[0:np_, 6:7], scale=st[0:np_, 5:6]),
         reads=['st5', 'st6', rres, 'sqb'], writes=[rres])
    def tail():
        S.op(eng2, lambda e: e.tensor_tensor(out=rv, in0=rv, in1=g_ap, op=ALU.mult),
             reads=[rres] + list(extra_reads), writes=[rres])
        S.op(eng2, lambda e: e.tensor_tensor(out=rv, in0=rv, in1=b_ap, op=ALU.add), reads=[rres] + list(extra_reads), writes=[rres])
        if out_dram is not None:
            S.op('sp', lambda e: e.dma_start(out=out_dram, in_=rv), reads=[rres], dma=True)
    if split:
        return tail
    tail()


_PROG = None


def _get_prog():
    global _PROG
    if _PROG is None:
        from contextlib import ExitStack
        es = ExitStack()
        nc, S = build_program()
        esems = {e: es.enter_context(nc.semaphore("s_" + e)) for e in ENGS}
        dsems = [es.enter_context(nc.semaphore("d%d" % i)) for i in range(ND)]
        block = es.enter_context(nc.Block())
        S.emit(nc, block, esems, dsems)
        es.close()
        _PROG = nc
    return _PROG


def _fm(v, nchunk):
    return np.ascontiguousarray(v.reshape(nchunk, 128).T)


def _wfm(w):
    K, N = w.shape
    return np.ascontiguousarray(w.reshape(K // 128, 128, N).transpose(1, 0, 2))


def kernel(x_prompt, x_sample, c_prompt, c_sample, cache_kv_w128, cache_kv_w512, cache_kv_w2048,
           state_conv, w_c, b_c, w_in, b_in, conv_w, conv_b, conv_norm_g, conv_norm_b,
           w_pa, w_pb, w_o, ln_g, ln_b):
    f32 = np.float32
    x_prompt = np.asarray(x_prompt, f32)
    x_sample = np.asarray(x_sample, f32)
    nc = _get_prog()
    hidx = np.arange(1, 13, dtype=np.float64)
    slopes = 2.0 ** (-8.0 * hidx / 12.0)
    kk = np.arange(128)[:, None]
    qq = np.arange(128)[None, :]
    tb = np.full((NCORES, 128, 12, 2, 256), NEGB, f32)
    for h in range(12):
        d = GROUPS[h // 4][1]
        prev = np.where(kk >= qq, -slopes[h] * d * (qq - kk + 128), NEGB)
        cur = np.where(kk <= qq, -slopes[h] * d * (qq - kk), NEGB)
        for c in range(NCORES):
            tb[c, :, h, 0, 0:128] = prev
            tb[c, :, h, 0, 128:256] = cur
            tb[c, :, h, 1, 0:128] = prev if c > 0 else NEGB
            tb[c, :, h, 1, 128:256] = cur
    tbs = np.zeros((128, 12), f32)
    for h in range(12):
        d = GROUPS[h // 4][1]
        tbs[:, h] = -slopes[h] * d * (128 - np.arange(128))
    ident = np.eye(128, dtype=f32)
    sel = np.zeros((NS, NS, 128), f32)
    for b in range(NS):
        sel[b, b, :] = 1.0

    w_c = np.asarray(w_c, f32); w_in = np.asarray(w_in, f32)
    b_c = np.asarray(b_c, f32); b_in = np.asarray(b_in, f32)
    conv_w = np.asarray(conv_w, f32)
    shared = {
        "wc": _wfm(w_c), "bcT": _fm(b_c[0:2048], 16),
        "bg": np.ascontiguousarray(np.broadcast_to(b_c[2048:3072], (128, 1024))),
        "win": np.ascontiguousarray(w_in.reshape(8, 128, 68, 128).transpose(2, 1, 0, 3)), "bfm": _fm(b_in, 68),
        "bvbc": np.ascontiguousarray(np.broadcast_to(b_in[3072:4608], (128, 1536))),
        "binbc": np.ascontiguousarray(np.broadcast_to(b_in, (NS, 8704))),
        "cwfm": np.ascontiguousarray(conv_w.T.reshape(4, 128, 31).transpose(1, 0, 2)),
        "cbfm": _fm(np.asarray(conv_b, f32), 4), "cgfm": _fm(np.asarray(conv_norm_g, f32), 4),
        "cnbfm": _fm(np.asarray(conv_norm_b, f32), 4),
        "cwbc": np.ascontiguousarray(np.broadcast_to(conv_w, (NS, 31, 512))),
        "cvec": np.ascontiguousarray(np.broadcast_to(
            np.stack([np.asarray(conv_b, f32), np.asarray(conv_norm_g, f32), np.asarray(conv_norm_b, f32)]), (NS, 3, 512))),
        "wpa": _wfm(np.asarray(w_pa, f32)), "wpb": _wfm(np.asarray(w_pb, f32)), "wo": _wfm(np.asarray(w_o, f32)),
        "lng": np.ascontiguousarray(np.broadcast_to(np.asarray(ln_g, f32), (128, 1024))),
        "lnb": np.ascontiguousarray(np.broadcast_to(np.asarray(ln_b, f32), (128, 1024))),
        "tbs": tbs, "ident": ident, "sel": sel,
    }
    xx = x_prompt[0]
    xpad = np.concatenate([np.zeros((TOK, 1024), f32), xx], axis=0)
    cp = np.asarray(c_prompt, f32)[0]
    cs = np.asarray(c_sample, f32)
    xs2 = x_sample[:, 0, :]
    caches = [np.asarray(cache_kv_w128, f32).reshape(128, 128, 1024), np.asarray(cache_kv_w512, f32).reshape(128, 512, 1024),
              np.asarray(cache_kv_w2048, f32).reshape(128, 2048, 1024)]
    sconv = np.asarray(state_conv, f32)
    in_maps = []
    for c in range(NCORES):
        seg = xpad[c * TOK:(c + 2) * TOK]
        m = dict(shared)
        m["xT"] = np.ascontiguousarray(seg.T.reshape(8, 128, 2 * TOK).transpose(1, 0, 2))
        m["xtok"] = np.ascontiguousarray(xx[c * TOK:(c + 1) * TOK])
        sb = slice(c * NS, (c + 1) * NS)
        m["xsT"] = np.ascontiguousarray(xs2[sb].T.reshape(8, 128, NS).transpose(1, 0, 2))
        m["xstok"] = np.ascontiguousarray(xs2[sb])
        call = np.concatenate([cp[None, :], cs[sb]], axis=0)
        m["cT"] = np.ascontiguousarray(call.T.reshape(8, 128, 17).transpose(1, 0, 2))
        m["cbc"] = np.ascontiguousarray(np.broadcast_to(cp.reshape(8, 128).T[:, :, None], (128, 8, 128)))
        m["tb"] = tb[c]
        m["valid"] = np.full((128, 1), 0.0 if c == 0 else 1.0, f32)
        m["ck0"] = np.ascontiguousarray(caches[0][sb])
        m["ck1"] = np.ascontiguousarray(caches[1][sb])
        m["ck2"] = np.ascontiguousarray(caches[2][sb])
        m["sconv"] = np.ascontiguousarray(sconv[sb])
        in_maps.append(m)
    res = run_bass_kernel_spmd(nc, in_maps, core_ids=list(range(NCORES)))
    R = res.results
    y = np.concatenate([R[c]["y"] for c in range(NCORES)], axis=0)[None]
    ys = np.concatenate([R[c]["ys"] for c in range(NCORES)], axis=0)[:, None, :]
    last = R[NCORES - 1]
    kvp = []
    for g in range(3):
        k = np.asarray(last["kT%d" % g]).transpose(2, 1, 0)
        v = np.asarray(last["v%d" % g]).reshape(-1, 4, 128)
        kvp.append(np.ascontiguousarray(np.stack([k, v], axis=1))[None].astype(f32))
    convp = np.ascontiguousarray(np.asarray(last["uT"]).transpose(2, 1, 0).reshape(30, 512))[None].astype(f32)
    ksn = np.concatenate([R[c]["ksn"] for c in range(NCORES)], axis=0).reshape(128, 3, 4, 128)
    vsn = np.concatenate([R[c]["vsn"] for c in range(NCORES)], axis=0).reshape(128, 3, 4, 128)
    kvs = [np.ascontiguousarray(np.stack([ksn[:, g], vsn[:, g]], axis=1))[:, None].astype(f32) for g in range(3)]
    convs = np.concatenate([R[c]["convs"] for c in range(NCORES)], axis=0).astype(f32)
    return (y.astype(f32), ys.astype(f32), kvp[0], kvp[1], kvp[2], convp, kvs[0], kvs[1], kvs[2], convs)
```
